# Optimizing a Trainium2 kernel written in Bass

```python
import math
import jax, jax.numpy as jnp
from jax import lax
import numpy as np

D_MODEL = 1024
BATCH = 8
SEQ = 2048
DEPTH = 4
DEC_BATCH = 128
DEC_SEQ = 4
PAST_LEN = 8192
PAGE_SIZE = 128

MIX_WIDTH = D_MODEL
GDN_HEADS = 4
GDN_HEAD_DIM = (MIX_WIDTH // 2) // GDN_HEADS
GDN_KEY = GDN_HEADS * GDN_HEAD_DIM
GDN_VAL = GDN_HEADS * GDN_HEAD_DIM
CONV_WIDTH = 4
CONV_CH = 2 * GDN_KEY + GDN_VAL
GDN_CHUNK = 64
SWA_HEADS = 8
SWA_KV_HEADS = 2
SWA_WIDTH = MIX_WIDTH - GDN_VAL
SWA_HEAD_DIM = SWA_WIDTH // SWA_HEADS
SWA_GROUP = SWA_HEADS // SWA_KV_HEADS
SWA_KV_WIDTH = SWA_KV_HEADS * SWA_HEAD_DIM
WINDOW = 128
D_FF = 2816
EPS = 1e-6
L2_EPS = 1e-6
OFF_Z = CONV_CH
OFF_B = OFF_Z + GDN_VAL
OFF_A = OFF_B + GDN_HEADS
OFF_SQ = OFF_A + GDN_HEADS
OFF_SK = OFF_SQ + SWA_WIDTH
OFF_SV = OFF_SK + SWA_KV_WIDTH
IN_COLS = OFF_SV + SWA_KV_WIDTH

kernel_name = "hymba_gdn_swa_sink_macaron_step"


def _rmsnorm(x, w):
    xf = x.astype(jnp.float32)
    y = xf * lax.rsqrt(jnp.mean(xf * xf, axis=-1, keepdims=True) + EPS)
    return (y * w.astype(jnp.float32)).astype(x.dtype)


def _swiglu(x, w_gate, w_up, w_down):
    return (jax.nn.silu(x @ w_gate) * (x @ w_up)) @ w_down


def _l2norm(x):
    return x * lax.rsqrt(jnp.sum(x * x, axis=-1, keepdims=True) + L2_EPS)


def _short_conv(u, buf, w):
    L = u.shape[1]
    full = jnp.concatenate([buf.astype(u.dtype), u], axis=1)
    out = sum(full[:, i:i + L] * w[i] for i in range(CONV_WIDTH))
    return jax.nn.silu(out), full[:, -(CONV_WIDTH - 1):]


def _gated_delta_chunked(q, k, v, g, beta, s0):
    B, L, H, DK = q.shape
    DV = v.shape[-1]
    C = min(GDN_CHUNK, L)
    pad = (-L) % C
    if pad:
        pw = ((0, 0), (0, pad), (0, 0), (0, 0))
        q, k, v = jnp.pad(q, pw), jnp.pad(k, pw), jnp.pad(v, pw)
        g, beta = jnp.pad(g, pw[:3]), jnp.pad(beta, pw[:3])
    N = (L + pad) // C

    def blk(t):
        return jnp.moveaxis(t.reshape((B, N, C, H) + t.shape[3:]), 3, 1)

    q, k, v, g, beta = blk(q), blk(k), blk(v), blk(g), blk(beta)
    gc = jnp.cumsum(g, axis=-1)
    idx = jnp.arange(C)
    causal = idx[:, None] >= idx[None, :]
    strict = idx[:, None] > idx[None, :]
    decay = jnp.exp(jnp.where(causal, gc[..., :, None] - gc[..., None, :], -jnp.inf))
    kb = k * beta[..., None]
    a_mat = jnp.where(strict, jnp.einsum('bhnid,bhnjd->bhnij', kb, k) * decay, 0.0)
    lhs = a_mat + jnp.eye(C, dtype=jnp.float32)
    rhs = jnp.concatenate([v * beta[..., None], kb * jnp.exp(gc)[..., None]], axis=-1)
    sol = lax.linalg.triangular_solve(lhs, rhs, left_side=True, lower=True, unit_diagonal=True)
    w_val, k_cum = sol[..., :DV], sol[..., DV:]
    qk = jnp.einsum('bhnid,bhnjd->bhnij', q, k) * decay
    q_dec = q * jnp.exp(gc)[..., None]
    k_dec = k * jnp.exp(gc[..., -1:] - gc)[..., None]
    c_dec = jnp.exp(gc[..., -1])
    xs = tuple(jnp.moveaxis(t, 2, 0) for t in (w_val, k_cum, qk, q_dec, k_dec, c_dec))

    def step(S, inp):
        w_i, kc_i, qk_i, qd_i, kd_i, cd_i = inp
        u = w_i - jnp.einsum('bhck,bhkv->bhcv', kc_i, S)
        o = jnp.einsum('bhck,bhkv->bhcv', qd_i, S) + jnp.einsum('bhij,bhjv->bhiv', qk_i, u)
        S = S * cd_i[..., None, None] + jnp.einsum('bhck,bhcv->bhkv', kd_i, u)
        return S, o

    s_final, o = lax.scan(step, s0, xs)
    o = jnp.moveaxis(jnp.moveaxis(o, 0, 2), 1, 3).reshape(B, N * C, H, DV)[:, :L]
    return o, s_final


def _gdn_branch(h_qkv, z, b, a, conv_buf, s0, conv_w, a_log, dt_bias, norm_w):
    B, L, _ = h_qkv.shape
    u, new_buf = _short_conv(h_qkv, conv_buf, conv_w)
    u = u.astype(jnp.float32)
    q = _l2norm(u[..., :GDN_KEY].reshape(B, L, GDN_HEADS, GDN_HEAD_DIM)) * (GDN_HEAD_DIM ** -0.5)
    k = _l2norm(u[..., GDN_KEY:2 * GDN_KEY].reshape(B, L, GDN_HEADS, GDN_HEAD_DIM))
    v = u[..., 2 * GDN_KEY:].reshape(B, L, GDN_HEADS, GDN_HEAD_DIM)
    beta = jax.nn.sigmoid(b.astype(jnp.float32))
    g = -jnp.exp(a_log.astype(jnp.float32)) * jax.nn.softplus(a.astype(jnp.float32) + dt_bias.astype(jnp.float32))
    o, s_new = _gated_delta_chunked(q, k, v, g, beta, s0.astype(jnp.float32))
    o = o * lax.rsqrt(jnp.mean(o * o, axis=-1, keepdims=True) + EPS) * norm_w.astype(jnp.float32)
    o = o * jax.nn.silu(z.astype(jnp.float32).reshape(B, L, GDN_HEADS, GDN_HEAD_DIM))
    return o.reshape(B, L, GDN_VAL).astype(h_qkv.dtype), new_buf, s_new


def _sink_attention(q, k, v, allowed, sinks):
    s = jnp.einsum('bnqhgd,bnkhd->bnhgqk', q, k).astype(jnp.float32) * (SWA_HEAD_DIM ** -0.5)
    s = jnp.where(allowed[None, :, None, None], s, -jnp.inf)
    sink = jnp.broadcast_to(sinks.astype(jnp.float32)[None, None, :, :, None, None], s.shape[:-1] + (1,))
    p = jax.nn.softmax(jnp.concatenate([s, sink], axis=-1), axis=-1)[..., :-1]
    return jnp.einsum('bnhgqk,bnkhd->bnqhgd', p.astype(v.dtype), v)


def _swa_branch(q, k, v, k_buf, v_buf, sinks, norm_w):
    B, L = q.shape[:2]
    q = q.reshape(B, L, SWA_KV_HEADS, SWA_GROUP, SWA_HEAD_DIM)
    k = k.reshape(B, L, SWA_KV_HEADS, SWA_HEAD_DIM)
    v = v.reshape(B, L, SWA_KV_HEADS, SWA_HEAD_DIM)
    sk = sinks.reshape(SWA_KV_HEADS, SWA_GROUP)
    if k_buf is None:
        N = L // WINDOW
        qb = q.reshape(B, N, WINDOW, SWA_KV_HEADS, SWA_GROUP, SWA_HEAD_DIM)

        def band(t):
            prev = jnp.concatenate([jnp.zeros_like(t[:, :WINDOW]), t[:, :L - WINDOW]], axis=1)
            shp = (B, N, WINDOW, SWA_KV_HEADS, SWA_HEAD_DIM)
            return jnp.concatenate([prev.reshape(shp), t.reshape(shp)], axis=2)

        i = jnp.arange(WINDOW)[:, None]
        j = jnp.arange(2 * WINDOW)[None, :]
        diff = i + WINDOW - j
        local = (diff >= 0) & (diff < WINDOW)
        allowed = local[None] & ((jnp.arange(N) > 0)[:, None, None] | (j >= WINDOW)[None])
        o = _sink_attention(qb, band(k), band(v), allowed, sk).reshape(B, L, SWA_WIDTH)
        buf_len = min(WINDOW, L)
        k_new, v_new = k[:, -buf_len:], v[:, -buf_len:]
    else:
        buf_len = k_buf.shape[1]
        kk = jnp.concatenate([k_buf.astype(k.dtype), k], axis=1)
        vv = jnp.concatenate([v_buf.astype(v.dtype), v], axis=1)
        i = jnp.arange(L)[:, None]
        j = jnp.arange(buf_len + L)[None, :]
        diff = buf_len + i - j
        allowed = ((diff >= 0) & (diff < WINDOW))[None]
        o = _sink_attention(q[:, None], kk[:, None], vv[:, None], allowed, sk).reshape(B, L, SWA_WIDTH)
        k_new, v_new = kk[:, -buf_len:], vv[:, -buf_len:]
    return _rmsnorm(o, norm_w), k_new, v_new


def _layer(x, conv_buf, gdn_s, k_buf, v_buf, p):
    (n1, g1, u1, d1, nm, w_in, conv_w, a_log, dt_bias, gdn_norm, sinks, swa_norm, w_out, n2, g2, u2, d2) = p
    x = x + 0.5 * _swiglu(_rmsnorm(x, n1), g1, u1, d1)
    h = _rmsnorm(x, nm) @ w_in
    gdn_o, conv_new, s_new = _gdn_branch(h[..., :OFF_Z], h[..., OFF_Z:OFF_B], h[..., OFF_B:OFF_A],
                                         h[..., OFF_A:OFF_SQ], conv_buf, gdn_s, conv_w, a_log, dt_bias, gdn_norm)
    swa_o, k_new, v_new = _swa_branch(h[..., OFF_SQ:OFF_SK], h[..., OFF_SK:OFF_SV], h[..., OFF_SV:],
                                      k_buf, v_buf, sinks, swa_norm)
    x = x + jnp.concatenate([gdn_o, swa_o], axis=-1) @ w_out
    x = x + 0.5 * _swiglu(_rmsnorm(x, n2), g2, u2, d2)
    return x, conv_new, s_new, k_new, v_new


def setup_inputs(seed: int = 0) -> dict:
    key = jax.random.key(seed)
    ks = jax.random.split(key, 32)
    f32 = jnp.float32
    nrm = lambda k, shp, s: jax.random.normal(k, shp, f32) * s
    gain = lambda k, shp: 1.0 + 0.02 * jax.random.normal(k, shp, f32)
    buf_len = min(WINDOW, PAST_LEN)
    a_init = jax.random.uniform(ks[10], (DEPTH, GDN_HEADS), f32, 1.0, 16.0)
    dt = jnp.exp(jax.random.uniform(ks[11], (DEPTH, GDN_HEADS), f32) * (math.log(0.1) - math.log(0.001)) + math.log(0.001))
    return {
        "x_prompt": nrm(ks[0], (BATCH, SEQ, D_MODEL), 1.0),
        "x_sample": nrm(ks[1], (DEC_BATCH, DEC_SEQ, D_MODEL), 1.0),
        "state_gdn_conv": nrm(ks[2], (DEPTH, DEC_BATCH, CONV_WIDTH - 1, CONV_CH), 1.0),
        "state_gdn": nrm(ks[3], (DEPTH, DEC_BATCH, GDN_HEADS, GDN_HEAD_DIM, GDN_HEAD_DIM), GDN_HEAD_DIM ** -0.5),
        "cache_swa_k": nrm(ks[4], (DEPTH, DEC_BATCH, buf_len, SWA_KV_HEADS, SWA_HEAD_DIM), 1.0),
        "cache_swa_v": nrm(ks[5], (DEPTH, DEC_BATCH, buf_len, SWA_KV_HEADS, SWA_HEAD_DIM), 1.0),
        "ffn1_norm": gain(ks[6], (DEPTH, D_MODEL)),
        "ffn1_w_gate": nrm(ks[7], (DEPTH, D_MODEL, D_FF), D_MODEL ** -0.5),
        "ffn1_w_up": nrm(ks[8], (DEPTH, D_MODEL, D_FF), D_MODEL ** -0.5),
        "ffn1_w_down": nrm(ks[9], (DEPTH, D_FF, D_MODEL), D_FF ** -0.5),
        "mix_norm": gain(ks[12], (DEPTH, D_MODEL)),
        "w_in": nrm(ks[13], (DEPTH, D_MODEL, IN_COLS), D_MODEL ** -0.5),
        "gdn_conv_w": nrm(ks[14], (DEPTH, CONV_WIDTH, CONV_CH), CONV_WIDTH ** -0.5),
        "gdn_a_log": jnp.log(a_init),
        "gdn_dt_bias": dt + jnp.log(-jnp.expm1(-dt)),
        "gdn_out_norm": gain(ks[15], (DEPTH, GDN_HEAD_DIM)),
        "swa_sinks": nrm(ks[16], (DEPTH, SWA_HEADS), 0.5),
        "swa_out_norm": gain(ks[17], (DEPTH, SWA_WIDTH)),
        "w_out": nrm(ks[18], (DEPTH, MIX_WIDTH, D_MODEL), MIX_WIDTH ** -0.5),
        "ffn2_norm": gain(ks[19], (DEPTH, D_MODEL)),
        "ffn2_w_gate": nrm(ks[20], (DEPTH, D_MODEL, D_FF), D_MODEL ** -0.5),
        "ffn2_w_up": nrm(ks[21], (DEPTH, D_MODEL, D_FF), D_MODEL ** -0.5),
        "ffn2_w_down": nrm(ks[22], (DEPTH, D_FF, D_MODEL), D_FF ** -0.5),
        "final_norm": gain(ks[23], (D_MODEL,)),
    }


def reference(x_prompt, x_sample, state_gdn_conv, state_gdn, cache_swa_k, cache_swa_v,
              ffn1_norm, ffn1_w_gate, ffn1_w_up, ffn1_w_down, mix_norm, w_in, gdn_conv_w,
              gdn_a_log, gdn_dt_bias, gdn_out_norm, swa_sinks, swa_out_norm, w_out,
              ffn2_norm, ffn2_w_gate, ffn2_w_up, ffn2_w_down, final_norm):
    xp, xs = x_prompt, x_sample
    bp = xp.shape[0]
    pc, ps, pk, pv = [], [], [], []
    sc, ss, sk, sv = [], [], [], []
    for l in range(DEPTH):
        p = (ffn1_norm[l], ffn1_w_gate[l], ffn1_w_up[l], ffn1_w_down[l], mix_norm[l], w_in[l],
             gdn_conv_w[l], gdn_a_log[l], gdn_dt_bias[l], gdn_out_norm[l], swa_sinks[l],
             swa_out_norm[l], w_out[l], ffn2_norm[l], ffn2_w_gate[l], ffn2_w_up[l], ffn2_w_down[l])
        xp, c, s, kn, vn = _layer(xp, jnp.zeros((bp, CONV_WIDTH - 1, CONV_CH), xp.dtype),
                                  jnp.zeros((bp, GDN_HEADS, GDN_HEAD_DIM, GDN_HEAD_DIM), jnp.float32),
                                  None, None, p)
        pc.append(c); ps.append(s.astype(state_gdn.dtype)); pk.append(kn); pv.append(vn)
        xs, c, s, kn, vn = _layer(xs, state_gdn_conv[l], state_gdn[l], cache_swa_k[l], cache_swa_v[l], p)
        sc.append(c); ss.append(s.astype(state_gdn.dtype)); sk.append(kn); sv.append(vn)
    y_prompt = _rmsnorm(xp, final_norm)
    y_sample = _rmsnorm(xs, final_norm)
    return (y_prompt, y_sample,
            jnp.stack(pc), jnp.stack(ps), jnp.stack(pk), jnp.stack(pv),
            jnp.stack(sc), jnp.stack(ss), jnp.stack(sk), jnp.stack(sv))
```

```python
import contextlib
import os
SKIP = set(os.environ.get('KSKIP', '').split(','))
NOSELF = set(os.environ.get('KNOSELF', '').split(','))
import numpy as np
import ml_dtypes
import concourse.bass as bass
import concourse.mybir as mybir
from concourse.bass_utils import run_bass_kernel_spmd

F32 = mybir.dt.float32
BF16 = mybir.dt.bfloat16
AF = mybir.ActivationFunctionType
ALU = mybir.AluOpType
AX = mybir.AxisListType

D = 1024
DEPTH = 4
NPR = 2048
NSM = 64
T = NPR + NSM
DFF = 2816
NF = DFF // 128
INC = 2824
OFF_Z, OFF_B, OFF_A, OFF_SQ, OFF_SK, OFF_SV = 1536, 2048, 2052, 2056, 2568, 2696
EPS = 1e-6
TBS = [(0, 512), (512, 512), (1024, 512), (1536, 512), (2048, 64)]
FGROUPS = [(0, 4), (4, 4), (8, 4), (12, 4), (16, 4), (20, 2)]
ENGS = ["pe", "act", "dve", "pool", "sp"]
NDS = 12


ARENA = 134144


class Arena:
    def __init__(self, ap):
        self.ap = ap
        self.top = 0

    def view(self, off, shape, dt):
        esz = 4 if dt == F32 else 2
        n = esz
        for d_ in shape[1:]:
            n *= d_
        assert off % 4 == 0 and off + n <= ARENA, (off, n, shape)
        v = self.ap[0:shape[0], off // 2:(off + n) // 2]
        if dt == F32:
            v = v.bitcast(F32)
        if len(shape) == 3:
            v = v.rearrange("p (a b) -> p a b", a=shape[1])
        elif len(shape) == 4:
            v = v.rearrange("p (a b c) -> p a b c", a=shape[1], b=shape[2])
        return v

    @contextlib.contextmanager
    def alloc(self, shape, dt):
        off = (self.top + 63) // 64 * 64
        v = self.view(off, shape, dt)
        esz = 4 if dt == F32 else 2
        n = esz
        for d_ in shape[1:]:
            n *= d_
        old = self.top
        self.top = off + n
        try:
            yield v
        finally:
            self.top = old


class Op:
    __slots__ = ("fn", "deps", "dma", "sig", "count", "dsem", "dval", "gsem", "gval")

    def __init__(self, fn, deps, dma):
        self.fn = fn
        self.deps = deps
        self.dma = dma
        self.sig = False
        self.count = 0
        self.dsem = None
        self.dval = 0
        self.gval = 0


class Prog:
    def __init__(self, nc, sems, dsems):
        self.nc = nc
        self.sems = sems
        self.dsems = dsems
        self.cnt = {e: 0 for e in ENGS}
        self.dcnt = {e: 0 for e in ENGS}
        self.waited = {e: {} for e in ENGS}
        self.reset()

    def reset(self):
        self.ops = {e: [] for e in ENGS}
        self.last_w = {}
        self.readers = {}

    def op(self, eng, fn, r=(), w=(), dma=False):
        idx = len(self.ops[eng])
        w = list(w) + [t for t in r if t.startswith("ps")]
        r = [t for t in r if not t.startswith("ps")]
        deps = set()
        for t in r:
            lw = self.last_w.get(t)
            if lw is not None:
                deps.add(lw)
        for t in w:
            lw = self.last_w.get(t)
            if lw is not None:
                deps.add(lw)
            for rd in self.readers.get(t, ()):
                deps.add(rd)
        if eng == "pe" or (eng in NOSELF):
            deps = {d for d in deps if d[0] != eng}
        deps.discard((eng, idx))
        self.ops[eng].append(Op(fn, deps, dma))
        for t in w:
            self.last_w[t] = (eng, idx)
            self.readers[t] = []
        for t in r:
            self.readers.setdefault(t, []).append((eng, idx))

    def flush(self, name):
        for e in ENGS:
            for op in self.ops[e]:
                for (te, ti) in op.deps:
                    t = self.ops[te][ti]
                    if not t.dma:
                        t.sig = True
        for e in ENGS:
            for op in self.ops[e]:
                if op.dma:
                    i = self.dcnt[e]
                    self.dcnt[e] += 1
                    op.dsem = self.dsems[e][i % NDS]
                    op.dval = 16 * (i // NDS + 1)
                    op.gval = 16 * (i // NDS)
                elif op.sig:
                    self.cnt[e] += 1
                    op.count = self.cnt[e]
        with self.nc.Block() as blk:
            for e, bname in (("pe", "tensor"), ("act", "scalar"), ("dve", "vector"),
                             ("pool", "gpsimd"), ("sp", "sync")):
                getattr(blk, bname)(self._body(e))
        self.reset()

    def _body(self, e):
        ops = self.ops
        allops = self.ops

        def body(eng):
            waited = self.waited[e]

            def wait(sem, val):
                key = id(sem)
                if waited.get(key, 0) < val:
                    eng.wait_ge(sem, val)
                    waited[key] = val

            last_d = {}
            for op in ops[e]:
                for (te, ti) in sorted(op.deps):
                    t = allops[te][ti]
                    if t.dma:
                        wait(t.dsem, t.dval)
                    else:
                        wait(self.sems[te], t.count)
                if op.dma and op.gval > 0:
                    wait(op.dsem, op.gval)
                inst = op.fn(eng)
                if op.dma:
                    inst.then_inc(op.dsem, 16)
                    last_d[id(op.dsem)] = (op.dsem, op.dval)
                elif op.sig:
                    inst.then_inc(self.sems[e], 1)
            for (sem, val) in last_d.values():
                wait(sem, val)

        return body


def build_nc(nlayers=DEPTH, dbg=None):
    NL = nlayers
    nc = bass.Bass("TRN2", target_bir_lowering=False)
    dram = {}

    def din(name, shape, dt=F32):
        dram[name] = nc.dram_tensor(name, list(shape), dt, kind="ExternalInput").ap()
        return dram[name]

    def dout(name, shape, dt=F32):
        dram[name] = nc.dram_tensor(name, list(shape), dt, kind="ExternalOutput").ap()
        return dram[name]

    xp = din("xp", [NPR, D])
    xs = din("xs", [16, 4, D])
    smalls = din("smalls", [128, NSMALL])
    identf_d = din("identf", [128, 128])
    w1g = din("w1g", [NL, D, DFF])
    w1u = din("w1u", [NL, D, DFF])
    w1d = din("w1d", [NL, DFF, D])
    w2g = din("w2g", [NL, D, DFF])
    w2u = din("w2u", [NL, D, DFF])
    w2d = din("w2d", [NL, DFF, D])
    w_in = din("w_in", [NL, D, INC])
    w_out = din("w_out", [NL, D, D])
    consts_d = din("consts", [128, NCONST])
    identb_d = din("identb", [128, 128], BF16)
    ck_d = din("ck", [NL, 16, 128, 128])
    cv_d = din("cv", [NL, 16, 128, 128])
    cstate_d = din("cstate", [NL, 48, 1536])
    sgd = din("sgd", [NL, 16, 4, 128, 128])
    maskS_d = din("maskS", [128, 16, 64], BF16)
    maskb_d = din("maskb", [128, 256], BF16)
    Ad = nc.dram_tensor("Ad_scr", [132, 4096], F32, kind="Internal").ap()
    Ud = nc.dram_tensor("Ud_scr", [132, 4096], F32, kind="Internal").ap()
    ocp = dout("ocp", [NL, 3, 1536])
    ocs = dout("ocs", [NL, 16, 3, 1536])
    ogp = dout("ogp", [NL, 4, 128, 128])
    ogs = dout("ogs", [NL, 16, 4, 128, 128])
    okp = dout("okp", [NL, 128, 128])
    ovp = dout("ovp", [NL, 128, 128])
    oks = dout("oks", [NL, 16, 128, 128])
    ovs = dout("ovs", [NL, 16, 128, 128])
    yp = dout("yp", [NPR, D])
    ys = dout("ys", [16, 4, D])
    if dbg:
        dbgx = dout("dbgx", [128, 8, T])

    uid = [0]

    def SBT(name, shape, dt=F32):
        return AR.alloc(list(shape), dt)

    def SBR(name, shape, dt=F32):
        uid[0] += 1
        return nc.sbuf_tensor(f"{name}_u{uid[0]}", list(shape), dt)

    es = contextlib.ExitStack()
    with es:
        def sb(name, shape, dt=F32):
            return es.enter_context(SBR(name, list(shape), dt))

        sems = {e: es.enter_context(nc.semaphore("s_" + e)) for e in ENGS}
        dsems = {e: [es.enter_context(nc.semaphore(f"d_{e}{i}")) for i in range(NDS)]
                 for e in ("sp", "pool", "act")}
        P = Prog(nc, sems, dsems)
        pall = es.enter_context(nc.psum_tensor("pall", [128, 4096], F32))
        ps = [pall[:, i * 512:(i + 1) * 512] for i in range(8)]

        xT = sb("xT", [128, 8, T])
        arena_t = sb("arena", [128, ARENA // 2], BF16)
        AR = Arena(arena_t[:])
        XN_BYTES = 8 * T * 2
        xn = AR.view(0, [128, 8, T], BF16)
        AR.top = XN_BYTES
        sm = sb("sm", [128, NSMALL])
        identf = sb("identf_sb", [128, 128])
        ones_bf = sb("ones_bf", [128, 128], BF16)
        epsb = sb("epsb", [128, 1])
        cst = sb("cst", [128, NCONST])
        identb = sb("identb_sb", [128, 128], BF16)
        ones_f = sb("ones_f", [64, 128])
        e0sel = sb("e0sel", [64, 128])
        oneb = sb("oneb", [128, 1])
        triu_bf = sb("triu_bf", [64, 64], BF16)
        maskS = sb("maskS_sb", [128, 16, 64], BF16)
        maskb = sb("maskb_sb", [128, 256], BF16)

        P.op("sp", lambda e: e.dma_start(out=sm[:], in_=smalls), w=["sm"], dma=True)
        P.op("sp", lambda e: e.dma_start(out=identf[:], in_=identf_d), w=["identf"], dma=True)
        P.op("sp", lambda e: e.dma_start(out=cst[:], in_=consts_d), w=["cst"], dma=True)
        P.op("sp", lambda e: e.dma_start(out=identb[:], in_=identb_d), w=["identb"], dma=True)
        P.op("dve", lambda e: e.memset(ones_bf[:], 1.0), w=["ones"])
        P.op("dve", lambda e: e.memset(epsb[:], EPS), w=["epsb"])
        P.op("dve", lambda e: e.memset(ones_f[:], 1.0), w=["onesf"])
        P.op("dve", lambda e: e.memset(oneb[:], 1.0), w=["oneb"])
        P.op("dve", lambda e: e.memset(e0sel[:], 0.0), w=["e0sel"])
        P.op("dve", lambda e: e.memset(e0sel[0:1, :], 1.0), w=["e0sel"])
        P.op("dve", lambda e: e.tensor_copy(out=triu_bf[:], in_=cst[0:64, C_TRIU:C_TRIU + 64]), r=["cst"], w=["triu"])
        P.op("sp", lambda e: e.dma_start(out=maskS[:], in_=maskS_d), w=["maskS"], dma=True)
        P.op("sp", lambda e: e.dma_start(out=maskb[:], in_=maskb_d), w=["maskb"], dma=True)
        with contextlib.ExitStack() as ph:
            xin = [ph.enter_context(SBT(f"xin{i}", [128, D], F32)) for i in range(2)]
            for tt in range(17):
                b = tt % 2
                rows = 128 if tt < 16 else 64
                if tt < 16:
                    src = xp[tt * 128:(tt + 1) * 128, :]
                    P.op("sp", lambda e, b=b, rows=rows, src=src: e.dma_start(out=xin[b][0:rows, :], in_=src),
                         w=[f"xin{b}"], dma=True)
                else:
                    for t_ in range(4):
                        P.op("sp", lambda e, b=b, t_=t_: e.dma_start(out=xin[b][t_ * 16:(t_ + 1) * 16, :],
                                                                     in_=xs[:, t_, :]),
                             w=[f"xin{b}"], dma=True)
                for half in range(2):
                    pb = ps[(tt * 2 + half) % 4]
                    ptag = f"ps{(tt * 2 + half) % 4}"
                    for c4 in range(4):
                        c = half * 4 + c4
                        P.op("pe", lambda e, pb=pb, c4=c4, c=c, b=b, rows=rows: e.transpose(
                            out=pb[:, c4 * 128:c4 * 128 + rows], in_=xin[b][0:rows, c * 128:(c + 1) * 128],
                            identity=identf[0:rows, 0:rows]),
                            r=[f"xin{b}", "identf"], w=[ptag])
                    P.op("dve", lambda e, pb=pb, half=half, tt=tt, rows=rows: e.tensor_copy(
                        out=xT[:, half * 4:half * 4 + 4, tt * 128:tt * 128 + rows],
                        in_=pb.rearrange("p (c t) -> p c t", c=4)[:, :, 0:rows]),
                        r=[ptag], w=[f"xT{tt // 4}"])
            P.flush("init")

        def xtag(t0):
            return f"xT{t0 // 512}"

        def rmsnorm_ops(ph, wcol, pb=(6, 7)):
            sq = [ph.enter_context(SBT(f"sq{i}", [128, 8, 512], BF16)) for i in range(2)]
            rstd = [ph.enter_context(SBT(f"rstd{i}", [128, 512], F32)) for i in range(2)]

            def square(bi):
                t0, tn = TBS[bi]
                b = bi % 2
                E("act", "activation", [xtag(t0)], [f"nsq{b}"], out=sq[b][:, :, 0:tn], in_=xT[:, :, t0:t0 + tn], func=AF.Square)

            square(0)
            for bi, (t0, tn) in enumerate(TBS):
                b = bi % 2
                pk = pb[b]
                if bi + 1 < len(TBS):
                    square(bi + 1)
                for c in range(8):
                    E("pe", "matmul", [f"nsq{b}", "ones"], [f"ps{pk}"], out=ps[pk][:, 0:tn], lhsT=ones_bf[:], rhs=sq[b][:, c, 0:tn],
                      start=(c == 0), stop=(c == 7))
                E("act", "activation", [f"ps{pk}", "epsb"], [f"nrstd{b}"], out=rstd[b][:, 0:tn], in_=ps[pk][:, 0:tn], func=AF.Ln, scale=1.0 / D, bias=epsb[:])
                E("act", "activation", [f"nrstd{b}"], [f"nrstd{b}"], out=rstd[b][:, 0:tn], in_=rstd[b][:, 0:tn], func=AF.Exp, scale=-0.5)
                for c in range(8):
                    E("dve", "scalar_tensor_tensor", [xtag(t0), "sm", f"nrstd{b}"], [f"xn{bi}"], out=xn[:, c, t0:t0 + tn], in0=xT[:, c, t0:t0 + tn],
                      scalar=sm[:, wcol + c:wcol + c + 1], in1=rstd[b][:, 0:tn], op0=ALU.mult, op1=ALU.mult)

        def ffn(l, wg_d, wu_d, wd_d, ncol):
            with contextlib.ExitStack() as ph:
                wgu = [ph.enter_context(SBT(f"wgu{i}", [128, 2, 8, 512], BF16)) for i in range(2)]
                wdb = [ph.enter_context(SBT(f"wdb{i}", [128, 4, D], BF16)) for i in range(2)]
                hb = ph.enter_context(SBT("hb", [128, 4, T], BF16))
                sg = [ph.enter_context(SBT(f"sg{i}", [128, 512], F32)) for i in range(2)]
                rmsnorm_ops(ph, ncol)
                cnt = 0
                for gi, (f0, fn_) in enumerate(FGROUPS):
                    b = gi % 2
                    wcols = fn_ * 128
                    for fl in range(fn_):
                        for which, wsrc in ((0, wg_d), (1, wu_d)):
                            DMA("pool", [], [f"wgu{b}_{which}_{fl}"], out=wgu[b][:, which, :, fl * 128:(fl + 1) * 128],
                                in_=wsrc[l, :, (f0 + fl) * 128:(f0 + fl + 1) * 128].rearrange("(c p) f -> p c f", p=128))
                    for fl in range(fn_):
                        DMA("pool", [], [f"wdb{b}_{fl}"], out=wdb[b][:, fl, :], in_=wd_d[l, (f0 + fl) * 128:(f0 + fl + 1) * 128, :])
                    for bi, (t0, tn) in enumerate(TBS):
                        for fl in range(fn_):
                            pg, pu = ps[(cnt % 2) * 2], ps[(cnt % 2) * 2 + 1]
                            tg, tu = f"ps{(cnt % 2) * 2}", f"ps{(cnt % 2) * 2 + 1}"
                            sgi = cnt % 2
                            cnt += 1
                            for which, pt, tag in ((0, pg, tg), (1, pu, tu)):
                                for c in range(8):
                                    P.op("pe", lambda e, pt=pt, b=b, which=which, c=c, fl=fl, t0=t0, tn=tn: e.matmul(
                                        pt[:, 0:tn], lhsT=wgu[b][:, which, c, fl * 128:(fl + 1) * 128],
                                        rhs=xn[:, c, t0:t0 + tn], start=(c == 0), stop=(c == 7)),
                                        r=[f"wgu{b}_{which}_{fl}", f"xn{bi}"], w=[tag])
                            P.op("act", lambda e, pg=pg, sgi=sgi, tn=tn: e.activation(
                                out=sg[sgi][:, 0:tn], in_=pg[:, 0:tn], func=AF.Silu), r=[tg], w=[f"sg{sgi}"])
                            P.op("dve", lambda e, pu=pu, sgi=sgi, fl=fl, t0=t0, tn=tn: e.tensor_tensor(
                                out=hb[:, fl, t0:t0 + tn], in0=sg[sgi][:, 0:tn], in1=pu[:, 0:tn], op=ALU.mult),
                                r=[tu, f"sg{sgi}"], w=[f"hb{bi}"])
                    for bi, (t0, tn) in enumerate(TBS):
                        for o in range(8):
                            py, ty = ps[4 + (cnt % 4)], f"ps{4 + (cnt % 4)}"
                            cnt += 1
                            for fl in range(fn_):
                                P.op("pe", lambda e, py=py, b=b, fl=fl, o=o, t0=t0, tn=tn, fn_=fn_: e.matmul(
                                    py[:, 0:tn], lhsT=wdb[b][:, fl, o * 128:(o + 1) * 128],
                                    rhs=hb[:, fl, t0:t0 + tn], start=(fl == 0), stop=(fl == fn_ - 1)),
                                    r=[f"wdb{b}_{fl}", f"hb{bi}"], w=[ty])
                            P.op("dve", lambda e, py=py, o=o, t0=t0, tn=tn: e.scalar_tensor_tensor(
                                out=xT[:, o, t0:t0 + tn], in0=py[:, 0:tn], scalar=0.5, in1=xT[:, o, t0:t0 + tn],
                                op0=ALU.mult, op1=ALU.add), r=[ty, xtag(t0)], w=[xtag(t0)])
                P.flush("ffn")


        def swa(l):
            with contextlib.ExitStack() as ph:
                def pb_(name, shape, dt=F32):
                    return ph.enter_context(SBT(name, list(shape), dt))
                wsw = pb_("wsw", [128, 8, 768], BF16)
                wkd = pb_("wkd", [128, 8, 2, 128], BF16)
                wo = pb_("wo_s", [128, 4, D], BF16)
                kd = pb_("kd", [128, 2, T], BF16)
                vtm = pb_("vtm", [128, 17, 128], BF16)
                kvo = [pb_(f"kvo{i}", [128, 128]) for i in range(4)]
                so = pb_("so", [128, 4, 128])
                sq = pb_("sq_s", [128, 4, 128], BF16)
                rstd = pb_("rstd_s", [128, 128])
                mo = pb_("mo", [128, 4, 128], BF16)
                p2 = contextlib.ExitStack()
                p2.__enter__()
                def pb2(name, shape, dt=F32):
                    return p2.enter_context(SBT(name, list(shape), dt))
                qT = pb2("qT", [128, 4, T], BF16)
                sc = [pb2("sc", [128, 4, 256]) for _ in range(2)]
                mx = [pb2("mx", [128, 4]) for _ in range(2)]; mx2 = [pb2("mx2", [128, 4]) for _ in range(2)]
                rs = [pb2("rs", [128, 4]) for _ in range(2)]; es_ = [pb2("es_", [128, 4]) for _ in range(2)]
                pn = [pb2("pn", [128, 4, 256], BF16) for _ in range(2)]
                pTs = [pb2("pT", [128, 8, 128], BF16) for _ in range(2)]
                rmsnorm_ops(p2, SM_NM + 8 * l)
                P.op("pool", lambda e: e.dma_start(out=wsw[:], in_=w_in[l, :, OFF_SQ:OFF_SQ + 768].rearrange(
                    "(c p) f -> p c f", p=128)), w=["wsw"], dma=True)
                for g in range(2):
                    for hf in range(2):
                        P.op("pool", lambda e, g=g, hf=hf: e.dma_start(
                            out=wkd[:, :, g, hf * 64:(hf + 1) * 64],
                            in_=w_in[l, :, OFF_SK + g * 64:OFF_SK + (g + 1) * 64].rearrange("(c p) f -> p c f", p=128)),
                            w=["wkd"], dma=True)
                P.op("pool", lambda e: e.dma_start(out=wo[:], in_=w_out[l, 512:1024, :].rearrange(
                    "(c p) f -> p c f", p=128)), w=["wo"], dma=True)
                if "d2d" not in SKIP:
                    P.op("sp", lambda e: e.dma_start(out=oks[l, :, 0:124, :], in_=ck_d[l, :, 4:128, :]), dma=True)
                    P.op("sp", lambda e: e.dma_start(out=ovs[l, :, 0:124, :], in_=cv_d[l, :, 4:128, :]), dma=True)

                cnt = [0]
                def nb():
                    cnt[0] += 1
                    return cnt[0] % 4
                for j in range(4 if "projq" not in SKIP else 0):
                    for bi, (t0, tn) in enumerate(TBS):
                        k_ = nb()
                        for c in range(8):
                            P.op("pe", lambda e, k_=k_, c=c, j=j, t0=t0, tn=tn: e.matmul(
                                ps[k_][:, 0:tn], lhsT=wsw[:, c, j * 128:(j + 1) * 128], rhs=xn[:, c, t0:t0 + tn],
                                start=(c == 0), stop=(c == 7)), r=["wsw", f"xn{bi}"], w=[f"ps{k_}"])
                        P.op("act", lambda e, k_=k_, j=j, t0=t0, tn=tn: e.copy(out=qT[:, j, t0:t0 + tn], in_=ps[k_][:, 0:tn]),
                             r=[f"ps{k_}"], w=["qT"])
                for g in range(2 if "projk" not in SKIP else 0):
                    for bi, (t0, tn) in enumerate(TBS):
                        k_ = nb()
                        for c in range(8):
                            P.op("pe", lambda e, k_=k_, c=c, g=g, t0=t0, tn=tn: e.matmul(
                                ps[k_][:, 0:tn], lhsT=wkd[:, c, g, :], rhs=xn[:, c, t0:t0 + tn],
                                start=(c == 0), stop=(c == 7)), r=["wkd", f"xn{bi}"], w=[f"ps{k_}"])
                        P.op("act", lambda e, k_=k_, g=g, t0=t0, tn=tn: e.copy(out=kd[:, g, t0:t0 + tn], in_=ps[k_][:, 0:tn]),
                             r=[f"ps{k_}"], w=["kd"])
                for tt in range(17 if "projv" not in SKIP else 0):
                    rows = 128 if tt < 16 else 64
                    k_ = nb()
                    for c in range(8):
                        P.op("pe", lambda e, k_=k_, c=c, tt=tt, rows=rows: e.matmul(
                            ps[k_][0:rows, 0:256], lhsT=xn[:, c, tt * 128:tt * 128 + rows], rhs=wsw[:, c, 512:768],
                            start=(c == 0), stop=(c == 7)), r=["wsw", f"xn{tt // 4}"], w=[f"ps{k_}"])
                    P.op("act", lambda e, k_=k_, tt=tt, rows=rows: e.copy(out=vtm[0:rows, tt, :], in_=ps[k_][0:rows, 128:256]),
                         r=[f"ps{k_}"], w=["vtm"])
                    if tt >= 15:
                        i0 = (tt - 15) * 2
                        P.op("dve", lambda e, k_=k_, i0=i0, rows=rows: e.tensor_copy(out=kvo[i0][0:rows, :], in_=ps[k_][0:rows, 0:128]),
                             r=[f"ps{k_}"], w=[f"kvo{i0}"])
                        P.op("dve", lambda e, k_=k_, i0=i0, rows=rows: e.tensor_copy(out=kvo[i0 + 1][0:rows, :], in_=ps[k_][0:rows, 128:256]),
                             r=[f"ps{k_}"], w=[f"kvo{i0 + 1}"])
                if "kvo" not in SKIP:
                    P.op("sp", lambda e: e.dma_start(out=okp[l], in_=kvo[0][:]), r=["kvo0"], dma=True)
                    P.op("sp", lambda e: e.dma_start(out=ovp[l], in_=kvo[1][:]), r=["kvo1"], dma=True)
                for t_ in range(4 if "kvo" not in SKIP else 0):
                    P.op("sp", lambda e, t_=t_: e.dma_start(out=oks[l, :, 124 + t_, :], in_=kvo[2][t_ * 16:(t_ + 1) * 16, :]),
                         r=["kvo2"], dma=True)
                    P.op("sp", lambda e, t_=t_: e.dma_start(out=ovs[l, :, 124 + t_, :], in_=kvo[3][t_ * 16:(t_ + 1) * 16, :]),
                         r=["kvo3"], dma=True)

                def epilogue_gen(t0, n):
                    E("act", "activation", ["so"], ["sq_s"], out=sq[:, :, 0:n], in_=so[:, :, 0:n], func=AF.Square)
                    yield
                    for c in range(4):
                        E("pe", "matmul", ["sq_s", "ones"], ["ps3"], out=ps[3][:, 0:n], lhsT=ones_bf[:], rhs=sq[:, c, 0:n], start=(c == 0), stop=(c == 3))
                    yield
                    E("act", "activation", ["ps3", "epsb"], ["rstd_s"], out=rstd[:, 0:n], in_=ps[3][:, 0:n], func=AF.Ln, scale=1.0 / 512, bias=epsb[:])
                    yield
                    E("act", "activation", ["rstd_s"], ["rstd_s"], out=rstd[:, 0:n], in_=rstd[:, 0:n], func=AF.Exp, scale=-0.5)
                    yield
                    for c in range(4):
                        E("dve", "scalar_tensor_tensor", ["so", "sm", "rstd_s"], ["mo"], out=mo[:, c, 0:n], in0=so[:, c, 0:n],
                          scalar=sm[:, SM_SWN + 4 * l + c:SM_SWN + 4 * l + c + 1], in1=rstd[:, 0:n], op0=ALU.mult, op1=ALU.mult)
                        if c % 2 == 1:
                            yield
                    for half in range(2):
                        for o4 in range(4):
                            o = half * 4 + o4
                            for c in range(4):
                                E("pe", "matmul", ["wo", "mo"], ["ps7"], out=ps[7][:, o4 * 128:o4 * 128 + n], lhsT=wo[:, c, o * 128:(o + 1) * 128],
                                  rhs=mo[:, c, 0:n], start=(c == 0), stop=(c == 3))
                        yield
                        E("dve", "tensor_tensor", ["ps7", xtag(t0)], [xtag(t0)], out=xT[:, half * 4:half * 4 + 4, t0:t0 + n],
                          in0=ps[7].rearrange("p (o t) -> p o t", o=4)[:, :, 0:n], in1=xT[:, half * 4:half * 4 + 4, t0:t0 + n], op=ALU.add)
                        yield

                def epilogue(t0, n):
                    for _ in epilogue_gen(t0, n):
                        pass

                mask2 = cst[:, C_MASK2:C_MASK2 + 256]
                S4s = [pall[:, 0:1024].rearrange("p (h k) -> p h k", h=4), pall[:, 2048:3072].rearrange("p (h k) -> p h k", h=4)]
                S4t = [("ps0", "ps1"), ("ps4", "ps5")]
                PTbs = [ps[2].bitcast(BF16).rearrange("p (i q) -> p i q", i=8), ps[6].bitcast(BF16).rearrange("p (i q) -> p i q", i=8)]
                PO = ps[3].rearrange("p (j q) -> p j q", j=4)
                items = [(b, g) for b in range(16) for g in range(2)]

                def geom(b):
                    c0 = 128 if b == 0 else 0
                    return c0, 256 - c0, (b - 1) * 128 + c0

                def scores(i):
                    b, g = items[i]
                    c0, nk, k0 = geom(b)
                    par = i % 2
                    for hl in range(4):
                        h = g * 4 + hl
                        base = (h % 2) * 64
                        sl_ = (hl % 2) * 2 + hl // 2
                        E("pe", "matmul", ["qT", "kd"], [S4t[par][sl_ // 2]], out=S4s[par][:, sl_, c0:256],
                          lhsT=qT[base:base + 64, h // 2, b * 128:(b + 1) * 128], rhs=kd[base:base + 64, g, k0:k0 + nk], start=True, stop=False)
                        E("pe", "matmul", ["identb", "maskb"], [S4t[par][sl_ // 2]], out=S4s[par][:, sl_, c0:256],
                          lhsT=identb[:], rhs=maskb[:, c0:256], start=False, stop=True)

                def softmax_gen(i):
                    b, g = items[i]
                    c0, nk, k0 = geom(b)
                    par = i % 2
                    S4, sc_, pn_ = S4s[par], sc[par], pn[par]
                    mx_, mx2_, rs_, es2 = mx[par], mx2[par], rs[par], es_[par]
                    tg = lambda nm: f"{nm}{par}"
                    pt = [S4t[par][0], S4t[par][1]]
                    sk = sm[:, SM_SINK + 8 * l + 4 * g:SM_SINK + 8 * l + 4 * g + 4]
                    E("dve", "tensor_reduce", pt, [tg("mx")], out=mx_[:], in_=S4[:, :, c0:256], axis=AX.X, op=ALU.max)
                    yield
                    E("dve", "scalar_tensor_tensor", [tg("mx"), "sm"], [tg("mx2")], out=mx2_[:], in0=mx_[:], scalar=0.125, in1=sk, op0=ALU.mult, op1=ALU.max)
                    yield
                    E("dve", "tensor_scalar", [tg("mx2")], [tg("mx")], out=mx_[:], in0=mx2_[:], scalar1=-1.0, scalar2=None, op0=ALU.mult)
                    yield
                    for sl_ in range(4):
                        E("act", "activation", [pt[sl_ // 2], tg("mx")], [tg("sc"), tg("rs")], out=sc_[:, sl_, c0:256], in_=S4[:, sl_, c0:256], func=AF.Exp,
                          scale=0.125, bias=mx_[:, sl_:sl_ + 1], accum_out=rs_[:, sl_:sl_ + 1])
                    yield
                    E("dve", "tensor_tensor", [tg("mx"), "sm"], [tg("es")], out=es2[:], in0=sk, in1=mx_[:], op=ALU.add)
                    yield
                    E("act", "activation", [tg("es")], [tg("es")], out=es2[:], in_=es2[:], func=AF.Exp)
                    yield
                    E("dve", "tensor_tensor", [tg("rs"), tg("es")], [tg("rs")], out=rs_[:], in0=rs_[:], in1=es2[:], op=ALU.add)
                    yield
                    E("dve", "reciprocal", [tg("rs")], [tg("rs")], out=rs_[:], in_=rs_[:])
                    yield
                    E("dve", "tensor_tensor", [tg("sc"), tg("rs")], [tg("pn")], out=pn_[:, :, c0:256], in0=sc_[:, :, c0:256],
                      in1=bc(rs_[:], 2, [128, 4, nk]), op=ALU.mult)
                    yield

                def tail(i):
                    b, g = items[i]
                    par = i % 2
                    pn_ = pn[par]
                    PTb = PTbs[par]; pT = pTs[par]; ptag = ["ps2", "ps6"][par]; ttag = f"pT{par}"
                    kts = [1] if b == 0 else [0, 1]
                    for hl in range(4):
                        for kt in kts:
                            E("pe", "transpose", [f"pn{par}", "identb"], [ptag], out=PTb[:, hl * 2 + kt, :],
                              in_=pn_[:, (hl % 2) * 2 + hl // 2, kt * 128:(kt + 1) * 128], identity=identb[:])
                    if b == 0:
                        E("act", "copy", [ptag], [ttag], out=pT[:, 1::2, :], in_=PTb[:, 1::2, :])
                    else:
                        E("act", "copy", [ptag], [ttag], out=pT[:], in_=PTb)
                    for hl in range(4):
                        r0 = (hl % 2) * 64
                        for kt in kts:
                            E("pe", "matmul", ["vtm", ttag], ["ps3"], out=PO[r0:r0 + 64, g * 2 + hl // 2, :], lhsT=vtm[:, b - 1 + kt, g * 64:(g + 1) * 64],
                              rhs=pT[:, hl * 2 + kt, :], start=(kt == kts[0]), stop=(kt == kts[-1]), tile_position=(0, r0))

                if "attn" not in SKIP:
                    scores(0)
                    scores(1)
                    for b_ in range(16):
                        gens = [softmax_gen(2 * b_), softmax_gen(2 * b_ + 1)]
                        for _ in range(4):
                            for g_ in gens:
                                next(g_)
                        if b_ + 1 < 16:
                            scores(2 * b_ + 2)
                            scores(2 * b_ + 3)
                        alive = list(gens)
                        if b_ > 0:
                            alive.append(epilogue_gen((b_ - 1) * 128, 128))
                        while alive:
                            for g_ in list(alive):
                                try:
                                    next(g_)
                                except StopIteration:
                                    alive.remove(g_)
                        tail(2 * b_)
                        tail(2 * b_ + 1)
                        E("act", "copy", ["ps3"], ["so"], out=so[:], in_=PO)
                    epilogue(15 * 128, 128)

                P.flush("swa_p")
                p2.__exit__(None, None, None)
                if dbg == "swa_p":
                    return
                p3 = contextlib.ExitStack()
                p3.__enter__()
                def pb3(name, shape, dt=F32):
                    return p3.enter_context(SBT(name, list(shape), dt))
                ckb = pb3("ckb", [128, 16, 128], BF16)
                cvb = pb3("cvb", [128, 16, 128], BF16)
                qtm = pb3("qtm", [64, 512], BF16)
                qTs = pb3("qTs", [64, 16, 8, 4], BF16)
                kf = pb3("kf", [64, 16, 2, 132], BF16)
                scs = pb3("scs", [16, 8, 132])
                mxs = pb3("mxs", [16, 8]); mxs2 = pb3("mxs2", [16, 8]); rss = pb3("rss", [16, 8]); ess = pb3("ess", [16, 8])
                pc = pb3("pc", [16, 8, 128], BF16)
                pz = pb3("pz", [16, 16, 2, 64], BF16)
                ptc = pb3("ptc", [128, 8, 16], BF16)
                ptz = pb3("ptz", [64, 8, 16], BF16)
                osb = pb3("osb", [16, 16, 2, 64])
                otm = pb3("otm", [64, 512])
                P.op("pool", lambda e: e.dma_start(out=ckb[:], in_=ck_d[l].rearrange("s k f -> k s f")), w=["ckb"], dma=True)
                P.op("pool", lambda e: e.dma_start(out=cvb[:], in_=cv_d[l].rearrange("s k f -> k s f")), w=["cvb"], dma=True)
                P.op("dve", lambda e: e.memset(pz[:], 0.0), w=["pz"])
                k_ = 0
                for c in range(8):
                    P.op("pe", lambda e, c=c: e.matmul(ps[0][0:64, :], lhsT=xn[:, c, NPR:T], rhs=wsw[:, c, 0:512],
                                                      start=(c == 0), stop=(c == 7)), r=["wsw", "xn4"], w=["ps0"])
                P.op("act", lambda e: e.copy(out=qtm[:], in_=ps[0][0:64, :]), r=["ps0"], w=["qtm"])
                QTb = ps[1].bitcast(BF16)[0:64, 0:512].rearrange("p (h t s) -> p h t s", h=8, t=4)
                for h in range(8):
                    P.op("pe", lambda e, h=h: e.transpose(out=ps[1].bitcast(BF16)[0:64, h * 64:(h + 1) * 64],
                                                          in_=qtm[:, h * 64:(h + 1) * 64], identity=identb[0:64, 0:64]),
                         r=["qtm", "identb"], w=["ps1"])
                P.op("dve", lambda e: e.tensor_copy(out=qTs[:].rearrange("p s h t -> p h t s"), in_=QTb), r=["ps1"], w=["qTs"])
                KTb = ps[2].bitcast(BF16)[0:64, :].rearrange("p (i k) -> p i k", i=8)
                for w4 in range(4):
                    for sl in range(4):
                        for g in range(2):
                            P.op("pe", lambda e, w4=w4, sl=sl, g=g: e.transpose(
                                out=KTb[:, sl * 2 + g, :], in_=ckb[:, w4 * 4 + sl, g * 64:(g + 1) * 64], identity=identb[:]),
                                r=["ckb", "identb"], w=["ps2"])
                    P.op("act", lambda e, w4=w4: e.copy(
                        out=kf[:, w4 * 4:w4 * 4 + 4, :, 0:128],
                        in_=KTb.rearrange("p (s g) k -> p s g k", g=2)), r=["ps2"], w=["kf"])
                P.op("dve", lambda e: e.tensor_copy(
                    out=kf[:, :, :, 128:132].rearrange("p s g t -> p g t s"),
                    in_=kd[0:64, :, NPR:T].rearrange("p g (t s) -> p g t s", t=4)), r=["kd"], w=["kf"])
                smask = cst[0:16, C_SMASK:C_SMASK + 132]
                SC = pall[0:16, 0:1024].rearrange("p (i k) -> p i k", i=8)
                SN = ps[2][0:16, 0:32].rearrange("p (i k) -> p i k", i=8)
                PTC = ps[3].bitcast(BF16)[:, 0:128].rearrange("p (i q) -> p i q", i=8)
                PTZ = ps[3].bitcast(BF16)[0:64, 128:256].rearrange("p (i q) -> p i q", i=8)
                OS = ps[4][0:16, :].rearrange("p (i d) -> p i d", i=8)
                for w4 in range(4):
                    for sl in range(4):
                        s_ = w4 * 4 + sl
                        for g in range(2):
                            i = sl * 2 + g
                            P.op("pe", lambda e, s_=s_, g=g, i=i: e.matmul(
                                SC[:, i, :], lhsT=qTs[:, s_, g * 4:(g + 1) * 4, :], rhs=kf[:, s_, g, 0:128],
                                start=True, stop=True), r=["qTs", "kf"], w=[f"ps{i // 4}"])
                            P.op("pe", lambda e, s_=s_, g=g, i=i: e.matmul(
                                SN[:, i, :], lhsT=qTs[:, s_, g * 4:(g + 1) * 4, :], rhs=kf[:, s_, g, 128:132],
                                start=True, stop=True), r=["qTs", "kf"], w=["ps2"])
                    P.op("dve", lambda e: e.scalar_tensor_tensor(
                        out=scs[:, :, 0:128], in0=SC, scalar=0.125,
                        in1=smask[:, 0:128].unsqueeze(1).broadcast_to([16, 8, 128]), op0=ALU.mult, op1=ALU.add),
                        r=["ps0", "ps1", "cst"], w=["scs"])
                    P.op("dve", lambda e: e.scalar_tensor_tensor(
                        out=scs[:, :, 128:132], in0=SN, scalar=0.125,
                        in1=smask[:, 128:132].unsqueeze(1).broadcast_to([16, 8, 4]), op0=ALU.mult, op1=ALU.add),
                        r=["ps2", "cst"], w=["scs"])
                    P.op("dve", lambda e: e.tensor_reduce(out=mxs[:], in_=scs[:], axis=AX.X, op=ALU.max), r=["scs"], w=["mxs"])
                    sks = sm[0:16, SM_SSINK + 2 * l:SM_SSINK + 2 * l + 2].unsqueeze(1).broadcast_to([16, 4, 2])
                    mxs3 = mxs[:].rearrange("p (s g) -> p s g", g=2)
                    mxs23 = mxs2[:].rearrange("p (s g) -> p s g", g=2)
                    ess3 = ess[:].rearrange("p (s g) -> p s g", g=2)
                    P.op("dve", lambda e, sks=sks, mxs3=mxs3, mxs23=mxs23: e.tensor_tensor(out=mxs23, in0=mxs3, in1=sks, op=ALU.max),
                         r=["mxs", "sm"], w=["mxs2"])
                    P.op("dve", lambda e: e.tensor_tensor(out=scs[:], in0=scs[:], in1=mxs2[:].unsqueeze(2).broadcast_to([16, 8, 132]),
                                                          op=ALU.subtract), r=["scs", "mxs2"], w=["scs"])
                    P.op("act", lambda e: e.activation(out=scs[:], in_=scs[:], func=AF.Exp), r=["scs"], w=["scs"])
                    P.op("dve", lambda e: e.tensor_reduce(out=rss[:], in_=scs[:], axis=AX.X, op=ALU.add), r=["scs"], w=["rss"])
                    P.op("dve", lambda e, sks=sks, mxs23=mxs23, ess3=ess3: e.tensor_tensor(out=ess3, in0=sks, in1=mxs23, op=ALU.subtract),
                         r=["mxs2", "sm"], w=["ess"])
                    P.op("act", lambda e: e.activation(out=ess[:], in_=ess[:], func=AF.Exp), r=["ess"], w=["ess"])
                    P.op("dve", lambda e: e.tensor_tensor(out=rss[:], in0=rss[:], in1=ess[:], op=ALU.add), r=["rss", "ess"], w=["rss"])
                    P.op("dve", lambda e: e.reciprocal(out=rss[:], in_=rss[:]), r=["rss"], w=["rss"])
                    P.op("dve", lambda e: e.tensor_tensor(out=pc[:], in0=scs[:, :, 0:128],
                                                          in1=rss[:].unsqueeze(2).broadcast_to([16, 8, 128]), op=ALU.mult),
                         r=["scs", "rss"], w=["pc"])
                    for sl in range(4):
                        s_ = w4 * 4 + sl
                        P.op("dve", lambda e, sl=sl, s_=s_: e.tensor_tensor(
                            out=pz[:, s_, :, :].rearrange("p g (t s) -> p g t s", t=4)[:, :, :, s_],
                            in0=scs[:, sl * 2:sl * 2 + 2, 128:132],
                            in1=rss[:, sl * 2:sl * 2 + 2].unsqueeze(2).broadcast_to([16, 2, 4]), op=ALU.mult),
                            r=["scs", "rss"], w=["pz"])
                    for sl in range(4):
                        s_ = w4 * 4 + sl
                        for g in range(2):
                            i = sl * 2 + g
                            P.op("pe", lambda e, i=i: e.transpose(out=PTC[:, i, :], in_=pc[:, i, :], identity=identb[0:16, 0:16]),
                                 r=["pc", "identb"], w=["ps3"])
                            P.op("pe", lambda e, i=i, s_=s_, g=g: e.transpose(out=PTZ[:, i, :], in_=pz[:, s_, g, :], identity=identb[0:16, 0:16]),
                                 r=["pz", "identb"], w=["ps3"])
                    P.op("act", lambda e: e.copy(out=ptc[:], in_=PTC), r=["ps3"], w=["ptc"])
                    P.op("act", lambda e: e.copy(out=ptz[:], in_=PTZ), r=["ps3"], w=["ptz"])
                    for sl in range(4):
                        s_ = w4 * 4 + sl
                        for g in range(2):
                            i = sl * 2 + g
                            P.op("pe", lambda e, i=i, s_=s_, g=g: e.matmul(
                                OS[:, i, :], lhsT=ptc[:, i, :], rhs=cvb[:, s_, g * 64:(g + 1) * 64], start=True, stop=False),
                                r=["ptc", "cvb"], w=["ps4"])
                            P.op("pe", lambda e, i=i, s_=s_, g=g: e.matmul(
                                OS[:, i, :], lhsT=ptz[:, i, :], rhs=vtm[0:64, 16, g * 64:(g + 1) * 64], start=False, stop=True),
                                r=["ptz", "vtm"], w=["ps4"])
                    P.op("dve", lambda e, w4=w4: e.tensor_copy(
                        out=osb[:, w4 * 4:w4 * 4 + 4, :, :], in_=OS.rearrange("p (s g) d -> p s g d", g=2)), r=["ps4"], w=["osb"])
                for hl in range(4):
                    for t_ in range(4):
                        P.op("sp", lambda e, hl=hl, t_=t_: e.dma_start(
                            out=otm[t_ * 16:(t_ + 1) * 16, :].rearrange("s (g h d) -> s g h d", g=2, h=4)[:, :, hl, :],
                            in_=osb[hl * 4 + t_:hl * 4 + t_ + 1, :, :, :]), r=["osb"], w=["otm"], dma=True)
                for c in range(4):
                    P.op("pe", lambda e, c=c: e.transpose(out=ps[3][:, c * 64:(c + 1) * 64], in_=otm[:, c * 128:(c + 1) * 128],
                                                          identity=identf[0:64, 0:64]), r=["otm", "identf"], w=["ps3"])
                P.op("dve", lambda e: e.tensor_copy(out=so[:, :, 0:64], in_=ps[3][:, 0:256].rearrange("p (c t) -> p c t", c=4)),
                     r=["ps3"], w=["so"])
                epilogue(NPR, 64)
                P.flush("swa_s")
                p3.__exit__(None, None, None)

        def E(eng, meth, r, w, **kw):
            P.op(eng, lambda e, kw=kw, meth=meth: getattr(e, meth)(**kw), r=r, w=w)

        def DMA(eng, r, w, **kw):
            P.op(eng, lambda e, kw=kw: e.dma_start(**kw), r=r, w=w, dma=True)

        def bc(ap, axis, shape):
            return ap.unsqueeze(axis).broadcast_to(list(shape))

        def gdn(l):
            QKV_OFF = XN_BYTES
            qkv = AR.view(QKV_OFF, [128, 12, T], BF16)
            ZS_OFF = ARENA - 4 * T * 2
            zs = AR.view(ZS_OFF, [128, 4, T], BF16)
            GP = ZS_OFF - 4608
            names = ["btm", "gtm", "gctm", "gltm", "egc", "kdc", "bk"]
            G = {nm: AR.view(GP + 528 * i, [64, 33, 4], F32) for i, nm in enumerate(names)}
            btm, gtm, gctm, gltm, egc, kdc, bk = [G[nm] for nm in names]
            cdec = AR.view(GP + 3696, [128, 32, 4], F32)
            cdecs = AR.view(GP + 4208, [128, 16, 4], F32)
            TMP0 = QKV_OFF + 12 * T * 2
            cwc = SM_CONVW + 48 * l
            id64 = identf[0:64, 0:64]

            AR.top = TMP0
            with contextlib.ExitStack() as ph:
                def al(shape, dt=F32):
                    return ph.enter_context(AR.alloc(list(shape), dt))
                wba = al([128, 8, 8], BF16)
                xa = al([64, 33, 4]); xb = al([64, 33, 4]); nA = al([64, 4]); rhs_s = al([64, 16, 4])
                DMA("pool", [], ["wba"], out=wba[:], in_=w_in[l, :, OFF_B:OFF_B + 8].rearrange("(c p) f -> p c f", p=128))
                BA = ps[5][0:64, 0:264].rearrange("p (n f) -> p n f", f=8)
                for n in range(33):
                    for c in range(8):
                        E("pe", "matmul", ["wba", "xn4" if n == 32 else f"xn{n // 8}"], ["ps5"], out=BA[:, n, :],
                          lhsT=xn[:, c, n * 64:(n + 1) * 64], rhs=wba[:, c, :], start=(c == 0), stop=(c == 7))
                E("act", "activation", ["ps5"], ["btm"], out=btm[:], in_=BA[:, :, 0:4], func=AF.Sigmoid)
                E("dve", "tensor_tensor", ["ps5", "sm"], ["xa"], out=xa[:], in0=BA[:, :, 4:8],
                  in1=bc(sm[0:64, SM_DTB + 4 * l:SM_DTB + 4 * l + 4], 1, [64, 33, 4]), op=ALU.add)
                E("act", "activation", ["xa"], ["xb"], out=xb[:], in_=xa[:], func=AF.Abs)
                E("act", "activation", ["xb"], ["xb"], out=xb[:], in_=xb[:], func=AF.Exp, scale=-1.0)
                E("act", "activation", ["xb", "oneb"], ["xb"], out=xb[:], in_=xb[:], func=AF.Ln, bias=oneb[0:64, :])
                E("dve", "tensor_scalar", ["xa"], ["xa"], out=xa[:], in0=xa[:], scalar1=0.0, scalar2=None, op0=ALU.max)
                E("dve", "tensor_tensor", ["xa", "xb"], ["xa"], out=xa[:], in0=xa[:], in1=xb[:], op=ALU.add)
                E("act", "activation", ["sm"], ["nA"], out=nA[:], in_=sm[0:64, SM_ALOG + 4 * l:SM_ALOG + 4 * l + 4], func=AF.Exp)
                E("dve", "tensor_scalar", ["nA"], ["nA"], out=nA[:], in0=nA[:], scalar1=-1.0, scalar2=None, op0=ALU.mult)
                E("dve", "tensor_tensor", ["xa", "nA"], ["gtm"], out=gtm[:], in0=xa[:], in1=bc(nA[:], 1, [64, 33, 4]), op=ALU.mult)
                gflat = gtm[:, 0:32, :].rearrange("p n h -> p (n h)")
                E("pe", "matmul", ["gtm", "cst"], ["ps4"], out=ps[4][0:64, 0:128], lhsT=cst[0:64, C_TRI:C_TRI + 64], rhs=gflat, start=True, stop=True)
                E("pe", "matmul", ["gtm", "cst"], ["ps4"], out=ps[4][0:64, 128:132], lhsT=cst[0:64, C_TRIS:C_TRIS + 64], rhs=gtm[:, 32, :], start=True, stop=True)
                E("pe", "matmul", ["gtm", "onesf"], ["ps4"], out=ps[4][0:64, 256:384], lhsT=ones_f[0:64, 0:64], rhs=gflat, start=True, stop=True)
                E("pe", "matmul", ["gtm", "cst"], ["ps4"], out=ps[4][0:64, 384:388], lhsT=cst[0:64, C_SAMES:C_SAMES + 64], rhs=gtm[:, 32, :], start=True, stop=True)
                E("dve", "tensor_copy", ["ps4"], ["gctm"], out=gctm[:].rearrange("p n h -> p (n h)"), in_=ps[4][0:64, 0:132])
                E("dve", "tensor_copy", ["ps4"], ["gltm"], out=gltm[:].rearrange("p n h -> p (n h)"), in_=ps[4][0:64, 256:388])
                E("act", "activation", ["gctm"], ["egc"], out=egc[:], in_=gctm[:], func=AF.Exp)
                E("dve", "tensor_tensor", ["gltm", "gctm"], ["kdc"], out=kdc[:], in0=gltm[:], in1=gctm[:], op=ALU.subtract)
                E("act", "activation", ["kdc"], ["kdc"], out=kdc[:], in_=kdc[:], func=AF.Exp)
                E("dve", "tensor_tensor", ["btm", "egc"], ["bk"], out=bk[:], in0=btm[:], in1=egc[:], op=ALU.mult)
                E("pe", "matmul", ["gltm", "e0sel"], ["ps4"], out=ps[4][:, 0:128], lhsT=e0sel[:], rhs=gltm[:, 0:32, :].rearrange("p n h -> p (n h)"),
                  start=True, stop=True)
                E("dve", "tensor_tensor", ["gltm", "cst"], ["rhs_s"], out=rhs_s[:], in0=bc(cst[0:64, C_OH48:C_OH48 + 16], 2, [64, 16, 4]),
                  in1=bc(gltm[:, 32, :], 1, [64, 16, 4]), op=ALU.mult)
                E("pe", "matmul", ["rhs_s", "onesf"], ["ps4"], out=ps[4][:, 128:192], lhsT=ones_f[0:64, :], rhs=rhs_s[:].rearrange("p s h -> p (s h)"),
                  start=True, stop=True)
                E("act", "activation", ["ps4"], ["cdec"], out=cdec[:].rearrange("p n h -> p (n h)"), in_=ps[4][:, 0:128], func=AF.Exp)
                E("act", "activation", ["ps4"], ["cdecs"], out=cdecs[:].rearrange("p s h -> p (s h)"), in_=ps[4][:, 128:192], func=AF.Exp)

                P.flush("gdn_gates")
            AR.top = TMP0
            with contextlib.ExitStack() as ph:
                def al(shape, dt=F32):
                    return ph.enter_context(AR.alloc(list(shape), dt))
                wblk = al([128, 8, 256], BF16)
                hpre = al([128, T]); accb = [al([128, 1088]) for _ in range(2)]
                cs = al([128, 16, 3]); cin = al([48, 128])
                fulls = al([128, 16, 7]); accs = al([128, 16, 4])
                sqbs = [al([128, 1024], BF16), al([128, 1088], BF16)]
                htm = al([96, 256])
                assert AR.top <= GP, AR.top
                acnt = [0]
                kcnt = [0]
                for wb in range(8):
                    col0 = wb * 256
                    DMA("pool", [], ["wblk"], out=wblk[:], in_=w_in[l, :, col0:col0 + 256].rearrange("(c p) f -> p c f", p=128))
                    if wb < 6:
                        for c in range(8):
                            E("pe", "matmul", ["wblk", "xn3", "xn4"], ["ps7"], out=ps[7][0:96, 0:256], lhsT=xn[:, c, NPR - 32:T], rhs=wblk[:, c, :],
                              start=(c == 0), stop=(c == 7))
                        E("act", "copy", ["ps7"], ["htm"], out=htm[:], in_=ps[7][0:96, 0:256])
                        DMA("sp", ["htm"], [], out=ocp[l, :, col0:col0 + 256], in_=htm[29:32, :])
                        for i3 in range(3):
                            DMA("sp", ["htm"], [], out=ocs[l, :, i3, col0:col0 + 256], in_=htm[32 + (i3 + 1) * 16:32 + (i3 + 2) * 16, :])
                    for jj in range(2):
                        j = wb * 2 + jj
                        for bi, (t0, tn) in enumerate(TBS):
                            k_ = (0, 1, 7)[kcnt[0] % 3]
                            kcnt[0] += 1
                            for c in range(8):
                                E("pe", "matmul", ["wblk", f"xn{bi}"], [f"ps{k_}"], out=ps[k_][:, 0:tn], lhsT=wblk[:, c, jj * 128:(jj + 1) * 128],
                                  rhs=xn[:, c, t0:t0 + tn], start=(c == 0), stop=(c == 7))
                            if j >= 12:
                                E("act", "activation", [f"ps{k_}"], ["zs"], out=zs[:, j - 12, t0:t0 + tn], in_=ps[k_][:, 0:tn], func=AF.Silu)
                            else:
                                E("act", "copy", [f"ps{k_}"], ["hpre"], out=hpre[:, t0:t0 + tn], in_=ps[k_][:, 0:tn])
                        if j >= 12:
                            continue
                        w_ = [sm[:, cwc + j * 4 + i:cwc + j * 4 + i + 1] for i in range(4)]
                        DMA("sp", [], ["cin"], out=cin[:], in_=cstate_d[l, :, j * 128:(j + 1) * 128])
                        E("pe", "transpose", ["cin", "identf"], ["ps7"], out=ps[7][:, 256:304], in_=cin[:], identity=identf[0:48, 0:48])
                        E("dve", "tensor_copy", ["ps7"], ["cs"], out=cs[:].rearrange("p s i -> p (s i)"), in_=ps[7][:, 256:304])
                        geo = []
                        for hf in range(2):
                            a0 = hf * 1024
                            ln = 1024
                            acc = accb[hf]
                            atg = f"acc{hf}"
                            E("dve", "tensor_scalar", ["hpre", "sm"], [atg], out=acc[:, 0:ln], in0=hpre[:, a0:a0 + ln], scalar1=w_[3], scalar2=None, op0=ALU.mult)
                            for sh in (1, 2, 3):
                                lo = sh if hf == 0 else 0
                                E("dve", "scalar_tensor_tensor", ["hpre", "sm", atg], [atg], out=acc[:, lo:ln], in0=hpre[:, a0 + lo - sh:a0 + ln - sh],
                                  scalar=w_[3 - sh], in1=acc[:, lo:ln], op0=ALU.mult, op1=ALU.add)
                            tot = ln
                            blocks = TBS[0:2] if hf == 0 else TBS[2:5]
                            if hf == 1:
                                E("dve", "tensor_copy", ["cs"], ["fulls"], out=fulls[:, :, 0:3], in_=cs[:])
                                E("dve", "tensor_copy", ["hpre"], ["fulls"], out=fulls[:, :, 3:7], in_=hpre[:, NPR:T].rearrange("p (t s) -> p s t", t=4))
                                E("dve", "tensor_scalar", ["fulls", "sm"], ["accs"], out=accs[:], in0=fulls[:, :, 3:7], scalar1=w_[3], scalar2=None, op0=ALU.mult)
                                for i in (0, 1):
                                    E("dve", "scalar_tensor_tensor", ["fulls", "sm", "accs"], ["accs"], out=accs[:], in0=fulls[:, :, i:i + 4], scalar=w_[i],
                                      in1=accs[:], op0=ALU.mult, op1=ALU.add)
                                E("dve", "scalar_tensor_tensor", ["fulls", "sm", "accs"], [atg], out=acc[:, 1024:1088].rearrange("p (t s) -> p s t", t=4),
                                  in0=fulls[:, :, 2:6], scalar=w_[2], in1=accs[:], op0=ALU.mult, op1=ALU.add)
                                tot = 1088
                            b0 = 2 if hf == 0 else 4
                            geo.append((hf, a0, tot, blocks, acc, atg, sqbs[hf], f"sqb{hf}", b0))
                        if j >= 8:
                            for (hf, a0, tot, blocks, acc, atg, sqb, stg, b0) in geo:
                                E("act", "activation", [atg], [f"qkv{j}"], out=qkv[:, j, a0:a0 + tot], in_=acc[:, 0:tot], func=AF.Silu)
                            continue
                        for (hf, a0, tot, blocks, acc, atg, sqb, stg, b0) in geo:
                            E("act", "activation", [atg], [atg], out=acc[:, 0:tot], in_=acc[:, 0:tot], func=AF.Silu)
                        for (hf, a0, tot, blocks, acc, atg, sqb, stg, b0) in geo:
                            E("act", "activation", [atg], [stg], out=sqb[:, 0:tot], in_=acc[:, 0:tot], func=AF.Square)
                        SSs = []
                        for (hf, a0, tot, blocks, acc, atg, sqb, stg, b0) in geo:
                            SS = pall[:, b0 * 512:b0 * 512 + tot]
                            sstags = [f"ps{b0 + q}" for q in range((tot + 511) // 512)]
                            SSs.append((SS, sstags))
                            for (t0, tn) in blocks:
                                o_ = t0 - a0
                                E("pe", "matmul", [stg, "ones"], [f"ps{b0 + o_ // 512}"], out=SS[:, o_:o_ + tn], lhsT=ones_bf[:], rhs=sqb[:, o_:o_ + tn], start=True, stop=True)
                        for (SS, sstags) in SSs:
                            E("act", "activation", sstags + ["epsb"], sstags, out=SS, in_=SS, func=AF.Ln, bias=epsb[:])
                        for (SS, sstags) in SSs:
                            E("act", "activation", sstags, sstags, out=SS, in_=SS, func=AF.Exp, scale=-0.5)
                        for (hf, a0, tot, blocks, acc, atg, sqb, stg, b0), (SS, sstags) in zip(geo, SSs):
                            E("dve", "scalar_tensor_tensor", [atg] + sstags, [f"qkv{j}"], out=qkv[:, j, a0:a0 + tot], in0=acc[:, 0:tot],
                              scalar=(128 ** -0.5 if j < 4 else 1.0), in1=SS, op0=ALU.mult, op1=ALU.mult)
                P.flush("gdn_p1")
            if dbg == "gdn_p1":
                return

            M = AR.view(0, [128, 4096], F32)
            M3 = M.rearrange("p (r c) -> p r c", c=64)
            tmp = AR.view(16384, [128, 1024], F32)
            AR.top = 20480
            with contextlib.ExitStack() as ph:
                def al(shape, dt=F32):
                    return ph.enter_context(AR.alloc(list(shape), dt))
                dg = al([64, 8, 64]); t1 = al([64, 8, 64]); A_sb = [al([64, 8, 64]) for _ in range(2)]
                assert AR.top <= XN_BYTES
                ai = 0
                for h in range(4):
                    for gq in range(5):
                        n0, nn = (gq * 8, 8) if gq < 4 else (32, 1)
                        c0 = n0 * 64
                        asb = A_sb[ai % 2]; atag = f"A_sb{ai % 2}"
                        gcb, gct = ps[ai % 2], f"ps{ai % 2}"
                        kkb, kkt = ps[2 + ai % 2], f"ps{2 + ai % 2}"
                        ai += 1
                        mp1 = cst[0:64, C_MP1:C_MP1 + 64] if gq < 4 else cst[0:64, C_MP1S:C_MP1S + 64]
                        E("dve", "tensor_tensor", ["identf", "gctm"], ["dg"], out=dg[:, 0:nn, :], in0=bc(id64, 1, [64, nn, 64]),
                          in1=bc(gctm[:, n0:n0 + nn, h], 2, [64, nn, 64]), op=ALU.mult)
                        E("pe", "matmul", ["dg", "onesf"], [gct], out=gcb[0:64, 0:nn * 64], lhsT=ones_f[0:64, 0:64],
                          rhs=dg[:, 0:nn, :].rearrange("p n j -> p (n j)"), start=True, stop=True)
                        for q in range(nn):
                            cc = c0 + q * 64
                            E("pe", "matmul", ["qkv"], [kkt], out=kkb[0:64, q * 64:(q + 1) * 64], lhsT=qkv[:, 4 + h, cc:cc + 64],
                              rhs=qkv[:, 4 + h, cc:cc + 64], start=True, stop=True)
                        g3 = gcb[0:64, 0:nn * 64].rearrange("p (n j) -> p n j", j=64)
                        k3 = kkb[0:64, 0:nn * 64].rearrange("p (n j) -> p n j", j=64)
                        E("dve", "tensor_tensor", [gct, "cst"], ["t1"], out=t1[:, 0:nn, :], in0=g3, in1=bc(mp1, 1, [64, nn, 64]), op=ALU.add)
                        E("dve", "tensor_tensor", ["t1", "gctm"], ["t1"], out=t1[:, 0:nn, :], in0=t1[:, 0:nn, :],
                          in1=bc(gctm[:, n0:n0 + nn, h], 2, [64, nn, 64]), op=ALU.subtract)
                        E("act", "activation", ["t1"], ["t1"], out=t1[:, 0:nn, :], in_=t1[:, 0:nn, :], func=AF.Exp, scale=-1.0)
                        E("dve", "tensor_tensor", [kkt, "t1"], [atag], out=asb[:, 0:nn, :], in0=k3, in1=t1[:, 0:nn, :], op=ALU.mult)
                        E("dve", "tensor_tensor", [atag, "btm"], [atag], out=asb[:, 0:nn, :], in0=asb[:, 0:nn, :],
                          in1=bc(btm[:, n0:n0 + nn, h], 2, [64, nn, 64]), op=ALU.mult)
                        p0 = h * 32 + n0 if gq < 4 else 128 + h
                        DMA("sp", [atag], ["Ad"], out=Ad[p0:p0 + nn, :].rearrange("n (i j) -> i n j", i=64), in_=asb[:, 0:nn, :])
                P.flush("gdn_A")
            DMA("sp", ["Ad"], ["M0"], out=M[:, :], in_=Ad[0:128, :])
            E("dve", "memset", [], ["M0"], ap=M[:, ::65], constant=1.0)
            for j in range(63):
                nr, ni = j + 1, 63 - j
                tv = tmp[:, 0:nr * ni].rearrange("p (r i) -> p r i", i=ni)
                E("dve", "tensor_tensor", ["M0"], ["tmp0"], out=tv, in0=bc(M3[:, 0:nr, j], 2, [128, nr, ni]), in1=bc(M3[:, j + 1:64, j], 1, [128, nr, ni]),
                  op=ALU.mult)
                E("dve", "tensor_tensor", ["M0", "tmp0"], ["M0"], out=M3[:, 0:nr, j + 1:64], in0=M3[:, 0:nr, j + 1:64], in1=tv, op=ALU.subtract)
            DMA("sp", ["M0"], ["Ud"], out=Ud[0:128, :], in_=M[:, :])
            DMA("sp", ["Ad", "M0"], ["M0"], out=M[0:4, :], in_=Ad[128:132, :])
            E("dve", "memset", [], ["M0"], ap=M[0:4, ::65], constant=1.0)
            M5 = M[0:4, :].rearrange("p (r c) -> p r c", c=64)
            for j in range(3):
                for r_ in range(j + 1):
                    pass
            Mf = M[0:4, :]
            def dsl(rt, ct):
                o0 = (rt * 16) * 64 + ct * 16
                return Mf[:, o0:o0 + 15 * 65 + 1:65]
            tmp4 = tmp[0:4, 0:16]
            for j in range(3):
                for i in range(j + 1, 4):
                    for r_ in range(j + 1):
                        E("dve", "tensor_tensor", ["M0"], ["tmp0"], out=tmp4, in0=dsl(r_, j), in1=dsl(i, j), op=ALU.mult)
                        E("dve", "tensor_tensor", ["M0", "tmp0"], ["M0"], out=dsl(r_, i), in0=dsl(r_, i), in1=tmp4, op=ALU.subtract)
            DMA("sp", ["M0"], ["Ud"], out=Ud[128:132, :], in_=M[0:4, :])
            P.flush("gdn_solve")

            AR.top = TMP0
            with contextlib.ExitStack() as ph:
                def al(shape, dt=F32):
                    return ph.enter_context(AR.alloc(list(shape), dt))
                Uc = [al([64, 4, 64], BF16) for _ in range(3)]
                wog = al([128, 4, D], BF16)
                wv1 = al([64, 4, 128])
                wv = [wv1, wv1]
                u_sb = al([64, 4, 128], BF16)
                Sf = al([128, 4, 128]); Sb = al([128, 4, 128], BF16)
                obufs = [al([128, 4, 256]) for _ in range(2)]; go = al([128, 4, 256], BF16)
                sqh = al([128, 256], BF16); rsh = al([128, 256])
                assert AR.top <= GP, AR.top
                top2 = AR.top
                AR.top = 0
                kbg = [al([64, 4, 128], BF16) for _ in range(2)]
                kdec = [al([64, 4, 128], BF16) for _ in range(2)]
                vb = [al([64, 4, 128], BF16) for _ in range(2)]
                qkT = [al([64, 4, 64], BF16) for _ in range(2)]
                t2 = al([64, 4, 64]); dg2 = al([64, 4, 64]); egb = al([128, 4, 64])
                kcT = [al([128, 4, 64], BF16) for _ in range(2)]
                qdT = [al([128, 4, 64], BF16) for _ in range(2)]
                Sfs = al([128, 16, 128]); Sbs = al([128, 16, 128], BF16)
                kdm = al([64, 16, 128], BF16); kcm = al([128, 16, 64], BF16); qdm = al([128, 16, 64], BF16)
                assert AR.top <= XN_BYTES, AR.top

                UdP = Ud[0:128, :].rearrange("(h n) (r i) -> n r h i", h=4, i=64)
                UdS = Ud[128:132, :].rearrange("h (r i) -> r h i", i=64)
                DMA("pool", [], ["wog"], out=wog[:], in_=w_out[l, 0:512, :].rearrange("(c p) f -> p c f", p=128))
                E("dve", "memset", [], ["Sf"], ap=Sf[:], constant=0.0)
                E("dve", "memset", [], ["Sb"], ap=Sb[:], constant=0.0)

                TP = ps[0].bitcast(BF16)[0:64, :].rearrange("p (i d) -> p i d", i=8)
                QK = ps[1][0:64, 0:256].rearrange("p (h i) -> p h i", h=4)
                GC2 = ps[1][0:64, 256:512].rearrange("p (h i) -> p h i", h=4)
                WV = ps[2][0:64, :].rearrange("p (h d) -> p h d", h=4)
                KC = ps[3][:, 0:256].rearrange("p (h i) -> p h i", h=4)
                EGB = ps[3][:, 256:512].rearrange("p (h i) -> p h i", h=4)
                UP = ps[4][0:64, :].rearrange("p (h d) -> p h d", h=4)
                OP = ps[5][:, 0:256].rearrange("p (h i) -> p h i", h=4)
                SP = ps[6].rearrange("p (h d) -> p h d", h=4)
                gnw = sm[:, SM_GNW + l:SM_GNW + l + 1]

                def prep1(n, rb):
                    c0 = n * 64
                    samp = n == 32
                    ub = n % 3
                    Ub = Uc[ub]
                    DMA("pool", ["Ud"], [f"Uc{ub}"], out=Ub[:], in_=(UdS if samp else UdP[n]))
                    E("dve", "tensor_tensor", [f"Uc{ub}", "triu"], [f"Uc{ub}"], out=Ub[:], in0=Ub[:], in1=bc(triu_bf[:], 1, [64, 4, 64]), op=ALU.mult)
                    E("dve", "tensor_tensor", ["identf", "gctm"], ["dg2"], out=dg2[:], in0=bc(id64, 1, [64, 4, 64]), in1=bc(gctm[:, n, :], 2, [64, 4, 64]), op=ALU.mult)
                    for h in range(4):
                        E("pe", "transpose", ["qkv", "identb"], ["ps0"], out=TP[:, h, :], in_=qkv[:, 4 + h, c0:c0 + 64], identity=identb[:])
                        E("pe", "transpose", ["qkv", "identb"], ["ps0"], out=TP[:, 4 + h, :], in_=qkv[:, 8 + h, c0:c0 + 64], identity=identb[:])
                    E("dve", "tensor_tensor", ["ps0", "bk"], [f"kbg{rb}"], out=kbg[rb][:], in0=TP[:, 0:4, :], in1=bc(bk[:, n, :], 2, [64, 4, 128]), op=ALU.mult)
                    E("dve", "tensor_tensor", ["ps0", "btm"], [f"vb{rb}"], out=vb[rb][:], in0=TP[:, 4:8, :], in1=bc(btm[:, n, :], 2, [64, 4, 128]), op=ALU.mult)
                    E("dve", "tensor_tensor", ["ps0", "kdc"], [f"kdec{rb}"], out=kdec[rb][:], in0=TP[:, 0:4, :], in1=bc(kdc[:, n, :], 2, [64, 4, 128]), op=ALU.mult)
                    for h in range(4):
                        E("pe", "matmul", ["qkv"], ["ps1"], out=QK[:, h, :], lhsT=qkv[:, 4 + h, c0:c0 + 64], rhs=qkv[:, h, c0:c0 + 64], start=True, stop=True)
                    E("pe", "matmul", ["dg2", "onesf"], ["ps1"], out=ps[1][0:64, 256:512], lhsT=ones_f[0:64, 0:64], rhs=dg2[:].rearrange("p h i -> p (h i)"),
                      start=True, stop=True)
                    E("pe", "matmul", ["dg2", "onesf"], ["ps3"], out=ps[3][:, 256:512], lhsT=ones_f[0:64, :], rhs=dg2[:].rearrange("p h i -> p (h i)"),
                      start=True, stop=True)

                def prep2(n, rb):
                    c0 = n * 64
                    samp = n == 32
                    ub = n % 3
                    Ub = Uc[ub]
                    mp2 = cst[0:64, C_MP2S:C_MP2S + 64] if samp else cst[0:64, C_MP2:C_MP2 + 64]
                    for h in range(4):
                        E("pe", "matmul", [f"Uc{ub}", f"vb{rb}"], ["ps2"], out=WV[:, h, :], lhsT=Ub[:, h, :], rhs=vb[rb][:, h, :], start=True, stop=True)
                    for h in range(4):
                        E("pe", "matmul", [f"Uc{ub}", f"kbg{rb}"], ["ps3"], out=KC[:, h, :], lhsT=kbg[rb][:, h, :], rhs=Ub[:, h, :], start=True, stop=True)
                    E("act", "activation", ["ps3"], ["egb"], out=egb[:], in_=EGB, func=AF.Exp)
                    E("act", "copy", ["ps3"], [f"kcT{rb}"], out=kcT[rb][:], in_=KC)
                    E("act", "copy", ["ps2"], ["wv"], out=wv[rb][:], in_=WV)
                    E("dve", "tensor_tensor", ["ps1", "cst"], ["t2"], out=t2[:], in0=GC2, in1=bc(mp2, 1, [64, 4, 64]), op=ALU.subtract)
                    E("dve", "tensor_tensor", ["t2", "gctm"], ["t2"], out=t2[:], in0=t2[:], in1=bc(gctm[:, n, :], 2, [64, 4, 64]), op=ALU.subtract)
                    E("act", "activation", ["t2"], ["t2"], out=t2[:], in_=t2[:], func=AF.Exp)
                    E("dve", "tensor_tensor", ["qkv", "egb"], [f"qdT{rb}"], out=qdT[rb][:], in0=qkv[:, 0:4, c0:c0 + 64], in1=egb[:], op=ALU.mult)
                    E("dve", "tensor_tensor", ["ps1", "t2"], [f"qkT{rb}"], out=qkT[rb][:], in0=QK, in1=t2[:], op=ALU.mult)

                def epilogue_gen(c0, ncol, ob):
                    obuf = obufs[ob]
                    otag = f"obuf{ob}"
                    for h in range(4):
                        E("act", "activation", [otag], ["sqh"], out=sqh[:, 0:ncol], in_=obuf[:, h, 0:ncol], func=AF.Square)
                        yield
                        E("pe", "matmul", ["sqh", "ones"], ["ps7"], out=ps[7][:, 0:ncol], lhsT=ones_bf[:], rhs=sqh[:, 0:ncol], start=True, stop=True)
                        yield
                        E("act", "activation", ["ps7", "epsb"], ["rsh"], out=rsh[:, 0:ncol], in_=ps[7][:, 0:ncol], func=AF.Ln, scale=1.0 / 128, bias=epsb[:])
                        yield
                        E("act", "activation", ["rsh"], ["rsh"], out=rsh[:, 0:ncol], in_=rsh[:, 0:ncol], func=AF.Exp, scale=-0.5)
                        yield
                        E("dve", "scalar_tensor_tensor", [otag, "sm", "rsh"], [otag], out=obuf[:, h, 0:ncol], in0=obuf[:, h, 0:ncol], scalar=gnw,
                          in1=rsh[:, 0:ncol], op0=ALU.mult, op1=ALU.mult)
                        yield
                        E("dve", "tensor_tensor", [otag, "zs"], ["go"], out=go[:, h, 0:ncol], in0=obuf[:, h, 0:ncol], in1=zs[:, h, c0:c0 + ncol], op=ALU.mult)
                        yield
                    for qr in range(4):
                        for o2 in range(2):
                            o = qr * 2 + o2
                            for h in range(4):
                                E("pe", "matmul", ["wog", "go"], ["ps7"], out=ps[7][:, o2 * 256:o2 * 256 + ncol], lhsT=wog[:, h, o * 128:(o + 1) * 128],
                                  rhs=go[:, h, 0:ncol], start=(h == 0), stop=(h == 3))
                        yield
                        E("dve", "tensor_tensor", ["ps7", xtag(c0)], [xtag(c0)], out=xT[:, qr * 2:qr * 2 + 2, c0:c0 + ncol],
                          in0=ps[7].rearrange("p (o t) -> p o t", o=2)[:, :, 0:ncol], in1=xT[:, qr * 2:qr * 2 + 2, c0:c0 + ncol], op=ALU.add)
                        yield

                pend = [None]

                def step(k):
                    for _ in range(k):
                        if pend[0] is None:
                            return
                        try:
                            next(pend[0])
                        except StopIteration:
                            pend[0] = None

                def epilogue(c0, ncol, ob):
                    step(10 ** 6)
                    pend[0] = epilogue_gen(c0, ncol, ob)
                    step(10 ** 6)

                prep1(0, 0)
                prep2(0, 0)
                for n in range(32):
                    rb = n % 2
                    prep1(n + 1, (n + 1) % 2)
                    step(3)
                    for h in range(4):
                        E("pe", "matmul", [f"kcT{rb}", "Sb"], ["ps4"], out=UP[:, h, :], lhsT=kcT[rb][:, h, :], rhs=Sb[:, h, :], start=True, stop=True)
                    E("dve", "tensor_tensor", ["wv", "ps4"], ["u_sb"], out=u_sb[:], in0=wv[rb][:], in1=UP, op=ALU.subtract)
                    prep2(n + 1, (n + 1) % 2)
                    step(3)
                    for h in range(4):
                        E("pe", "matmul", [f"kdec{rb}", "u_sb"], ["ps6"], out=SP[:, h, :], lhsT=kdec[rb][:, h, :], rhs=u_sb[:, h, :], start=True, stop=True)
                    for h in range(4):
                        E("pe", "matmul", ["Sb", f"qdT{rb}"], ["ps5"], out=OP[:, h, :], lhsT=Sb[:, h, :], rhs=qdT[rb][:, h, :], start=True, stop=False)
                        E("pe", "matmul", ["u_sb", f"qkT{rb}"], ["ps5"], out=OP[:, h, :], lhsT=u_sb[:, h, :], rhs=qkT[rb][:, h, :], start=False, stop=True)
                    for h in range(4):
                        E("dve", "scalar_tensor_tensor", ["Sf", "cdec", "ps6"], ["Sf"], out=Sf[:, h, :], in0=Sf[:, h, :], scalar=cdec[:, n, h:h + 1],
                          in1=SP[:, h, :], op0=ALU.mult, op1=ALU.add)
                    ob = (n // 4) % 2
                    E("act", "copy", ["ps5"], [f"obuf{ob}"], out=obufs[ob][:, :, (n % 4) * 64:(n % 4) * 64 + 64], in_=OP)
                    E("act", "copy", ["Sf"], ["Sb"], out=Sb[:], in_=Sf[:])
                    step(3)
                    if n % 4 == 3:
                        step(10 ** 6)
                        pend[0] = epilogue_gen((n - 3) * 64, 256, ob)
                step(10 ** 6)
                DMA("sp", ["Sf"], [], out=ogp[l].rearrange("h k v -> k h v"), in_=Sf[:])

                for h in range(4):
                    DMA("sp", [], ["Sfs"], out=Sfs[:], in_=sgd[l, :, h].rearrange("s k v -> k s v"))
                    DMA("pool", [], ["Sbs"], out=Sbs[:], in_=sgd[l, :, h].rearrange("s k v -> k s v"))
                    E("dve", "tensor_tensor", ["kcT0", "maskS"], ["kcm"], out=kcm[:], in0=bc(kcT[0][:, h, :], 1, [128, 16, 64]), in1=maskS[:], op=ALU.mult)
                    E("dve", "tensor_tensor", ["qdT0", "maskS"], ["qdm"], out=qdm[:], in0=bc(qdT[0][:, h, :], 1, [128, 16, 64]), in1=maskS[:], op=ALU.mult)
                    E("dve", "tensor_tensor", ["kdec0", "cst"], ["kdm"], out=kdm[:], in0=bc(kdec[0][:, h, :], 1, [64, 16, 128]),
                      in1=bc(cst[0:64, C_OHS:C_OHS + 16], 2, [64, 16, 128]), op=ALU.mult)
                    for s_ in range(16):
                        E("pe", "matmul", ["kcm", "Sbs"], ["ps4"], out=UP[:, h, :], lhsT=kcm[:, s_, :], rhs=Sbs[:, s_, :], start=(s_ == 0), stop=(s_ == 15))
                    E("dve", "tensor_tensor", ["wv", "ps4"], ["u_sb"], out=u_sb[:, h, :], in0=wv[0][:, h, :], in1=UP[:, h, :], op=ALU.subtract)
                    for s_ in range(16):
                        E("pe", "matmul", ["qdm", "Sbs"], ["ps5"], out=OP[:, h, :], lhsT=Sbs[:, s_, :], rhs=qdm[:, s_, :], start=(s_ == 0), stop=False)
                    E("pe", "matmul", ["u_sb", "qkT0"], ["ps5"], out=OP[:, h, :], lhsT=u_sb[:, h, :], rhs=qkT[0][:, h, :], start=False, stop=True)
                    E("act", "copy", ["ps5"], ["obuf0"], out=obufs[0][:, h, 0:64], in_=OP[:, h, :])
                    for s4 in range(4):
                        for sl in range(4):
                            s_ = s4 * 4 + sl
                            E("pe", "matmul", ["kdm", "u_sb"], ["ps6"], out=SP[:, sl, :], lhsT=kdm[:, s_, :], rhs=u_sb[:, h, :], start=True, stop=True)
                        for sl in range(4):
                            s_ = s4 * 4 + sl
                            E("dve", "scalar_tensor_tensor", ["Sfs", "cdecs", "ps6"], ["Sfs"], out=Sfs[:, s_, :], in0=Sfs[:, s_, :],
                              scalar=cdecs[:, s_, h:h + 1], in1=SP[:, sl, :], op0=ALU.mult, op1=ALU.add)
                    DMA("sp", ["Sfs"], [], out=ogs[l, :, h].rearrange("s k v -> k s v"), in_=Sfs[:])
                epilogue(NPR, 64, 0)
                P.flush("gdn_scan")
            AR.top = XN_BYTES

        def final_out():
            AR.top = 0
            with contextlib.ExitStack() as ph:
                def al(shape, dt=F32):
                    return ph.enter_context(AR.alloc(list(shape), dt))
                sq = [al([128, 8, 512], BF16) for _ in range(2)]
                rstd = [al([128, 512]) for _ in range(2)]
                yfm = [al([128, 8, 512]) for _ in range(2)]
                ytm = [al([128, D]) for _ in range(2)]
                ti = 0
                for bi, (t0, tn) in enumerate(TBS):
                    b = bi % 2
                    E("act", "activation", [xtag(t0)], [f"sq{b}"], out=sq[b][:, :, 0:tn], in_=xT[:, :, t0:t0 + tn], func=AF.Square)
                    for c in range(8):
                        E("pe", "matmul", [f"sq{b}", "ones"], [f"ps{b}"], out=ps[b][:, 0:tn], lhsT=ones_bf[:], rhs=sq[b][:, c, 0:tn],
                          start=(c == 0), stop=(c == 7))
                    E("act", "activation", [f"ps{b}", "epsb"], [f"rstd{b}"], out=rstd[b][:, 0:tn], in_=ps[b][:, 0:tn], func=AF.Ln, scale=1.0 / D, bias=epsb[:])
                    E("act", "activation", [f"rstd{b}"], [f"rstd{b}"], out=rstd[b][:, 0:tn], in_=rstd[b][:, 0:tn], func=AF.Exp, scale=-0.5)
                    for c in range(8):
                        E("dve", "scalar_tensor_tensor", [xtag(t0), "sm", f"rstd{b}"], [f"yfm{b}"], out=yfm[b][:, c, 0:tn], in0=xT[:, c, t0:t0 + tn],
                          scalar=sm[:, SM_NF + c:SM_NF + c + 1], in1=rstd[b][:, 0:tn], op0=ALU.mult, op1=ALU.mult)
                    for q in range((tn + 127) // 128):
                        rows = min(128, tn - q * 128)
                        yb = ti % 2
                        ti += 1
                        for c in range(8):
                            bank = 2 + c // 4
                            E("pe", "transpose", [f"yfm{b}", "identf"], [f"ps{bank}"], out=ps[bank][0:rows, (c % 4) * 128:(c % 4 + 1) * 128],
                              in_=yfm[b][:, c, q * 128:q * 128 + rows], identity=identf[:])
                        E("dve", "tensor_copy", ["ps2", "ps3"], [f"ytm{yb}"], out=ytm[yb][0:rows, :], in_=pall[0:rows, 1024:2048])
                        if t0 < NPR:
                            r0 = t0 + q * 128
                            DMA("sp", [f"ytm{yb}"], [], out=yp[r0:r0 + 128, :], in_=ytm[yb][:, :])
                        else:
                            for t_ in range(4):
                                DMA("sp", [f"ytm{yb}"], [], out=ys[:, t_, :], in_=ytm[yb][t_ * 16:(t_ + 1) * 16, :])
                P.flush("final")

        for l in range(nlayers):
            ffn(l, w1g, w1u, w1d, SM_N1 + 8 * l)
            if dbg == "ffn1" and l == nlayers - 1:
                break
            swa(l)
            if dbg in ("swa", "swa_p") and l == nlayers - 1:
                break
            gdn(l)
            if dbg in ("mix", "gdn_p1") and l == nlayers - 1:
                break
            ffn(l, w2g, w2u, w2d, SM_N2 + 8 * l)

        if dbg:
            P.op("sp", lambda e: e.dma_start(out=dbgx, in_=xT[:]), r=[f"xT{i}" for i in range(5)], dma=True)
            P.flush("dbg")

        if not dbg:
            final_out()
    return nc


SM_N1 = 0
SM_N2 = SM_N1 + 8 * DEPTH
SM_NM = SM_N2 + 8 * DEPTH
SM_NF = SM_NM + 8 * DEPTH
SM_SINK = SM_NF + 8
SM_SWN = SM_SINK + 8 * DEPTH
SM_SSINK = SM_SWN + 4 * DEPTH
SM_CONVW = SM_SSINK + 2 * DEPTH
SM_GNW = SM_CONVW + 48 * DEPTH
SM_ALOG = SM_GNW + DEPTH
SM_DTB = SM_ALOG + 4 * DEPTH
NSMALL = SM_DTB + 4 * DEPTH
C_MASK2 = 0
C_SMASK = 256
C_MP1, C_MP2, C_MP1S, C_MP2S, C_TRI, C_TRIS, C_SAMES, C_TRIU, C_OH48, C_OHS = 392, 456, 520, 584, 648, 712, 776, 840, 904, 920
NCONST = 936
NEG = -30000.0
BIGM = 30000.0


def make_consts():
    c = np.zeros((128, NCONST), np.float32)
    i = np.arange(128)[:, None]
    j = np.arange(128)[None, :]
    c[:, C_MASK2:C_MASK2 + 128] = np.where(j > i, 0.0, NEG)
    c[:, C_MASK2 + 128:C_MASK2 + 256] = np.where(j <= i, 0.0, NEG)
    t = (np.arange(16) % 4)[:, None]
    c[0:16, C_SMASK:C_SMASK + 128] = np.where(np.arange(128)[None, :] > t, 0.0, NEG)
    c[0:16, C_SMASK + 128:C_SMASK + 132] = np.where(np.arange(4)[None, :] <= t, 0.0, NEG)
    a = np.arange(64)[:, None]
    b = np.arange(64)[None, :]
    ta, sa, tb_, sb_ = a // 16, a % 16, b // 16, b % 16
    c[0:64, C_MP1:C_MP1 + 64] = np.where(a > b, 0.0, BIGM)
    c[0:64, C_MP2:C_MP2 + 64] = np.where(b >= a, 0.0, BIGM)
    c[0:64, C_MP1S:C_MP1S + 64] = np.where((sa == sb_) & (ta > tb_), 0.0, BIGM)
    c[0:64, C_MP2S:C_MP2S + 64] = np.where((sa == sb_) & (tb_ >= ta), 0.0, BIGM)
    c[0:64, C_TRI:C_TRI + 64] = (a <= b)
    c[0:64, C_TRIS:C_TRIS + 64] = (sa == sb_) & (ta <= tb_)
    c[0:64, C_SAMES:C_SAMES + 64] = (sa == sb_)
    c[0:64, C_TRIU:C_TRIU + 64] = (b >= a)
    c[0:64, C_OH48:C_OH48 + 16] = (a == 48 + np.arange(16)[None, :])
    c[0:64, C_OHS:C_OHS + 16] = (sa == np.arange(16)[None, :])
    return c


def make_maskS():
    m = np.zeros((128, 16, 64), np.float32)
    for s_ in range(16):
        m[:, s_, s_::16] = 1.0
    return m.astype(ml_dtypes.bfloat16)


def make_smalls(inp):
    s = np.zeros((128, NSMALL), np.float32)
    for l in range(DEPTH):
        s[:, SM_N1 + 8 * l:SM_N1 + 8 * l + 8] = inp["ffn1_norm"][l].reshape(8, 128).T
        s[:, SM_N2 + 8 * l:SM_N2 + 8 * l + 8] = inp["ffn2_norm"][l].reshape(8, 128).T
        s[:, SM_NM + 8 * l:SM_NM + 8 * l + 8] = inp["mix_norm"][l].reshape(8, 128).T
    s[:, SM_NF:SM_NF + 8] = inp["final_norm"].reshape(8, 128).T
    for l in range(DEPTH):
        for g in range(2):
            for sl_ in range(4):
                s[:, SM_SINK + 8 * l + 4 * g + sl_] = inp["swa_sinks"][l][4 * g + (sl_ % 2) * 2 + sl_ // 2]
        s[:, SM_SWN + 4 * l:SM_SWN + 4 * l + 4] = inp["swa_out_norm"][l].reshape(4, 128).T
        for g in range(2):
            s[0:16, SM_SSINK + 2 * l + g] = np.repeat(inp["swa_sinks"][l][g * 4:(g + 1) * 4], 4)
        cw = np.asarray(inp["gdn_conv_w"][l])
        s[:, SM_CONVW + 48 * l:SM_CONVW + 48 * l + 48] = cw.reshape(4, 12, 128).transpose(2, 1, 0).reshape(128, 48)
        s[:, SM_GNW + l] = inp["gdn_out_norm"][l]
        s[:, SM_ALOG + 4 * l:SM_ALOG + 4 * l + 4] = inp["gdn_a_log"][l][None, :]
        s[:, SM_DTB + 4 * l:SM_DTB + 4 * l + 4] = inp["gdn_dt_bias"][l][None, :]
    return s


def make_in_maps(inp, nl=DEPTH, ncores=8):
    f = lambda a: np.ascontiguousarray(np.asarray(a[:nl], dtype=np.float32))
    g = lambda a: np.ascontiguousarray(np.asarray(a, dtype=np.float32))
    smalls = make_smalls(inp)
    shared = {
        "smalls": smalls, "identf": np.eye(128, dtype=np.float32),
        "consts": make_consts(), "identb": np.eye(128).astype(ml_dtypes.bfloat16), "maskS": make_maskS(), "maskb": (8.0 * make_consts()[:, C_MASK2:C_MASK2 + 256]).astype(ml_dtypes.bfloat16),
        "w_in": f(inp["w_in"]), "w_out": f(inp["w_out"]),
        "w1g": f(inp["ffn1_w_gate"]), "w1u": f(inp["ffn1_w_up"]), "w1d": f(inp["ffn1_w_down"]),
        "w2g": f(inp["ffn2_w_gate"]), "w2u": f(inp["ffn2_w_up"]), "w2d": f(inp["ffn2_w_down"]),
    }
    maps = []
    for c in range(ncores):
        m = dict(shared)
        m["xp"] = g(inp["x_prompt"][c])
        m["cstate"] = g(inp["state_gdn_conv"][:nl, 16 * c:16 * c + 16].reshape(nl, 48, 1536))
        m["sgd"] = g(inp["state_gdn"][:nl, 16 * c:16 * c + 16])
        m["ck"] = g(inp["cache_swa_k"][:nl, 16 * c:16 * c + 16].reshape(nl, 16, 128, 128))
        m["cv"] = g(inp["cache_swa_v"][:nl, 16 * c:16 * c + 16].reshape(nl, 16, 128, 128))
        m["xs"] = g(inp["x_sample"][16 * c:16 * c + 16])
        maps.append(m)
    return maps


def kernel(**inputs):
    nc = build_nc()
    res = run_bass_kernel_spmd(nc, make_in_maps(inputs), core_ids=list(range(8)))
    R = res.results
    cat = lambda k, ax: np.concatenate([np.asarray(r[k]) for r in R], axis=ax)
    y_prompt = np.stack([np.asarray(r["yp"]) for r in R], 0)
    y_sample = cat("ys", 0)
    conv_p = np.stack([np.asarray(r["ocp"]) for r in R], 1)
    gdn_p = np.stack([np.asarray(r["ogp"]) for r in R], 1)
    k_p = np.stack([np.asarray(r["okp"]) for r in R], 1).reshape(DEPTH, 8, 128, 2, 64)
    v_p = np.stack([np.asarray(r["ovp"]) for r in R], 1).reshape(DEPTH, 8, 128, 2, 64)
    conv_s = cat("ocs", 1)
    gdn_s = cat("ogs", 1)
    k_s = cat("oks", 1).reshape(DEPTH, 128, 128, 2, 64)
    v_s = cat("ovs", 1).reshape(DEPTH, 128, 128, 2, 64)
    outs = (y_prompt, y_sample, conv_p, gdn_p, k_p, v_p, conv_s, gdn_s, k_s, v_s)
    return tuple(np.ascontiguousarray(o, dtype=np.float32) for o in outs)
```

```python
import contextlib
import os
SKIP = set(os.environ.get('KSKIP', '').split(','))
NOSELF = set(os.environ.get('KNOSELF', '').split(','))
import numpy as np
import ml_dtypes
import concourse.bass as bass
import concourse.mybir as mybir
from concourse.bass_utils import run_bass_kernel_spmd

F32 = mybir.dt.float32
BF16 = mybir.dt.bfloat16
AF = mybir.ActivationFunctionType
ALU = mybir.AluOpType
AX = mybir.AxisListType

D = 1024
DEPTH = 4
NPR = 2048
NSM = 64
T = NPR + NSM
DFF = 2816
NF = DFF // 128
INC = 2824
OFF_Z, OFF_B, OFF_A, OFF_SQ, OFF_SK, OFF_SV = 1536, 2048, 2052, 2056, 2568, 2696
EPS = 1e-6
TBS = [(0, 512), (512, 512), (1024, 512), (1536, 512), (2048, 64)]
FGROUPS = [(0, 4), (4, 4), (8, 4), (12, 4), (16, 4), (20, 2)]
ENGS = ["pe", "act", "dve", "pool", "sp"]
NDS = 12


ARENA = 134144


class Arena:
    def __init__(self, ap):
        self.ap = ap
        self.top = 0

    def view(self, off, shape, dt):
        esz = 4 if dt == F32 else 2
        n = esz
        for d_ in shape[1:]:
            n *= d_
        assert off % 4 == 0 and off + n <= ARENA, (off, n, shape)
        v = self.ap[0:shape[0], off // 2:(off + n) // 2]
        if dt == F32:
            v = v.bitcast(F32)
        if len(shape) == 3:
            v = v.rearrange("p (a b) -> p a b", a=shape[1])
        elif len(shape) == 4:
            v = v.rearrange("p (a b c) -> p a b c", a=shape[1], b=shape[2])
        return v

    @contextlib.contextmanager
    def alloc(self, shape, dt):
        off = (self.top + 63) // 64 * 64
        v = self.view(off, shape, dt)
        esz = 4 if dt == F32 else 2
        n = esz
        for d_ in shape[1:]:
            n *= d_
        old = self.top
        self.top = off + n
        try:
            yield v
        finally:
            self.top = old


class Op:
    __slots__ = ("fn", "deps", "dma", "sig", "count", "dsem", "dval", "gsem", "gval")

    def __init__(self, fn, deps, dma):
        self.fn = fn
        self.deps = deps
        self.dma = dma
        self.sig = False
        self.count = 0
        self.dsem = None
        self.dval = 0
        self.gval = 0


class Prog:
    def __init__(self, nc, sems, dsems):
        self.nc = nc
        self.sems = sems
        self.dsems = dsems
        self.cnt = {e: 0 for e in ENGS}
        self.dcnt = {e: 0 for e in ENGS}
        self.waited = {e: {} for e in ENGS}
        self.reset()

    def reset(self):
        self.ops = {e: [] for e in ENGS}
        self.last_w = {}
        self.readers = {}

    def op(self, eng, fn, r=(), w=(), dma=False):
        idx = len(self.ops[eng])
        w = list(w) + [t for t in r if t.startswith("ps")]
        r = [t for t in r if not t.startswith("ps")]
        deps = set()
        for t in r:
            lw = self.last_w.get(t)
            if lw is not None:
                deps.add(lw)
        for t in w:
            lw = self.last_w.get(t)
            if lw is not None:
                deps.add(lw)
            for rd in self.readers.get(t, ()):
                deps.add(rd)
        if eng == "pe" or (eng in NOSELF):
            deps = {d for d in deps if d[0] != eng}
        deps.discard((eng, idx))
        self.ops[eng].append(Op(fn, deps, dma))
        for t in w:
            self.last_w[t] = (eng, idx)
            self.readers[t] = []
        for t in r:
            self.readers.setdefault(t, []).append((eng, idx))

    def flush(self, name):
        for e in ENGS:
            for op in self.ops[e]:
                for (te, ti) in op.deps:
                    t = self.ops[te][ti]
                    if not t.dma:
                        t.sig = True
        for e in ENGS:
            for op in self.ops[e]:
                if op.dma:
                    i = self.dcnt[e]
                    self.dcnt[e] += 1
                    op.dsem = self.dsems[e][i % NDS]
                    op.dval = 16 * (i // NDS + 1)
                    op.gval = 16 * (i // NDS)
                elif op.sig:
                    self.cnt[e] += 1
                    op.count = self.cnt[e]
        with self.nc.Block() as blk:
            for e, bname in (("pe", "tensor"), ("act", "scalar"), ("dve", "vector"),
                             ("pool", "gpsimd"), ("sp", "sync")):
                getattr(blk, bname)(self._body(e))
        self.reset()

    def _body(self, e):
        ops = self.ops
        allops = self.ops

        def body(eng):
            waited = self.waited[e]

            def wait(sem, val):
                key = id(sem)
                if waited.get(key, 0) < val:
                    eng.wait_ge(sem, val)
                    waited[key] = val

            last_d = {}
            for op in ops[e]:
                for (te, ti) in sorted(op.deps):
                    t = allops[te][ti]
                    if t.dma:
                        wait(t.dsem, t.dval)
                    else:
                        wait(self.sems[te], t.count)
                if op.dma and op.gval > 0:
                    wait(op.dsem, op.gval)
                inst = op.fn(eng)
                if op.dma:
                    inst.then_inc(op.dsem, 16)
                    last_d[id(op.dsem)] = (op.dsem, op.dval)
                elif op.sig:
                    inst.then_inc(self.sems[e], 1)
            for (sem, val) in last_d.values():
                wait(sem, val)

        return body


def build_nc(nlayers=DEPTH, dbg=None):
    NL = nlayers
    nc = bass.Bass("TRN2", target_bir_lowering=False)
    dram = {}

    def din(name, shape, dt=F32):
        dram[name] = nc.dram_tensor(name, list(shape), dt, kind="ExternalInput").ap()
        return dram[name]

    def dout(name, shape, dt=F32):
        dram[name] = nc.dram_tensor(name, list(shape), dt, kind="ExternalOutput").ap()
        return dram[name]

    xp = din("xp", [NPR, D])
    xs = din("xs", [16, 4, D])
    smalls = din("smalls", [128, NSMALL])
    identf_d = din("identf", [128, 128])
    w1g = din("w1g", [NL, D, DFF])
    w1u = din("w1u", [NL, D, DFF])
    w1d = din("w1d", [NL, DFF, D])
    w2g = din("w2g", [NL, D, DFF])
    w2u = din("w2u", [NL, D, DFF])
    w2d = din("w2d", [NL, DFF, D])
    w_in = din("w_in", [NL, D, INC])
    w_out = din("w_out", [NL, D, D])
    consts_d = din("consts", [128, NCONST])
    identb_d = din("identb", [128, 128], BF16)
    ck_d = din("ck", [NL, 16, 128, 128])
    cv_d = din("cv", [NL, 16, 128, 128])
    cstate_d = din("cstate", [NL, 48, 1536])
    sgd = din("sgd", [NL, 16, 4, 128, 128])
    maskS_d = din("maskS", [128, 16, 64], BF16)
    maskb_d = din("maskb", [128, 256], BF16)
    Ad = nc.dram_tensor("Ad_scr", [132, 4096], F32, kind="Internal").ap()
    Ud = nc.dram_tensor("Ud_scr", [132, 4096], F32, kind="Internal").ap()
    ocp = dout("ocp", [NL, 3, 1536])
    ocs = dout("ocs", [NL, 16, 3, 1536])
    ogp = dout("ogp", [NL, 4, 128, 128])
    ogs = dout("ogs", [NL, 16, 4, 128, 128])
    okp = dout("okp", [NL, 128, 128])
    ovp = dout("ovp", [NL, 128, 128])
    oks = dout("oks", [NL, 16, 128, 128])
    ovs = dout("ovs", [NL, 16, 128, 128])
    yp = dout("yp", [NPR, D])
    ys = dout("ys", [16, 4, D])
    if dbg:
        dbgx = dout("dbgx", [128, 8, T])

    uid = [0]

    def SBT(name, shape, dt=F32):
        return AR.alloc(list(shape), dt)

    def SBR(name, shape, dt=F32):
        uid[0] += 1
        return nc.sbuf_tensor(f"{name}_u{uid[0]}", list(shape), dt)

    es = contextlib.ExitStack()
    with es:
        def sb(name, shape, dt=F32):
            return es.enter_context(SBR(name, list(shape), dt))

        sems = {e: es.enter_context(nc.semaphore("s_" + e)) for e in ENGS}
        dsems = {e: [es.enter_context(nc.semaphore(f"d_{e}{i}")) for i in range(NDS)]
                 for e in ("sp", "pool", "act")}
        P = Prog(nc, sems, dsems)
        pall = es.enter_context(nc.psum_tensor("pall", [128, 4096], F32))
        ps = [pall[:, i * 512:(i + 1) * 512] for i in range(8)]

        xT = sb("xT", [128, 8, T])
        arena_t = sb("arena", [128, ARENA // 2], BF16)
        AR = Arena(arena_t[:])
        XN_BYTES = 8 * T * 2
        xn = AR.view(0, [128, 8, T], BF16)
        AR.top = XN_BYTES
        sm = sb("sm", [128, NSMALL])
        identf = sb("identf_sb", [128, 128])
        ones_bf = sb("ones_bf", [128, 128], BF16)
        epsb = sb("epsb", [128, 1])
        cst = sb("cst", [128, NCONST])
        identb = sb("identb_sb", [128, 128], BF16)
        ones_f = sb("ones_f", [64, 128])
        e0sel = sb("e0sel", [64, 128])
        oneb = sb("oneb", [128, 1])
        triu_bf = sb("triu_bf", [64, 64], BF16)
        maskS = sb("maskS_sb", [128, 16, 64], BF16)
        maskb = sb("maskb_sb", [128, 256], BF16)

        P.op("sp", lambda e: e.dma_start(out=sm[:], in_=smalls), w=["sm"], dma=True)
        P.op("sp", lambda e: e.dma_start(out=identf[:], in_=identf_d), w=["identf"], dma=True)
        P.op("sp", lambda e: e.dma_start(out=cst[:], in_=consts_d), w=["cst"], dma=True)
        P.op("sp", lambda e: e.dma_start(out=identb[:], in_=identb_d), w=["identb"], dma=True)
        P.op("dve", lambda e: e.memset(ones_bf[:], 1.0), w=["ones"])
        P.op("dve", lambda e: e.memset(epsb[:], EPS), w=["epsb"])
        P.op("dve", lambda e: e.memset(ones_f[:], 1.0), w=["onesf"])
        P.op("dve", lambda e: e.memset(oneb[:], 1.0), w=["oneb"])
        P.op("dve", lambda e: e.memset(e0sel[:], 0.0), w=["e0sel"])
        P.op("dve", lambda e: e.memset(e0sel[0:1, :], 1.0), w=["e0sel"])
        P.op("dve", lambda e: e.tensor_copy(out=triu_bf[:], in_=cst[0:64, C_TRIU:C_TRIU + 64]), r=["cst"], w=["triu"])
        P.op("sp", lambda e: e.dma_start(out=maskS[:], in_=maskS_d), w=["maskS"], dma=True)
        P.op("sp", lambda e: e.dma_start(out=maskb[:], in_=maskb_d), w=["maskb"], dma=True)
        with contextlib.ExitStack() as ph:
            xin = [ph.enter_context(SBT(f"xin{i}", [128, D], F32)) for i in range(2)]
            for tt in range(17):
                b = tt % 2
                rows = 128 if tt < 16 else 64
                if tt < 16:
                    src = xp[tt * 128:(tt + 1) * 128, :]
                    P.op("sp", lambda e, b=b, rows=rows, src=src: e.dma_start(out=xin[b][0:rows, :], in_=src),
                         w=[f"xin{b}"], dma=True)
                else:
                    for t_ in range(4):
                        P.op("sp", lambda e, b=b, t_=t_: e.dma_start(out=xin[b][t_ * 16:(t_ + 1) * 16, :],
                                                                     in_=xs[:, t_, :]),
                             w=[f"xin{b}"], dma=True)
                for half in range(2):
                    pb = ps[(tt * 2 + half) % 4]
                    ptag = f"ps{(tt * 2 + half) % 4}"
                    for c4 in range(4):
                        c = half * 4 + c4
                        P.op("pe", lambda e, pb=pb, c4=c4, c=c, b=b, rows=rows: e.transpose(
                            out=pb[:, c4 * 128:c4 * 128 + rows], in_=xin[b][0:rows, c * 128:(c + 1) * 128],
                            identity=identf[0:rows, 0:rows]),
                            r=[f"xin{b}", "identf"], w=[ptag])
                    P.op("dve", lambda e, pb=pb, half=half, tt=tt, rows=rows: e.tensor_copy(
                        out=xT[:, half * 4:half * 4 + 4, tt * 128:tt * 128 + rows],
                        in_=pb.rearrange("p (c t) -> p c t", c=4)[:, :, 0:rows]),
                        r=[ptag], w=[f"xT{tt // 4}"])
            P.flush("init")

        def xtag(t0):
            return f"xT{t0 // 512}"

        def rmsnorm_ops(ph, wcol, pb=(6, 7)):
            sq = [ph.enter_context(SBT(f"sq{i}", [128, 8, 512], BF16)) for i in range(2)]
            rstd = [ph.enter_context(SBT(f"rstd{i}", [128, 512], F32)) for i in range(2)]

            def square(bi):
                t0, tn = TBS[bi]
                b = bi % 2
                E("act", "activation", [xtag(t0)], [f"nsq{b}"], out=sq[b][:, :, 0:tn], in_=xT[:, :, t0:t0 + tn], func=AF.Square)

            square(0)
            for bi, (t0, tn) in enumerate(TBS):
                b = bi % 2
                pk = pb[b]
                if bi + 1 < len(TBS):
                    square(bi + 1)
                for c in range(8):
                    E("pe", "matmul", [f"nsq{b}", "ones"], [f"ps{pk}"], out=ps[pk][:, 0:tn], lhsT=ones_bf[:], rhs=sq[b][:, c, 0:tn],
                      start=(c == 0), stop=(c == 7))
                E("act", "activation", [f"ps{pk}", "epsb"], [f"nrstd{b}"], out=rstd[b][:, 0:tn], in_=ps[pk][:, 0:tn], func=AF.Ln, scale=1.0 / D, bias=epsb[:])
                E("act", "activation", [f"nrstd{b}"], [f"nrstd{b}"], out=rstd[b][:, 0:tn], in_=rstd[b][:, 0:tn], func=AF.Exp, scale=-0.5)
                for c in range(8):
                    E("dve", "scalar_tensor_tensor", [xtag(t0), "sm", f"nrstd{b}"], [f"xn{bi}"], out=xn[:, c, t0:t0 + tn], in0=xT[:, c, t0:t0 + tn],
                      scalar=sm[:, wcol + c:wcol + c + 1], in1=rstd[b][:, 0:tn], op0=ALU.mult, op1=ALU.mult)

        def ffn(l, wg_d, wu_d, wd_d, ncol):
            with contextlib.ExitStack() as ph:
                wgu = [ph.enter_context(SBT(f"wgu{i}", [128, 2, 8, 512], BF16)) for i in range(2)]
                wdb = [ph.enter_context(SBT(f"wdb{i}", [128, 4, D], BF16)) for i in range(2)]
                hb = ph.enter_context(SBT("hb", [128, 4, T], BF16))
                sg = [ph.enter_context(SBT(f"sg{i}", [128, 512], F32)) for i in range(2)]
                rmsnorm_ops(ph, ncol)
                cnt = 0
                for gi, (f0, fn_) in enumerate(FGROUPS):
                    b = gi % 2
                    wcols = fn_ * 128
                    for fl in range(fn_):
                        for which, wsrc in ((0, wg_d), (1, wu_d)):
                            DMA("pool", [], [f"wgu{b}_{which}_{fl}"], out=wgu[b][:, which, :, fl * 128:(fl + 1) * 128],
                                in_=wsrc[l, :, (f0 + fl) * 128:(f0 + fl + 1) * 128].rearrange("(c p) f -> p c f", p=128))
                    for fl in range(fn_):
                        DMA("pool", [], [f"wdb{b}_{fl}"], out=wdb[b][:, fl, :], in_=wd_d[l, (f0 + fl) * 128:(f0 + fl + 1) * 128, :])
                    for bi, (t0, tn) in enumerate(TBS):
                        for fl in range(fn_):
                            pg, pu = ps[(cnt % 2) * 2], ps[(cnt % 2) * 2 + 1]
                            tg, tu = f"ps{(cnt % 2) * 2}", f"ps{(cnt % 2) * 2 + 1}"
                            sgi = cnt % 2
                            cnt += 1
                            for which, pt, tag in ((0, pg, tg), (1, pu, tu)):
                                for c in range(8):
                                    P.op("pe", lambda e, pt=pt, b=b, which=which, c=c, fl=fl, t0=t0, tn=tn: e.matmul(
                                        pt[:, 0:tn], lhsT=wgu[b][:, which, c, fl * 128:(fl + 1) * 128],
                                        rhs=xn[:, c, t0:t0 + tn], start=(c == 0), stop=(c == 7)),
                                        r=[f"wgu{b}_{which}_{fl}", f"xn{bi}"], w=[tag])
                            P.op("act", lambda e, pg=pg, sgi=sgi, tn=tn: e.activation(
                                out=sg[sgi][:, 0:tn], in_=pg[:, 0:tn], func=AF.Silu), r=[tg], w=[f"sg{sgi}"])
                            P.op("dve", lambda e, pu=pu, sgi=sgi, fl=fl, t0=t0, tn=tn: e.tensor_tensor(
                                out=hb[:, fl, t0:t0 + tn], in0=sg[sgi][:, 0:tn], in1=pu[:, 0:tn], op=ALU.mult),
                                r=[tu, f"sg{sgi}"], w=[f"hb{bi}"])
                    for bi, (t0, tn) in enumerate(TBS):
                        for o in range(8):
                            py, ty = ps[4 + (cnt % 4)], f"ps{4 + (cnt % 4)}"
                            cnt += 1
                            for fl in range(fn_):
                                P.op("pe", lambda e, py=py, b=b, fl=fl, o=o, t0=t0, tn=tn, fn_=fn_: e.matmul(
                                    py[:, 0:tn], lhsT=wdb[b][:, fl, o * 128:(o + 1) * 128],
                                    rhs=hb[:, fl, t0:t0 + tn], start=(fl == 0), stop=(fl == fn_ - 1)),
                                    r=[f"wdb{b}_{fl}", f"hb{bi}"], w=[ty])
                            P.op("dve", lambda e, py=py, o=o, t0=t0, tn=tn: e.scalar_tensor_tensor(
                                out=xT[:, o, t0:t0 + tn], in0=py[:, 0:tn], scalar=0.5, in1=xT[:, o, t0:t0 + tn],
                                op0=ALU.mult, op1=ALU.add), r=[ty, xtag(t0)], w=[xtag(t0)])
                P.flush("ffn")


        def swa(l):
            with contextlib.ExitStack() as ph:
                def pb_(name, shape, dt=F32):
                    return ph.enter_context(SBT(name, list(shape), dt))
                wsw = pb_("wsw", [128, 8, 768], BF16)
                wkd = pb_("wkd", [128, 8, 2, 128], BF16)
                wo = pb_("wo_s", [128, 4, D], BF16)
                kd = pb_("kd", [128, 2, T], BF16)
                vtm = pb_("vtm", [128, 17, 128], BF16)
                kvo = [pb_(f"kvo{i}", [128, 128]) for i in range(4)]
                so = pb_("so", [128, 4, 128])
                sq = pb_("sq_s", [128, 4, 128], BF16)
                rstd = pb_("rstd_s", [128, 128])
                mo = pb_("mo", [128, 4, 128], BF16)
                p2 = contextlib.ExitStack()
                p2.__enter__()
                def pb2(name, shape, dt=F32):
                    return p2.enter_context(SBT(name, list(shape), dt))
                qT = pb2("qT", [128, 4, T], BF16)
                sc = [pb2("sc", [128, 4, 256]) for _ in range(2)]
                mx = [pb2("mx", [128, 4]) for _ in range(2)]; mx2 = [pb2("mx2", [128, 4]) for _ in range(2)]
                rs = [pb2("rs", [128, 4]) for _ in range(2)]; es_ = [pb2("es_", [128, 4]) for _ in range(2)]
                pn = [pb2("pn", [128, 4, 256], BF16) for _ in range(2)]
                pTs = [pb2("pT", [128, 8, 128], BF16) for _ in range(2)]
                rmsnorm_ops(p2, SM_NM + 8 * l)
                P.op("pool", lambda e: e.dma_start(out=wsw[:], in_=w_in[l, :, OFF_SQ:OFF_SQ + 768].rearrange(
                    "(c p) f -> p c f", p=128)), w=["wsw"], dma=True)
                for g in range(2):
                    for hf in range(2):
                        P.op("pool", lambda e, g=g, hf=hf: e.dma_start(
                            out=wkd[:, :, g, hf * 64:(hf + 1) * 64],
                            in_=w_in[l, :, OFF_SK + g * 64:OFF_SK + (g + 1) * 64].rearrange("(c p) f -> p c f", p=128)),
                            w=["wkd"], dma=True)
                P.op("pool", lambda e: e.dma_start(out=wo[:], in_=w_out[l, 512:1024, :].rearrange(
                    "(c p) f -> p c f", p=128)), w=["wo"], dma=True)
                if "d2d" not in SKIP:
                    P.op("sp", lambda e: e.dma_start(out=oks[l, :, 0:124, :], in_=ck_d[l, :, 4:128, :]), dma=True)
                    P.op("sp", lambda e: e.dma_start(out=ovs[l, :, 0:124, :], in_=cv_d[l, :, 4:128, :]), dma=True)

                cnt = [0]
                def nb():
                    cnt[0] += 1
                    return cnt[0] % 4
                for j in range(4 if "projq" not in SKIP else 0):
                    for bi, (t0, tn) in enumerate(TBS):
                        k_ = nb()
                        for c in range(8):
                            P.op("pe", lambda e, k_=k_, c=c, j=j, t0=t0, tn=tn: e.matmul(
                                ps[k_][:, 0:tn], lhsT=wsw[:, c, j * 128:(j + 1) * 128], rhs=xn[:, c, t0:t0 + tn],
                                start=(c == 0), stop=(c == 7)), r=["wsw", f"xn{bi}"], w=[f"ps{k_}"])
                        P.op("act", lambda e, k_=k_, j=j, t0=t0, tn=tn: e.copy(out=qT[:, j, t0:t0 + tn], in_=ps[k_][:, 0:tn]),
                             r=[f"ps{k_}"], w=["qT"])
                for g in range(2 if "projk" not in SKIP else 0):
                    for bi, (t0, tn) in enumerate(TBS):
                        k_ = nb()
                        for c in range(8):
                            P.op("pe", lambda e, k_=k_, c=c, g=g, t0=t0, tn=tn: e.matmul(
                                ps[k_][:, 0:tn], lhsT=wkd[:, c, g, :], rhs=xn[:, c, t0:t0 + tn],
                                start=(c == 0), stop=(c == 7)), r=["wkd", f"xn{bi}"], w=[f"ps{k_}"])
                        P.op("act", lambda e, k_=k_, g=g, t0=t0, tn=tn: e.copy(out=kd[:, g, t0:t0 + tn], in_=ps[k_][:, 0:tn]),
                             r=[f"ps{k_}"], w=["kd"])
                for tt in range(17 if "projv" not in SKIP else 0):
                    rows = 128 if tt < 16 else 64
                    k_ = nb()
                    for c in range(8):
                        P.op("pe", lambda e, k_=k_, c=c, tt=tt, rows=rows: e.matmul(
                            ps[k_][0:rows, 0:256], lhsT=xn[:, c, tt * 128:tt * 128 + rows], rhs=wsw[:, c, 512:768],
                            start=(c == 0), stop=(c == 7)), r=["wsw", f"xn{tt // 4}"], w=[f"ps{k_}"])
                    P.op("act", lambda e, k_=k_, tt=tt, rows=rows: e.copy(out=vtm[0:rows, tt, :], in_=ps[k_][0:rows, 128:256]),
                         r=[f"ps{k_}"], w=["vtm"])
                    if tt >= 15:
                        i0 = (tt - 15) * 2
                        P.op("dve", lambda e, k_=k_, i0=i0, rows=rows: e.tensor_copy(out=kvo[i0][0:rows, :], in_=ps[k_][0:rows, 0:128]),
                             r=[f"ps{k_}"], w=[f"kvo{i0}"])
                        P.op("dve", lambda e, k_=k_, i0=i0, rows=rows: e.tensor_copy(out=kvo[i0 + 1][0:rows, :], in_=ps[k_][0:rows, 128:256]),
                             r=[f"ps{k_}"], w=[f"kvo{i0 + 1}"])
                if "kvo" not in SKIP:
                    P.op("sp", lambda e: e.dma_start(out=okp[l], in_=kvo[0][:]), r=["kvo0"], dma=True)
                    P.op("sp", lambda e: e.dma_start(out=ovp[l], in_=kvo[1][:]), r=["kvo1"], dma=True)
                for t_ in range(4 if "kvo" not in SKIP else 0):
                    P.op("sp", lambda e, t_=t_: e.dma_start(out=oks[l, :, 124 + t_, :], in_=kvo[2][t_ * 16:(t_ + 1) * 16, :]),
                         r=["kvo2"], dma=True)
                    P.op("sp", lambda e, t_=t_: e.dma_start(out=ovs[l, :, 124 + t_, :], in_=kvo[3][t_ * 16:(t_ + 1) * 16, :]),
                         r=["kvo3"], dma=True)

                def epilogue_gen(t0, n):
                    E("act", "activation", ["so"], ["sq_s"], out=sq[:, :, 0:n], in_=so[:, :, 0:n], func=AF.Square)
                    yield
                    for c in range(4):
                        E("pe", "matmul", ["sq_s", "ones"], ["ps3"], out=ps[3][:, 0:n], lhsT=ones_bf[:], rhs=sq[:, c, 0:n], start=(c == 0), stop=(c == 3))
                    yield
                    E("act", "activation", ["ps3", "epsb"], ["rstd_s"], out=rstd[:, 0:n], in_=ps[3][:, 0:n], func=AF.Ln, scale=1.0 / 512, bias=epsb[:])
                    yield
                    E("act", "activation", ["rstd_s"], ["rstd_s"], out=rstd[:, 0:n], in_=rstd[:, 0:n], func=AF.Exp, scale=-0.5)
                    yield
                    for c in range(4):
                        E("dve", "scalar_tensor_tensor", ["so", "sm", "rstd_s"], ["mo"], out=mo[:, c, 0:n], in0=so[:, c, 0:n],
                          scalar=sm[:, SM_SWN + 4 * l + c:SM_SWN + 4 * l + c + 1], in1=rstd[:, 0:n], op0=ALU.mult, op1=ALU.mult)
                        if c % 2 == 1:
                            yield
                    for half in range(2):
                        for o4 in range(4):
                            o = half * 4 + o4
                            for c in range(4):
                                E("pe", "matmul", ["wo", "mo"], ["ps7"], out=ps[7][:, o4 * 128:o4 * 128 + n], lhsT=wo[:, c, o * 128:(o + 1) * 128],
                                  rhs=mo[:, c, 0:n], start=(c == 0), stop=(c == 3))
                        yield
                        E("dve", "tensor_tensor", ["ps7", xtag(t0)], [xtag(t0)], out=xT[:, half * 4:half * 4 + 4, t0:t0 + n],
                          in0=ps[7].rearrange("p (o t) -> p o t", o=4)[:, :, 0:n], in1=xT[:, half * 4:half * 4 + 4, t0:t0 + n], op=ALU.add)
                        yield

                def epilogue(t0, n):
                    for _ in epilogue_gen(t0, n):
                        pass

                mask2 = cst[:, C_MASK2:C_MASK2 + 256]
                S4s = [pall[:, 0:1024].rearrange("p (h k) -> p h k", h=4), pall[:, 2048:3072].rearrange("p (h k) -> p h k", h=4)]
                S4t = [("ps0", "ps1"), ("ps4", "ps5")]
                PTbs = [ps[2].bitcast(BF16).rearrange("p (i q) -> p i q", i=8), ps[6].bitcast(BF16).rearrange("p (i q) -> p i q", i=8)]
                PO = ps[3].rearrange("p (j q) -> p j q", j=4)
                items = [(b, g) for b in range(16) for g in range(2)]

                def geom(b):
                    c0 = 128 if b == 0 else 0
                    return c0, 256 - c0, (b - 1) * 128 + c0

                def scores(i):
                    b, g = items[i]
                    c0, nk, k0 = geom(b)
                    par = i % 2
                    for hl in range(4):
                        h = g * 4 + hl
                        base = (h % 2) * 64
                        sl_ = (hl % 2) * 2 + hl // 2
                        E("pe", "matmul", ["qT", "kd"], [S4t[par][sl_ // 2]], out=S4s[par][:, sl_, c0:256],
                          lhsT=qT[base:base + 64, h // 2, b * 128:(b + 1) * 128], rhs=kd[base:base + 64, g, k0:k0 + nk], start=True, stop=False)
                        E("pe", "matmul", ["identb", "maskb"], [S4t[par][sl_ // 2]], out=S4s[par][:, sl_, c0:256],
                          lhsT=identb[:], rhs=maskb[:, c0:256], start=False, stop=True)

                def softmax_gen(i):
                    b, g = items[i]
                    c0, nk, k0 = geom(b)
                    par = i % 2
                    S4, sc_, pn_ = S4s[par], sc[par], pn[par]
                    mx_, mx2_, rs_, es2 = mx[par], mx2[par], rs[par], es_[par]
                    tg = lambda nm: f"{nm}{par}"
                    pt = [S4t[par][0], S4t[par][1]]
                    sk = sm[:, SM_SINK + 8 * l + 4 * g:SM_SINK + 8 * l + 4 * g + 4]
                    E("dve", "tensor_reduce", pt, [tg("mx")], out=mx_[:], in_=S4[:, :, c0:256], axis=AX.X, op=ALU.max)
                    yield
                    E("dve", "scalar_tensor_tensor", [tg("mx"), "sm"], [tg("mx2")], out=mx2_[:], in0=mx_[:], scalar=0.125, in1=sk, op0=ALU.mult, op1=ALU.max)
                    yield
                    E("dve", "tensor_scalar", [tg("mx2")], [tg("mx")], out=mx_[:], in0=mx2_[:], scalar1=-1.0, scalar2=None, op0=ALU.mult)
                    yield
                    for sl_ in range(4):
                        E("act", "activation", [pt[sl_ // 2], tg("mx")], [tg("sc"), tg("rs")], out=sc_[:, sl_, c0:256], in_=S4[:, sl_, c0:256], func=AF.Exp,
                          scale=0.125, bias=mx_[:, sl_:sl_ + 1], accum_out=rs_[:, sl_:sl_ + 1])
                    yield
                    E("dve", "tensor_tensor", [tg("mx"), "sm"], [tg("es")], out=es2[:], in0=sk, in1=mx_[:], op=ALU.add)
                    yield
                    E("act", "activation", [tg("es")], [tg("es")], out=es2[:], in_=es2[:], func=AF.Exp)
                    yield
                    E("dve", "tensor_tensor", [tg("rs"), tg("es")], [tg("rs")], out=rs_[:], in0=rs_[:], in1=es2[:], op=ALU.add)
                    yield
                    E("dve", "reciprocal", [tg("rs")], [tg("rs")], out=rs_[:], in_=rs_[:])
                    yield
                    E("dve", "tensor_tensor", [tg("sc"), tg("rs")], [tg("pn")], out=pn_[:, :, c0:256], in0=sc_[:, :, c0:256],
                      in1=bc(rs_[:], 2, [128, 4, nk]), op=ALU.mult)
                    yield

                def tail(i):
                    b, g = items[i]
                    par = i % 2
                    pn_ = pn[par]
                    PTb = PTbs[par]; pT = pTs[par]; ptag = ["ps2", "ps6"][par]; ttag = f"pT{par}"
                    kts = [1] if b == 0 else [0, 1]
                    for hl in range(4):
                        for kt in kts:
                            E("pe", "transpose", [f"pn{par}", "identb"], [ptag], out=PTb[:, hl * 2 + kt, :],
                              in_=pn_[:, (hl % 2) * 2 + hl // 2, kt * 128:(kt + 1) * 128], identity=identb[:])
                    if b == 0:
                        E("act", "copy", [ptag], [ttag], out=pT[:, 1::2, :], in_=PTb[:, 1::2, :])
                    else:
                        E("act", "copy", [ptag], [ttag], out=pT[:], in_=PTb)
                    for hl in range(4):
                        r0 = (hl % 2) * 64
                        for kt in kts:
                            E("pe", "matmul", ["vtm", ttag], ["ps3"], out=PO[r0:r0 + 64, g * 2 + hl // 2, :], lhsT=vtm[:, b - 1 + kt, g * 64:(g + 1) * 64],
                              rhs=pT[:, hl * 2 + kt, :], start=(kt == kts[0]), stop=(kt == kts[-1]), tile_position=(0, r0))

                if "attn" not in SKIP:
                    scores(0)
                    scores(1)
                    for b_ in range(16):
                        gens = [softmax_gen(2 * b_), softmax_gen(2 * b_ + 1)]
                        for _ in range(4):
                            for g_ in gens:
                                next(g_)
                        if b_ + 1 < 16:
                            scores(2 * b_ + 2)
                            scores(2 * b_ + 3)
                        alive = list(gens)
                        if b_ > 0:
                            alive.append(epilogue_gen((b_ - 1) * 128, 128))
                        while alive:
                            for g_ in list(alive):
                                try:
                                    next(g_)
                                except StopIteration:
                                    alive.remove(g_)
                        tail(2 * b_)
                        tail(2 * b_ + 1)
                        E("act", "copy", ["ps3"], ["so"], out=so[:], in_=PO)
                    epilogue(15 * 128, 128)

                P.flush("swa_p")
                p2.__exit__(None, None, None)
                if dbg == "swa_p":
                    return
                p3 = contextlib.ExitStack()
                p3.__enter__()
                def pb3(name, shape, dt=F32):
                    return p3.enter_context(SBT(name, list(shape), dt))
                ckb = pb3("ckb", [128, 16, 128], BF16)
                cvb = pb3("cvb", [128, 16, 128], BF16)
                qtm = pb3("qtm", [64, 512], BF16)
                qTs = pb3("qTs", [64, 16, 8, 4], BF16)
                kf = pb3("kf", [64, 16, 2, 132], BF16)
                scs = pb3("scs", [16, 8, 132])
                mxs = pb3("mxs", [16, 8]); mxs2 = pb3("mxs2", [16, 8]); rss = pb3("rss", [16, 8]); ess = pb3("ess", [16, 8])
                pc = pb3("pc", [16, 8, 128], BF16)
                pz = pb3("pz", [16, 16, 2, 64], BF16)
                ptc = pb3("ptc", [128, 8, 16], BF16)
                ptz = pb3("ptz", [64, 8, 16], BF16)
                osb = pb3("osb", [16, 16, 2, 64])
                otm = pb3("otm", [64, 512])
                P.op("pool", lambda e: e.dma_start(out=ckb[:], in_=ck_d[l].rearrange("s k f -> k s f")), w=["ckb"], dma=True)
                P.op("pool", lambda e: e.dma_start(out=cvb[:], in_=cv_d[l].rearrange("s k f -> k s f")), w=["cvb"], dma=True)
                P.op("dve", lambda e: e.memset(pz[:], 0.0), w=["pz"])
                k_ = 0
                for c in range(8):
                    P.op("pe", lambda e, c=c: e.matmul(ps[0][0:64, :], lhsT=xn[:, c, NPR:T], rhs=wsw[:, c, 0:512],
                                                      start=(c == 0), stop=(c == 7)), r=["wsw", "xn4"], w=["ps0"])
                P.op("act", lambda e: e.copy(out=qtm[:], in_=ps[0][0:64, :]), r=["ps0"], w=["qtm"])
                QTb = ps[1].bitcast(BF16)[0:64, 0:512].rearrange("p (h t s) -> p h t s", h=8, t=4)
                for h in range(8):
                    P.op("pe", lambda e, h=h: e.transpose(out=ps[1].bitcast(BF16)[0:64, h * 64:(h + 1) * 64],
                                                          in_=qtm[:, h * 64:(h + 1) * 64], identity=identb[0:64, 0:64]),
                         r=["qtm", "identb"], w=["ps1"])
                P.op("dve", lambda e: e.tensor_copy(out=qTs[:].rearrange("p s h t -> p h t s"), in_=QTb), r=["ps1"], w=["qTs"])
                KTb = ps[2].bitcast(BF16)[0:64, :].rearrange("p (i k) -> p i k", i=8)
                for w4 in range(4):
                    for sl in range(4):
                        for g in range(2):
                            P.op("pe", lambda e, w4=w4, sl=sl, g=g: e.transpose(
                                out=KTb[:, sl * 2 + g, :], in_=ckb[:, w4 * 4 + sl, g * 64:(g + 1) * 64], identity=identb[:]),
                                r=["ckb", "identb"], w=["ps2"])
                    P.op("act", lambda e, w4=w4: e.copy(
                        out=kf[:, w4 * 4:w4 * 4 + 4, :, 0:128],
                        in_=KTb.rearrange("p (s g) k -> p s g k", g=2)), r=["ps2"], w=["kf"])
                P.op("dve", lambda e: e.tensor_copy(
                    out=kf[:, :, :, 128:132].rearrange("p s g t -> p g t s"),
                    in_=kd[0:64, :, NPR:T].rearrange("p g (t s) -> p g t s", t=4)), r=["kd"], w=["kf"])
                smask = cst[0:16, C_SMASK:C_SMASK + 132]
                SC = pall[0:16, 0:1024].rearrange("p (i k) -> p i k", i=8)
                SN = ps[2][0:16, 0:32].rearrange("p (i k) -> p i k", i=8)
                PTC = ps[3].bitcast(BF16)[:, 0:128].rearrange("p (i q) -> p i q", i=8)
                PTZ = ps[3].bitcast(BF16)[0:64, 128:256].rearrange("p (i q) -> p i q", i=8)
                OS = ps[4][0:16, :].rearrange("p (i d) -> p i d", i=8)
                for w4 in range(4):
                    for sl in range(4):
                        s_ = w4 * 4 + sl
                        for g in range(2):
                            i = sl * 2 + g
                            P.op("pe", lambda e, s_=s_, g=g, i=i: e.matmul(
                                SC[:, i, :], lhsT=qTs[:, s_, g * 4:(g + 1) * 4, :], rhs=kf[:, s_, g, 0:128],
                                start=True, stop=True), r=["qTs", "kf"], w=[f"ps{i // 4}"])
                            P.op("pe", lambda e, s_=s_, g=g, i=i: e.matmul(
                                SN[:, i, :], lhsT=qTs[:, s_, g * 4:(g + 1) * 4, :], rhs=kf[:, s_, g, 128:132],
                                start=True, stop=True), r=["qTs", "kf"], w=["ps2"])
                    P.op("dve", lambda e: e.scalar_tensor_tensor(
                        out=scs[:, :, 0:128], in0=SC, scalar=0.125,
                        in1=smask[:, 0:128].unsqueeze(1).broadcast_to([16, 8, 128]), op0=ALU.mult, op1=ALU.add),
                        r=["ps0", "ps1", "cst"], w=["scs"])
                    P.op("dve", lambda e: e.scalar_tensor_tensor(
                        out=scs[:, :, 128:132], in0=SN, scalar=0.125,
                        in1=smask[:, 128:132].unsqueeze(1).broadcast_to([16, 8, 4]), op0=ALU.mult, op1=ALU.add),
                        r=["ps2", "cst"], w=["scs"])
                    P.op("dve", lambda e: e.tensor_reduce(out=mxs[:], in_=scs[:], axis=AX.X, op=ALU.max), r=["scs"], w=["mxs"])
                    sks = sm[0:16, SM_SSINK + 2 * l:SM_SSINK + 2 * l + 2].unsqueeze(1).broadcast_to([16, 4, 2])
                    mxs3 = mxs[:].rearrange("p (s g) -> p s g", g=2)
                    mxs23 = mxs2[:].rearrange("p (s g) -> p s g", g=2)
                    ess3 = ess[:].rearrange("p (s g) -> p s g", g=2)
                    P.op("dve", lambda e, sks=sks, mxs3=mxs3, mxs23=mxs23: e.tensor_tensor(out=mxs23, in0=mxs3, in1=sks, op=ALU.max),
                         r=["mxs", "sm"], w=["mxs2"])
                    P.op("dve", lambda e: e.tensor_tensor(out=scs[:], in0=scs[:], in1=mxs2[:].unsqueeze(2).broadcast_to([16, 8, 132]),
                                                          op=ALU.subtract), r=["scs", "mxs2"], w=["scs"])
                    P.op("act", lambda e: e.activation(out=scs[:], in_=scs[:], func=AF.Exp), r=["scs"], w=["scs"])
                    P.op("dve", lambda e: e.tensor_reduce(out=rss[:], in_=scs[:], axis=AX.X, op=ALU.add), r=["scs"], w=["rss"])
                    P.op("dve", lambda e, sks=sks, mxs23=mxs23, ess3=ess3: e.tensor_tensor(out=ess3, in0=sks, in1=mxs23, op=ALU.subtract),
                         r=["mxs2", "sm"], w=["ess"])
                    P.op("act", lambda e: e.activation(out=ess[:], in_=ess[:], func=AF.Exp), r=["ess"], w=["ess"])
                    P.op("dve", lambda e: e.tensor_tensor(out=rss[:], in0=rss[:], in1=ess[:], op=ALU.add), r=["rss", "ess"], w=["rss"])
                    P.op("dve", lambda e: e.reciprocal(out=rss[:], in_=rss[:]), r=["rss"], w=["rss"])
                    P.op("dve", lambda e: e.tensor_tensor(out=pc[:], in0=scs[:, :, 0:128],
                                                          in1=rss[:].unsqueeze(2).broadcast_to([16, 8, 128]), op=ALU.mult),
                         r=["scs", "rss"], w=["pc"])
                    for sl in range(4):
                        s_ = w4 * 4 + sl
                        P.op("dve", lambda e, sl=sl, s_=s_: e.tensor_tensor(
                            out=pz[:, s_, :, :].rearrange("p g (t s) -> p g t s", t=4)[:, :, :, s_],
                            in0=scs[:, sl * 2:sl * 2 + 2, 128:132],
                            in1=rss[:, sl * 2:sl * 2 + 2].unsqueeze(2).broadcast_to([16, 2, 4]), op=ALU.mult),
                            r=["scs", "rss"], w=["pz"])
                    for sl in range(4):
                        s_ = w4 * 4 + sl
                        for g in range(2):
                            i = sl * 2 + g
                            P.op("pe", lambda e, i=i: e.transpose(out=PTC[:, i, :], in_=pc[:, i, :], identity=identb[0:16, 0:16]),
                                 r=["pc", "identb"], w=["ps3"])
                            P.op("pe", lambda e, i=i, s_=s_, g=g: e.transpose(out=PTZ[:, i, :], in_=pz[:, s_, g, :], identity=identb[0:16, 0:16]),
                                 r=["pz", "identb"], w=["ps3"])
                    P.op("act", lambda e: e.copy(out=ptc[:], in_=PTC), r=["ps3"], w=["ptc"])
                    P.op("act", lambda e: e.copy(out=ptz[:], in_=PTZ), r=["ps3"], w=["ptz"])
                    for sl in range(4):
                        s_ = w4 * 4 + sl
                        for g in range(2):
                            i = sl * 2 + g
                            P.op("pe", lambda e, i=i, s_=s_, g=g: e.matmul(
                                OS[:, i, :], lhsT=ptc[:, i, :], rhs=cvb[:, s_, g * 64:(g + 1) * 64], start=True, stop=False),
                                r=["ptc", "cvb"], w=["ps4"])
                            P.op("pe", lambda e, i=i, s_=s_, g=g: e.matmul(
                                OS[:, i, :], lhsT=ptz[:, i, :], rhs=vtm[0:64, 16, g * 64:(g + 1) * 64], start=False, stop=True),
                                r=["ptz", "vtm"], w=["ps4"])
                    P.op("dve", lambda e, w4=w4: e.tensor_copy(
                        out=osb[:, w4 * 4:w4 * 4 + 4, :, :], in_=OS.rearrange("p (s g) d -> p s g d", g=2)), r=["ps4"], w=["osb"])
                for hl in range(4):
                    for t_ in range(4):
                        P.op("sp", lambda e, hl=hl, t_=t_: e.dma_start(
                            out=otm[t_ * 16:(t_ + 1) * 16, :].rearrange("s (g h d) -> s g h d", g=2, h=4)[:, :, hl, :],
                            in_=osb[hl * 4 + t_:hl * 4 + t_ + 1, :, :, :]), r=["osb"], w=["otm"], dma=True)
                for c in range(4):
                    P.op("pe", lambda e, c=c: e.transpose(out=ps[3][:, c * 64:(c + 1) * 64], in_=otm[:, c * 128:(c + 1) * 128],
                                                          identity=identf[0:64, 0:64]), r=["otm", "identf"], w=["ps3"])
                P.op("dve", lambda e: e.tensor_copy(out=so[:, :, 0:64], in_=ps[3][:, 0:256].rearrange("p (c t) -> p c t", c=4)),
                     r=["ps3"], w=["so"])
                epilogue(NPR, 64)
                P.flush("swa_s")
                p3.__exit__(None, None, None)

        def E(eng, meth, r, w, **kw):
            P.op(eng, lambda e, kw=kw, meth=meth: getattr(e, meth)(**kw), r=r, w=w)

        def DMA(eng, r, w, **kw):
            P.op(eng, lambda e, kw=kw: e.dma_start(**kw), r=r, w=w, dma=True)

        def bc(ap, axis, shape):
            return ap.unsqueeze(axis).broadcast_to(list(shape))

        def gdn(l):
            QKV_OFF = XN_BYTES
            qkv = AR.view(QKV_OFF, [128, 12, T], BF16)
            ZS_OFF = ARENA - 4 * T * 2
            zs = AR.view(ZS_OFF, [128, 4, T], BF16)
            GP = ZS_OFF - 4608
            names = ["btm", "gtm", "gctm", "gltm", "egc", "kdc", "bk"]
            G = {nm: AR.view(GP + 528 * i, [64, 33, 4], F32) for i, nm in enumerate(names)}
            btm, gtm, gctm, gltm, egc, kdc, bk = [G[nm] for nm in names]
            cdec = AR.view(GP + 3696, [128, 32, 4], F32)
            cdecs = AR.view(GP + 4208, [128, 16, 4], F32)
            TMP0 = QKV_OFF + 12 * T * 2
            cwc = SM_CONVW + 48 * l
            id64 = identf[0:64, 0:64]

            AR.top = TMP0
            with contextlib.ExitStack() as ph:
                def al(shape, dt=F32):
                    return ph.enter_context(AR.alloc(list(shape), dt))
                wba = al([128, 8, 8], BF16)
                xa = al([64, 33, 4]); xb = al([64, 33, 4]); nA = al([64, 4]); rhs_s = al([64, 16, 4])
                DMA("pool", [], ["wba"], out=wba[:], in_=w_in[l, :, OFF_B:OFF_B + 8].rearrange("(c p) f -> p c f", p=128))
                BA = ps[5][0:64, 0:264].rearrange("p (n f) -> p n f", f=8)
                for n in range(33):
                    for c in range(8):
                        E("pe", "matmul", ["wba", "xn4" if n == 32 else f"xn{n // 8}"], ["ps5"], out=BA[:, n, :],
                          lhsT=xn[:, c, n * 64:(n + 1) * 64], rhs=wba[:, c, :], start=(c == 0), stop=(c == 7))
                E("act", "activation", ["ps5"], ["btm"], out=btm[:], in_=BA[:, :, 0:4], func=AF.Sigmoid)
                E("dve", "tensor_tensor", ["ps5", "sm"], ["xa"], out=xa[:], in0=BA[:, :, 4:8],
                  in1=bc(sm[0:64, SM_DTB + 4 * l:SM_DTB + 4 * l + 4], 1, [64, 33, 4]), op=ALU.add)
                E("act", "activation", ["xa"], ["xb"], out=xb[:], in_=xa[:], func=AF.Abs)
                E("act", "activation", ["xb"], ["xb"], out=xb[:], in_=xb[:], func=AF.Exp, scale=-1.0)
                E("act", "activation", ["xb", "oneb"], ["xb"], out=xb[:], in_=xb[:], func=AF.Ln, bias=oneb[0:64, :])
                E("dve", "tensor_scalar", ["xa"], ["xa"], out=xa[:], in0=xa[:], scalar1=0.0, scalar2=None, op0=ALU.max)
                E("dve", "tensor_tensor", ["xa", "xb"], ["xa"], out=xa[:], in0=xa[:], in1=xb[:], op=ALU.add)
                E("act", "activation", ["sm"], ["nA"], out=nA[:], in_=sm[0:64, SM_ALOG + 4 * l:SM_ALOG + 4 * l + 4], func=AF.Exp)
                E("dve", "tensor_scalar", ["nA"], ["nA"], out=nA[:], in0=nA[:], scalar1=-1.0, scalar2=None, op0=ALU.mult)
                E("dve", "tensor_tensor", ["xa", "nA"], ["gtm"], out=gtm[:], in0=xa[:], in1=bc(nA[:], 1, [64, 33, 4]), op=ALU.mult)
                gflat = gtm[:, 0:32, :].rearrange("p n h -> p (n h)")
                E("pe", "matmul", ["gtm", "cst"], ["ps4"], out=ps[4][0:64, 0:128], lhsT=cst[0:64, C_TRI:C_TRI + 64], rhs=gflat, start=True, stop=True)
                E("pe", "matmul", ["gtm", "cst"], ["ps4"], out=ps[4][0:64, 128:132], lhsT=cst[0:64, C_TRIS:C_TRIS + 64], rhs=gtm[:, 32, :], start=True, stop=True)
                E("pe", "matmul", ["gtm", "onesf"], ["ps4"], out=ps[4][0:64, 256:384], lhsT=ones_f[0:64, 0:64], rhs=gflat, start=True, stop=True)
                E("pe", "matmul", ["gtm", "cst"], ["ps4"], out=ps[4][0:64, 384:388], lhsT=cst[0:64, C_SAMES:C_SAMES + 64], rhs=gtm[:, 32, :], start=True, stop=True)
                E("dve", "tensor_copy", ["ps4"], ["gctm"], out=gctm[:].rearrange("p n h -> p (n h)"), in_=ps[4][0:64, 0:132])
                E("dve", "tensor_copy", ["ps4"], ["gltm"], out=gltm[:].rearrange("p n h -> p (n h)"), in_=ps[4][0:64, 256:388])
                E("act", "activation", ["gctm"], ["egc"], out=egc[:], in_=gctm[:], func=AF.Exp)
                E("dve", "tensor_tensor", ["gltm", "gctm"], ["kdc"], out=kdc[:], in0=gltm[:], in1=gctm[:], op=ALU.subtract)
                E("act", "activation", ["kdc"], ["kdc"], out=kdc[:], in_=kdc[:], func=AF.Exp)
                E("dve", "tensor_tensor", ["btm", "egc"], ["bk"], out=bk[:], in0=btm[:], in1=egc[:], op=ALU.mult)
                E("pe", "matmul", ["gltm", "e0sel"], ["ps4"], out=ps[4][:, 0:128], lhsT=e0sel[:], rhs=gltm[:, 0:32, :].rearrange("p n h -> p (n h)"),
                  start=True, stop=True)
                E("dve", "tensor_tensor", ["gltm", "cst"], ["rhs_s"], out=rhs_s[:], in0=bc(cst[0:64, C_OH48:C_OH48 + 16], 2, [64, 16, 4]),
                  in1=bc(gltm[:, 32, :], 1, [64, 16, 4]), op=ALU.mult)
                E("pe", "matmul", ["rhs_s", "onesf"], ["ps4"], out=ps[4][:, 128:192], lhsT=ones_f[0:64, :], rhs=rhs_s[:].rearrange("p s h -> p (s h)"),
                  start=True, stop=True)
                E("act", "activation", ["ps4"], ["cdec"], out=cdec[:].rearrange("p n h -> p (n h)"), in_=ps[4][:, 0:128], func=AF.Exp)
                E("act", "activation", ["ps4"], ["cdecs"], out=cdecs[:].rearrange("p s h -> p (s h)"), in_=ps[4][:, 128:192], func=AF.Exp)

                P.flush("gdn_gates")
            AR.top = TMP0
            with contextlib.ExitStack() as ph:
                def al(shape, dt=F32):
                    return ph.enter_context(AR.alloc(list(shape), dt))
                wblk = al([128, 8, 256], BF16)
                hpre = al([128, T]); accb = [al([128, 1088]) for _ in range(2)]
                cs = al([128, 16, 3]); cin = al([48, 128])
                fulls = al([128, 16, 7]); accs = al([128, 16, 4])
                sqbs = [al([128, 1024], BF16), al([128, 1088], BF16)]
                htm = al([96, 256])
                assert AR.top <= GP, AR.top
                acnt = [0]
                kcnt = [0]
                for wb in range(8):
                    col0 = wb * 256
                    DMA("pool", [], ["wblk"], out=wblk[:], in_=w_in[l, :, col0:col0 + 256].rearrange("(c p) f -> p c f", p=128))
                    if wb < 6:
                        for c in range(8):
                            E("pe", "matmul", ["wblk", "xn3", "xn4"], ["ps7"], out=ps[7][0:96, 0:256], lhsT=xn[:, c, NPR - 32:T], rhs=wblk[:, c, :],
                              start=(c == 0), stop=(c == 7))
                        E("act", "copy", ["ps7"], ["htm"], out=htm[:], in_=ps[7][0:96, 0:256])
                        DMA("sp", ["htm"], [], out=ocp[l, :, col0:col0 + 256], in_=htm[29:32, :])
                        for i3 in range(3):
                            DMA("sp", ["htm"], [], out=ocs[l, :, i3, col0:col0 + 256], in_=htm[32 + (i3 + 1) * 16:32 + (i3 + 2) * 16, :])
                    for jj in range(2):
                        j = wb * 2 + jj
                        for bi, (t0, tn) in enumerate(TBS):
                            k_ = (0, 1, 7)[kcnt[0] % 3]
                            kcnt[0] += 1
                            for c in range(8):
                                E("pe", "matmul", ["wblk", f"xn{bi}"], [f"ps{k_}"], out=ps[k_][:, 0:tn], lhsT=wblk[:, c, jj * 128:(jj + 1) * 128],
                                  rhs=xn[:, c, t0:t0 + tn], start=(c == 0), stop=(c == 7))
                            if j >= 12:
                                E("act", "activation", [f"ps{k_}"], ["zs"], out=zs[:, j - 12, t0:t0 + tn], in_=ps[k_][:, 0:tn], func=AF.Silu)
                            else:
                                E("act", "copy", [f"ps{k_}"], ["hpre"], out=hpre[:, t0:t0 + tn], in_=ps[k_][:, 0:tn])
                        if j >= 12:
                            continue
                        w_ = [sm[:, cwc + j * 4 + i:cwc + j * 4 + i + 1] for i in range(4)]
                        DMA("sp", [], ["cin"], out=cin[:], in_=cstate_d[l, :, j * 128:(j + 1) * 128])
                        E("pe", "transpose", ["cin", "identf"], ["ps7"], out=ps[7][:, 256:304], in_=cin[:], identity=identf[0:48, 0:48])
                        E("dve", "tensor_copy", ["ps7"], ["cs"], out=cs[:].rearrange("p s i -> p (s i)"), in_=ps[7][:, 256:304])
                        geo = []
                        for hf in range(2):
                            a0 = hf * 1024
                            ln = 1024
                            acc = accb[hf]
                            atg = f"acc{hf}"
                            E("dve", "tensor_scalar", ["hpre", "sm"], [atg], out=acc[:, 0:ln], in0=hpre[:, a0:a0 + ln], scalar1=w_[3], scalar2=None, op0=ALU.mult)
                            for sh in (1, 2, 3):
                                lo = sh if hf == 0 else 0
                                E("dve", "scalar_tensor_tensor", ["hpre", "sm", atg], [atg], out=acc[:, lo:ln], in0=hpre[:, a0 + lo - sh:a0 + ln - sh],
                                  scalar=w_[3 - sh], in1=acc[:, lo:ln], op0=ALU.mult, op1=ALU.add)
                            tot = ln
                            blocks = TBS[0:2] if hf == 0 else TBS[2:5]
                            if hf == 1:
                                E("dve", "tensor_copy", ["cs"], ["fulls"], out=fulls[:, :, 0:3], in_=cs[:])
                                E("dve", "tensor_copy", ["hpre"], ["fulls"], out=fulls[:, :, 3:7], in_=hpre[:, NPR:T].rearrange("p (t s) -> p s t", t=4))
                                E("dve", "tensor_scalar", ["fulls", "sm"], ["accs"], out=accs[:], in0=fulls[:, :, 3:7], scalar1=w_[3], scalar2=None, op0=ALU.mult)
                                for i in (0, 1):
                                    E("dve", "scalar_tensor_tensor", ["fulls", "sm", "accs"], ["accs"], out=accs[:], in0=fulls[:, :, i:i + 4], scalar=w_[i],
                                      in1=accs[:], op0=ALU.mult, op1=ALU.add)
                                E("dve", "scalar_tensor_tensor", ["fulls", "sm", "accs"], [atg], out=acc[:, 1024:1088].rearrange("p (t s) -> p s t", t=4),
                                  in0=fulls[:, :, 2:6], scalar=w_[2], in1=accs[:], op0=ALU.mult, op1=ALU.add)
                                tot = 1088
                            b0 = 2 if hf == 0 else 4
                            geo.append((hf, a0, tot, blocks, acc, atg, sqbs[hf], f"sqb{hf}", b0))
                        if j >= 8:
                            for (hf, a0, tot, blocks, acc, atg, sqb, stg, b0) in geo:
                                E("act", "activation", [atg], [f"qkv{j}"], out=qkv[:, j, a0:a0 + tot], in_=acc[:, 0:tot], func=AF.Silu)
                            continue
                        for (hf, a0, tot, blocks, acc, atg, sqb, stg, b0) in geo:
                            E("act", "activation", [atg], [atg], out=acc[:, 0:tot], in_=acc[:, 0:tot], func=AF.Silu)
                        for (hf, a0, tot, blocks, acc, atg, sqb, stg, b0) in geo:
                            E("act", "activation", [atg], [stg], out=sqb[:, 0:tot], in_=acc[:, 0:tot], func=AF.Square)
                        SSs = []
                        for (hf, a0, tot, blocks, acc, atg, sqb, stg, b0) in geo:
                            SS = pall[:, b0 * 512:b0 * 512 + tot]
                            sstags = [f"ps{b0 + q}" for q in range((tot + 511) // 512)]
                            SSs.append((SS, sstags))
                            for (t0, tn) in blocks:
                                o_ = t0 - a0
                                E("pe", "matmul", [stg, "ones"], [f"ps{b0 + o_ // 512}"], out=SS[:, o_:o_ + tn], lhsT=ones_bf[:], rhs=sqb[:, o_:o_ + tn], start=True, stop=True)
                        for (SS, sstags) in SSs:
                            E("act", "activation", sstags + ["epsb"], sstags, out=SS, in_=SS, func=AF.Ln, bias=epsb[:])
                        for (SS, sstags) in SSs:
                            E("act", "activation", sstags, sstags, out=SS, in_=SS, func=AF.Exp, scale=-0.5)
                        for (hf, a0, tot, blocks, acc, atg, sqb, stg, b0), (SS, sstags) in zip(geo, SSs):
                            E("dve", "scalar_tensor_tensor", [atg] + sstags, [f"qkv{j}"], out=qkv[:, j, a0:a0 + tot], in0=acc[:, 0:tot],
                              scalar=(128 ** -0.5 if j < 4 else 1.0), in1=SS, op0=ALU.mult, op1=ALU.mult)
                P.flush("gdn_p1")
            if dbg == "gdn_p1":
                return

            M = AR.view(0, [128, 4096], F32)
            M3 = M.rearrange("p (r c) -> p r c", c=64)
            tmp = AR.view(16384, [128, 1024], F32)
            AR.top = 20480
            with contextlib.ExitStack() as ph:
                def al(shape, dt=F32):
                    return ph.enter_context(AR.alloc(list(shape), dt))
                dgs = [al([64, 8, 64]) for _ in range(2)]; t1s = [al([64, 8, 64]) for _ in range(2)]
                A_sb = [al([64, 8, 64]) for _ in range(2)]
                assert AR.top <= XN_BYTES
                groups = [(h, gq) for h in range(4) for gq in range(5)]

                def geo_(ai):
                    h, gq = groups[ai]
                    n0, nn = (gq * 8, 8) if gq < 4 else (32, 1)
                    return h, gq, n0, nn, ai % 2

                def stage_x(ai):
                    h, gq, n0, nn, par = geo_(ai)
                    c0 = n0 * 64
                    dg, t1 = dgs[par], t1s[par]
                    gcb, gct = ps[par], f"ps{par}"
                    kkb, kkt = ps[2 + par], f"ps{2 + par}"
                    mp1 = cst[0:64, C_MP1:C_MP1 + 64] if gq < 4 else cst[0:64, C_MP1S:C_MP1S + 64]
                    E("dve", "tensor_tensor", ["identf", "gctm"], [f"dg{par}"], out=dg[:, 0:nn, :], in0=bc(id64, 1, [64, nn, 64]),
                      in1=bc(gctm[:, n0:n0 + nn, h], 2, [64, nn, 64]), op=ALU.mult)
                    E("pe", "matmul", [f"dg{par}", "onesf"], [gct], out=gcb[0:64, 0:nn * 64], lhsT=ones_f[0:64, 0:64],
                      rhs=dg[:, 0:nn, :].rearrange("p n j -> p (n j)"), start=True, stop=True)
                    for q in range(nn):
                        cc = c0 + q * 64
                        E("pe", "matmul", ["qkv"], [kkt], out=kkb[0:64, q * 64:(q + 1) * 64], lhsT=qkv[:, 4 + h, cc:cc + 64],
                          rhs=qkv[:, 4 + h, cc:cc + 64], start=True, stop=True)
                    g3 = gcb[0:64, 0:nn * 64].rearrange("p (n j) -> p n j", j=64)
                    E("dve", "tensor_tensor", [gct, "cst"], [f"t1{par}"], out=t1[:, 0:nn, :], in0=g3, in1=bc(mp1, 1, [64, nn, 64]), op=ALU.add)
                    E("dve", "tensor_tensor", [f"t1{par}", "gctm"], [f"t1{par}"], out=t1[:, 0:nn, :], in0=t1[:, 0:nn, :],
                      in1=bc(gctm[:, n0:n0 + nn, h], 2, [64, nn, 64]), op=ALU.subtract)
                    E("act", "activation", [f"t1{par}"], [f"t1{par}"], out=t1[:, 0:nn, :], in_=t1[:, 0:nn, :], func=AF.Exp, scale=-1.0)

                def stage_y(ai):
                    h, gq, n0, nn, par = geo_(ai)
                    t1 = t1s[par]
                    asb = A_sb[par]; atag = f"A_sb{par}"
                    kkb, kkt = ps[2 + par], f"ps{2 + par}"
                    k3 = kkb[0:64, 0:nn * 64].rearrange("p (n j) -> p n j", j=64)
                    E("dve", "tensor_tensor", [kkt, f"t1{par}"], [atag], out=asb[:, 0:nn, :], in0=k3, in1=t1[:, 0:nn, :], op=ALU.mult)
                    E("dve", "tensor_tensor", [atag, "btm"], [atag], out=asb[:, 0:nn, :], in0=asb[:, 0:nn, :],
                      in1=bc(btm[:, n0:n0 + nn, h], 2, [64, nn, 64]), op=ALU.mult)
                    p0 = h * 32 + n0 if gq < 4 else 128 + h
                    DMA("sp", [atag], ["Ad"], out=Ad[p0:p0 + nn, :].rearrange("n (i j) -> i n j", i=64), in_=asb[:, 0:nn, :])

                stage_x(0)
                for ai in range(len(groups)):
                    if ai + 1 < len(groups):
                        stage_x(ai + 1)
                    stage_y(ai)
                P.flush("gdn_A")
            DMA("sp", ["Ad"], ["M0"], out=M[:, :], in_=Ad[0:128, :])
            E("dve", "memset", [], ["M0"], ap=M[:, ::65], constant=1.0)
            for j in range(63):
                nr, ni = j + 1, 63 - j
                tv = tmp[:, 0:nr * ni].rearrange("p (r i) -> p r i", i=ni)
                E("dve", "tensor_tensor", ["M0"], ["tmp0"], out=tv, in0=bc(M3[:, 0:nr, j], 2, [128, nr, ni]), in1=bc(M3[:, j + 1:64, j], 1, [128, nr, ni]),
                  op=ALU.mult)
                E("dve", "tensor_tensor", ["M0", "tmp0"], ["M0"], out=M3[:, 0:nr, j + 1:64], in0=M3[:, 0:nr, j + 1:64], in1=tv, op=ALU.subtract)
            DMA("sp", ["M0"], ["Ud"], out=Ud[0:128, :], in_=M[:, :])
            DMA("sp", ["Ad", "M0"], ["M0"], out=M[0:4, :], in_=Ad[128:132, :])
            E("dve", "memset", [], ["M0"], ap=M[0:4, ::65], constant=1.0)
            M5 = M[0:4, :].rearrange("p (r c) -> p r c", c=64)
            for j in range(3):
                for r_ in range(j + 1):
                    pass
            Mf = M[0:4, :]
            def dsl(rt, ct):
                o0 = (rt * 16) * 64 + ct * 16
                return Mf[:, o0:o0 + 15 * 65 + 1:65]
            tmp4 = tmp[0:4, 0:16]
            for j in range(3):
                for i in range(j + 1, 4):
                    for r_ in range(j + 1):
                        E("dve", "tensor_tensor", ["M0"], ["tmp0"], out=tmp4, in0=dsl(r_, j), in1=dsl(i, j), op=ALU.mult)
                        E("dve", "tensor_tensor", ["M0", "tmp0"], ["M0"], out=dsl(r_, i), in0=dsl(r_, i), in1=tmp4, op=ALU.subtract)
            DMA("sp", ["M0"], ["Ud"], out=Ud[128:132, :], in_=M[0:4, :])
            P.flush("gdn_solve")

            AR.top = TMP0
            with contextlib.ExitStack() as ph:
                def al(shape, dt=F32):
                    return ph.enter_context(AR.alloc(list(shape), dt))
                Uc = [al([64, 4, 64], BF16) for _ in range(3)]
                wog = al([128, 4, D], BF16)
                wv1 = al([64, 4, 128])
                wv = [wv1, wv1]
                u_sb = al([64, 4, 128], BF16)
                Sf = al([128, 4, 128]); Sb = al([128, 4, 128], BF16)
                obufs = [al([128, 4, 256]) for _ in range(2)]; go = al([128, 4, 256], BF16)
                sqh = al([128, 256], BF16); rsh = al([128, 256])
                assert AR.top <= GP, AR.top
                top2 = AR.top
                AR.top = 0
                kbg = [al([64, 4, 128], BF16) for _ in range(2)]
                kdec = [al([64, 4, 128], BF16) for _ in range(2)]
                vb = [al([64, 4, 128], BF16) for _ in range(2)]
                qkT = [al([64, 4, 64], BF16) for _ in range(2)]
                t2 = al([64, 4, 64]); dg2 = al([64, 4, 64]); egb = al([128, 4, 64])
                kcT = [al([128, 4, 64], BF16) for _ in range(2)]
                qdT = [al([128, 4, 64], BF16) for _ in range(2)]
                Sfs = al([128, 16, 128]); Sbs = al([128, 16, 128], BF16)
                kdm = al([64, 16, 128], BF16); kcm = al([128, 16, 64], BF16); qdm = al([128, 16, 64], BF16)
                assert AR.top <= XN_BYTES, AR.top

                UdP = Ud[0:128, :].rearrange("(h n) (r i) -> n r h i", h=4, i=64)
                UdS = Ud[128:132, :].rearrange("h (r i) -> r h i", i=64)
                DMA("pool", [], ["wog"], out=wog[:], in_=w_out[l, 0:512, :].rearrange("(c p) f -> p c f", p=128))
                E("dve", "memset", [], ["Sf"], ap=Sf[:], constant=0.0)
                E("dve", "memset", [], ["Sb"], ap=Sb[:], constant=0.0)

                TP = ps[0].bitcast(BF16)[0:64, :].rearrange("p (i d) -> p i d", i=8)
                QK = ps[1][0:64, 0:256].rearrange("p (h i) -> p h i", h=4)
                GC2 = ps[1][0:64, 256:512].rearrange("p (h i) -> p h i", h=4)
                WV = ps[2][0:64, :].rearrange("p (h d) -> p h d", h=4)
                KC = ps[3][:, 0:256].rearrange("p (h i) -> p h i", h=4)
                EGB = ps[3][:, 256:512].rearrange("p (h i) -> p h i", h=4)
                UP = ps[4][0:64, :].rearrange("p (h d) -> p h d", h=4)
                OP = ps[5][:, 0:256].rearrange("p (h i) -> p h i", h=4)
                SP = ps[6].rearrange("p (h d) -> p h d", h=4)
                gnw = sm[:, SM_GNW + l:SM_GNW + l + 1]

                def prep1(n, rb):
                    c0 = n * 64
                    samp = n == 32
                    ub = n % 3
                    Ub = Uc[ub]
                    DMA("pool", ["Ud"], [f"Uc{ub}"], out=Ub[:], in_=(UdS if samp else UdP[n]))
                    E("dve", "tensor_tensor", [f"Uc{ub}", "triu"], [f"Uc{ub}"], out=Ub[:], in0=Ub[:], in1=bc(triu_bf[:], 1, [64, 4, 64]), op=ALU.mult)
                    E("dve", "tensor_tensor", ["identf", "gctm"], ["dg2"], out=dg2[:], in0=bc(id64, 1, [64, 4, 64]), in1=bc(gctm[:, n, :], 2, [64, 4, 64]), op=ALU.mult)
                    for h in range(4):
                        E("pe", "transpose", ["qkv", "identb"], ["ps0"], out=TP[:, h, :], in_=qkv[:, 4 + h, c0:c0 + 64], identity=identb[:])
                        E("pe", "transpose", ["qkv", "identb"], ["ps0"], out=TP[:, 4 + h, :], in_=qkv[:, 8 + h, c0:c0 + 64], identity=identb[:])
                    E("dve", "tensor_tensor", ["ps0", "bk"], [f"kbg{rb}"], out=kbg[rb][:], in0=TP[:, 0:4, :], in1=bc(bk[:, n, :], 2, [64, 4, 128]), op=ALU.mult)
                    E("dve", "tensor_tensor", ["ps0", "btm"], [f"vb{rb}"], out=vb[rb][:], in0=TP[:, 4:8, :], in1=bc(btm[:, n, :], 2, [64, 4, 128]), op=ALU.mult)
                    E("dve", "tensor_tensor", ["ps0", "kdc"], [f"kdec{rb}"], out=kdec[rb][:], in0=TP[:, 0:4, :], in1=bc(kdc[:, n, :], 2, [64, 4, 128]), op=ALU.mult)
                    for h in range(4):
                        E("pe", "matmul", ["qkv"], ["ps1"], out=QK[:, h, :], lhsT=qkv[:, 4 + h, c0:c0 + 64], rhs=qkv[:, h, c0:c0 + 64], start=True, stop=True)
                    E("pe", "matmul", ["dg2", "onesf"], ["ps1"], out=ps[1][0:64, 256:512], lhsT=ones_f[0:64, 0:64], rhs=dg2[:].rearrange("p h i -> p (h i)"),
                      start=True, stop=True)
                    E("pe", "matmul", ["dg2", "onesf"], ["ps3"], out=ps[3][:, 256:512], lhsT=ones_f[0:64, :], rhs=dg2[:].rearrange("p h i -> p (h i)"),
                      start=True, stop=True)

                def prep2(n, rb):
                    c0 = n * 64
                    samp = n == 32
                    ub = n % 3
                    Ub = Uc[ub]
                    mp2 = cst[0:64, C_MP2S:C_MP2S + 64] if samp else cst[0:64, C_MP2:C_MP2 + 64]
                    for h in range(4):
                        E("pe", "matmul", [f"Uc{ub}", f"vb{rb}"], ["ps2"], out=WV[:, h, :], lhsT=Ub[:, h, :], rhs=vb[rb][:, h, :], start=True, stop=True)
                    for h in range(4):
                        E("pe", "matmul", [f"Uc{ub}", f"kbg{rb}"], ["ps3"], out=KC[:, h, :], lhsT=kbg[rb][:, h, :], rhs=Ub[:, h, :], start=True, stop=True)
                    E("act", "activation", ["ps3"], ["egb"], out=egb[:], in_=EGB, func=AF.Exp)
                    E("act", "copy", ["ps3"], [f"kcT{rb}"], out=kcT[rb][:], in_=KC)
                    E("act", "copy", ["ps2"], ["wv"], out=wv[rb][:], in_=WV)
                    E("dve", "tensor_tensor", ["ps1", "cst"], ["t2"], out=t2[:], in0=GC2, in1=bc(mp2, 1, [64, 4, 64]), op=ALU.subtract)
                    E("dve", "tensor_tensor", ["t2", "gctm"], ["t2"], out=t2[:], in0=t2[:], in1=bc(gctm[:, n, :], 2, [64, 4, 64]), op=ALU.subtract)
                    E("act", "activation", ["t2"], ["t2"], out=t2[:], in_=t2[:], func=AF.Exp)
                    E("dve", "tensor_tensor", ["qkv", "egb"], [f"qdT{rb}"], out=qdT[rb][:], in0=qkv[:, 0:4, c0:c0 + 64], in1=egb[:], op=ALU.mult)
                    E("dve", "tensor_tensor", ["ps1", "t2"], [f"qkT{rb}"], out=qkT[rb][:], in0=QK, in1=t2[:], op=ALU.mult)

                def epilogue_gen(c0, ncol, ob):
                    obuf = obufs[ob]
                    otag = f"obuf{ob}"
                    for h in range(4):
                        E("act", "activation", [otag], ["sqh"], out=sqh[:, 0:ncol], in_=obuf[:, h, 0:ncol], func=AF.Square)
                        yield
                        E("pe", "matmul", ["sqh", "ones"], ["ps7"], out=ps[7][:, 0:ncol], lhsT=ones_bf[:], rhs=sqh[:, 0:ncol], start=True, stop=True)
                        yield
                        E("act", "activation", ["ps7", "epsb"], ["rsh"], out=rsh[:, 0:ncol], in_=ps[7][:, 0:ncol], func=AF.Ln, scale=1.0 / 128, bias=epsb[:])
                        yield
                        E("act", "activation", ["rsh"], ["rsh"], out=rsh[:, 0:ncol], in_=rsh[:, 0:ncol], func=AF.Exp, scale=-0.5)
                        yield
                        E("dve", "scalar_tensor_tensor", [otag, "sm", "rsh"], [otag], out=obuf[:, h, 0:ncol], in0=obuf[:, h, 0:ncol], scalar=gnw,
                          in1=rsh[:, 0:ncol], op0=ALU.mult, op1=ALU.mult)
                        yield
                        E("dve", "tensor_tensor", [otag, "zs"], ["go"], out=go[:, h, 0:ncol], in0=obuf[:, h, 0:ncol], in1=zs[:, h, c0:c0 + ncol], op=ALU.mult)
                        yield
                    for qr in range(4):
                        for o2 in range(2):
                            o = qr * 2 + o2
                            for h in range(4):
                                E("pe", "matmul", ["wog", "go"], ["ps7"], out=ps[7][:, o2 * 256:o2 * 256 + ncol], lhsT=wog[:, h, o * 128:(o + 1) * 128],
                                  rhs=go[:, h, 0:ncol], start=(h == 0), stop=(h == 3))
                        yield
                        E("dve", "tensor_tensor", ["ps7", xtag(c0)], [xtag(c0)], out=xT[:, qr * 2:qr * 2 + 2, c0:c0 + ncol],
                          in0=ps[7].rearrange("p (o t) -> p o t", o=2)[:, :, 0:ncol], in1=xT[:, qr * 2:qr * 2 + 2, c0:c0 + ncol], op=ALU.add)
                        yield

                pend = [None]

                def step(k):
                    for _ in range(k):
                        if pend[0] is None:
                            return
                        try:
                            next(pend[0])
                        except StopIteration:
                            pend[0] = None

                def epilogue(c0, ncol, ob):
                    step(10 ** 6)
                    pend[0] = epilogue_gen(c0, ncol, ob)
                    step(10 ** 6)

                prep1(0, 0)
                prep2(0, 0)
                for n in range(32):
                    rb = n % 2
                    prep1(n + 1, (n + 1) % 2)
                    step(3)
                    for h in range(4):
                        E("pe", "matmul", [f"kcT{rb}", "Sb"], ["ps4"], out=UP[:, h, :], lhsT=kcT[rb][:, h, :], rhs=Sb[:, h, :], start=True, stop=True)
                    E("dve", "tensor_tensor", ["wv", "ps4"], ["u_sb"], out=u_sb[:], in0=wv[rb][:], in1=UP, op=ALU.subtract)
                    prep2(n + 1, (n + 1) % 2)
                    step(3)
                    for h in range(4):
                        E("pe", "matmul", [f"kdec{rb}", "u_sb"], ["ps6"], out=SP[:, h, :], lhsT=kdec[rb][:, h, :], rhs=u_sb[:, h, :], start=True, stop=True)
                    for h in range(4):
                        E("pe", "matmul", ["Sb", f"qdT{rb}"], ["ps5"], out=OP[:, h, :], lhsT=Sb[:, h, :], rhs=qdT[rb][:, h, :], start=True, stop=False)
                        E("pe", "matmul", ["u_sb", f"qkT{rb}"], ["ps5"], out=OP[:, h, :], lhsT=u_sb[:, h, :], rhs=qkT[rb][:, h, :], start=False, stop=True)
                    for h in range(4):
                        E("dve", "scalar_tensor_tensor", ["Sf", "cdec", "ps6"], ["Sf"], out=Sf[:, h, :], in0=Sf[:, h, :], scalar=cdec[:, n, h:h + 1],
                          in1=SP[:, h, :], op0=ALU.mult, op1=ALU.add)
                    ob = (n // 4) % 2
                    E("act", "copy", ["ps5"], [f"obuf{ob}"], out=obufs[ob][:, :, (n % 4) * 64:(n % 4) * 64 + 64], in_=OP)
                    E("act", "copy", ["Sf"], ["Sb"], out=Sb[:], in_=Sf[:])
                    step(3)
                    if n % 4 == 3:
                        step(10 ** 6)
                        pend[0] = epilogue_gen((n - 3) * 64, 256, ob)
                step(10 ** 6)
                DMA("sp", ["Sf"], [], out=ogp[l].rearrange("h k v -> k h v"), in_=Sf[:])

                for h in range(4):
                    DMA("sp", [], ["Sfs"], out=Sfs[:], in_=sgd[l, :, h].rearrange("s k v -> k s v"))
                    DMA("pool", [], ["Sbs"], out=Sbs[:], in_=sgd[l, :, h].rearrange("s k v -> k s v"))
                    E("dve", "tensor_tensor", ["kcT0", "maskS"], ["kcm"], out=kcm[:], in0=bc(kcT[0][:, h, :], 1, [128, 16, 64]), in1=maskS[:], op=ALU.mult)
                    E("dve", "tensor_tensor", ["qdT0", "maskS"], ["qdm"], out=qdm[:], in0=bc(qdT[0][:, h, :], 1, [128, 16, 64]), in1=maskS[:], op=ALU.mult)
                    E("dve", "tensor_tensor", ["kdec0", "cst"], ["kdm"], out=kdm[:], in0=bc(kdec[0][:, h, :], 1, [64, 16, 128]),
                      in1=bc(cst[0:64, C_OHS:C_OHS + 16], 2, [64, 16, 128]), op=ALU.mult)
                    for s_ in range(16):
                        E("pe", "matmul", ["kcm", "Sbs"], ["ps4"], out=UP[:, h, :], lhsT=kcm[:, s_, :], rhs=Sbs[:, s_, :], start=(s_ == 0), stop=(s_ == 15))
                    E("dve", "tensor_tensor", ["wv", "ps4"], ["u_sb"], out=u_sb[:, h, :], in0=wv[0][:, h, :], in1=UP[:, h, :], op=ALU.subtract)
                    for s_ in range(16):
                        E("pe", "matmul", ["qdm", "Sbs"], ["ps5"], out=OP[:, h, :], lhsT=Sbs[:, s_, :], rhs=qdm[:, s_, :], start=(s_ == 0), stop=False)
                    E("pe", "matmul", ["u_sb", "qkT0"], ["ps5"], out=OP[:, h, :], lhsT=u_sb[:, h, :], rhs=qkT[0][:, h, :], start=False, stop=True)
                    E("act", "copy", ["ps5"], ["obuf0"], out=obufs[0][:, h, 0:64], in_=OP[:, h, :])
                    for s4 in range(4):
                        for sl in range(4):
                            s_ = s4 * 4 + sl
                            E("pe", "matmul", ["kdm", "u_sb"], ["ps6"], out=SP[:, sl, :], lhsT=kdm[:, s_, :], rhs=u_sb[:, h, :], start=True, stop=True)
                        for sl in range(4):
                            s_ = s4 * 4 + sl
                            E("dve", "scalar_tensor_tensor", ["Sfs", "cdecs", "ps6"], ["Sfs"], out=Sfs[:, s_, :], in0=Sfs[:, s_, :],
                              scalar=cdecs[:, s_, h:h + 1], in1=SP[:, sl, :], op0=ALU.mult, op1=ALU.add)
                    DMA("sp", ["Sfs"], [], out=ogs[l, :, h].rearrange("s k v -> k s v"), in_=Sfs[:])
                epilogue(NPR, 64, 0)
                P.flush("gdn_scan")
            AR.top = XN_BYTES

        def final_out():
            AR.top = 0
            with contextlib.ExitStack() as ph:
                def al(shape, dt=F32):
                    return ph.enter_context(AR.alloc(list(shape), dt))
                sq = [al([128, 8, 512], BF16) for _ in range(2)]
                rstd = [al([128, 512]) for _ in range(2)]
                yfm = [al([128, 8, 512]) for _ in range(2)]
                ytm = [al([128, D]) for _ in range(2)]
                ti = 0
                for bi, (t0, tn) in enumerate(TBS):
                    b = bi % 2
                    E("act", "activation", [xtag(t0)], [f"sq{b}"], out=sq[b][:, :, 0:tn], in_=xT[:, :, t0:t0 + tn], func=AF.Square)
                    for c in range(8):
                        E("pe", "matmul", [f"sq{b}", "ones"], [f"ps{b}"], out=ps[b][:, 0:tn], lhsT=ones_bf[:], rhs=sq[b][:, c, 0:tn],
                          start=(c == 0), stop=(c == 7))
                    E("act", "activation", [f"ps{b}", "epsb"], [f"rstd{b}"], out=rstd[b][:, 0:tn], in_=ps[b][:, 0:tn], func=AF.Ln, scale=1.0 / D, bias=epsb[:])
                    E("act", "activation", [f"rstd{b}"], [f"rstd{b}"], out=rstd[b][:, 0:tn], in_=rstd[b][:, 0:tn], func=AF.Exp, scale=-0.5)
                    for c in range(8):
                        E("dve", "scalar_tensor_tensor", [xtag(t0), "sm", f"rstd{b}"], [f"yfm{b}"], out=yfm[b][:, c, 0:tn], in0=xT[:, c, t0:t0 + tn],
                          scalar=sm[:, SM_NF + c:SM_NF + c + 1], in1=rstd[b][:, 0:tn], op0=ALU.mult, op1=ALU.mult)
                    for q in range((tn + 127) // 128):
                        rows = min(128, tn - q * 128)
                        yb = ti % 2
                        ti += 1
                        for c in range(8):
                            bank = 2 + c // 4
                            E("pe", "transpose", [f"yfm{b}", "identf"], [f"ps{bank}"], out=ps[bank][0:rows, (c % 4) * 128:(c % 4 + 1) * 128],
                              in_=yfm[b][:, c, q * 128:q * 128 + rows], identity=identf[:])
                        E("dve", "tensor_copy", ["ps2", "ps3"], [f"ytm{yb}"], out=ytm[yb][0:rows, :], in_=pall[0:rows, 1024:2048])
                        if t0 < NPR:
                            r0 = t0 + q * 128
                            DMA("sp", [f"ytm{yb}"], [], out=yp[r0:r0 + 128, :], in_=ytm[yb][:, :])
                        else:
                            for t_ in range(4):
                                DMA("sp", [f"ytm{yb}"], [], out=ys[:, t_, :], in_=ytm[yb][t_ * 16:(t_ + 1) * 16, :])
                P.flush("final")

        for l in range(nlayers):
            ffn(l, w1g, w1u, w1d, SM_N1 + 8 * l)
            if dbg == "ffn1" and l == nlayers - 1:
                break
            swa(l)
            if dbg in ("swa", "swa_p") and l == nlayers - 1:
                break
            gdn(l)
            if dbg in ("mix", "gdn_p1") and l == nlayers - 1:
                break
            ffn(l, w2g, w2u, w2d, SM_N2 + 8 * l)

        if dbg:
            P.op("sp", lambda e: e.dma_start(out=dbgx, in_=xT[:]), r=[f"xT{i}" for i in range(5)], dma=True)
            P.flush("dbg")

        if not dbg:
            final_out()
    return nc


SM_N1 = 0
SM_N2 = SM_N1 + 8 * DEPTH
SM_NM = SM_N2 + 8 * DEPTH
SM_NF = SM_NM + 8 * DEPTH
SM_SINK = SM_NF + 8
SM_SWN = SM_SINK + 8 * DEPTH
SM_SSINK = SM_SWN + 4 * DEPTH
SM_CONVW = SM_SSINK + 2 * DEPTH
SM_GNW = SM_CONVW + 48 * DEPTH
SM_ALOG = SM_GNW + DEPTH
SM_DTB = SM_ALOG + 4 * DEPTH
NSMALL = SM_DTB + 4 * DEPTH
C_MASK2 = 0
C_SMASK = 256
C_MP1, C_MP2, C_MP1S, C_MP2S, C_TRI, C_TRIS, C_SAMES, C_TRIU, C_OH48, C_OHS = 392, 456, 520, 584, 648, 712, 776, 840, 904, 920
NCONST = 936
NEG = -30000.0
BIGM = 30000.0


def make_consts():
    c = np.zeros((128, NCONST), np.float32)
    i = np.arange(128)[:, None]
    j = np.arange(128)[None, :]
    c[:, C_MASK2:C_MASK2 + 128] = np.where(j > i, 0.0, NEG)
    c[:, C_MASK2 + 128:C_MASK2 + 256] = np.where(j <= i, 0.0, NEG)
    t = (np.arange(16) % 4)[:, None]
    c[0:16, C_SMASK:C_SMASK + 128] = np.where(np.arange(128)[None, :] > t, 0.0, NEG)
    c[0:16, C_SMASK + 128:C_SMASK + 132] = np.where(np.arange(4)[None, :] <= t, 0.0, NEG)
    a = np.arange(64)[:, None]
    b = np.arange(64)[None, :]
    ta, sa, tb_, sb_ = a // 16, a % 16, b // 16, b % 16
    c[0:64, C_MP1:C_MP1 + 64] = np.where(a > b, 0.0, BIGM)
    c[0:64, C_MP2:C_MP2 + 64] = np.where(b >= a, 0.0, BIGM)
    c[0:64, C_MP1S:C_MP1S + 64] = np.where((sa == sb_) & (ta > tb_), 0.0, BIGM)
    c[0:64, C_MP2S:C_MP2S + 64] = np.where((sa == sb_) & (tb_ >= ta), 0.0, BIGM)
    c[0:64, C_TRI:C_TRI + 64] = (a <= b)
    c[0:64, C_TRIS:C_TRIS + 64] = (sa == sb_) & (ta <= tb_)
    c[0:64, C_SAMES:C_SAMES + 64] = (sa == sb_)
    c[0:64, C_TRIU:C_TRIU + 64] = (b >= a)
    c[0:64, C_OH48:C_OH48 + 16] = (a == 48 + np.arange(16)[None, :])
    c[0:64, C_OHS:C_OHS + 16] = (sa == np.arange(16)[None, :])
    return c


def make_maskS():
    m = np.zeros((128, 16, 64), np.float32)
    for s_ in range(16):
        m[:, s_, s_::16] = 1.0
    return m.astype(ml_dtypes.bfloat16)


def make_smalls(inp):
    s = np.zeros((128, NSMALL), np.float32)
    for l in range(DEPTH):
        s[:, SM_N1 + 8 * l:SM_N1 + 8 * l + 8] = inp["ffn1_norm"][l].reshape(8, 128).T
        s[:, SM_N2 + 8 * l:SM_N2 + 8 * l + 8] = inp["ffn2_norm"][l].reshape(8, 128).T
        s[:, SM_NM + 8 * l:SM_NM + 8 * l + 8] = inp["mix_norm"][l].reshape(8, 128).T
    s[:, SM_NF:SM_NF + 8] = inp["final_norm"].reshape(8, 128).T
    for l in range(DEPTH):
        for g in range(2):
            for sl_ in range(4):
                s[:, SM_SINK + 8 * l + 4 * g + sl_] = inp["swa_sinks"][l][4 * g + (sl_ % 2) * 2 + sl_ // 2]
        s[:, SM_SWN + 4 * l:SM_SWN + 4 * l + 4] = inp["swa_out_norm"][l].reshape(4, 128).T
        for g in range(2):
            s[0:16, SM_SSINK + 2 * l + g] = np.repeat(inp["swa_sinks"][l][g * 4:(g + 1) * 4], 4)
        cw = np.asarray(inp["gdn_conv_w"][l])
        s[:, SM_CONVW + 48 * l:SM_CONVW + 48 * l + 48] = cw.reshape(4, 12, 128).transpose(2, 1, 0).reshape(128, 48)
        s[:, SM_GNW + l] = inp["gdn_out_norm"][l]
        s[:, SM_ALOG + 4 * l:SM_ALOG + 4 * l + 4] = inp["gdn_a_log"][l][None, :]
        s[:, SM_DTB + 4 * l:SM_DTB + 4 * l + 4] = inp["gdn_dt_bias"][l][None, :]
    return s


def make_in_maps(inp, nl=DEPTH, ncores=8):
    f = lambda a: np.ascontiguousarray(np.asarray(a[:nl], dtype=np.float32))
    g = lambda a: np.ascontiguousarray(np.asarray(a, dtype=np.float32))
    smalls = make_smalls(inp)
    shared = {
        "smalls": smalls, "identf": np.eye(128, dtype=np.float32),
        "consts": make_consts(), "identb": np.eye(128).astype(ml_dtypes.bfloat16), "maskS": make_maskS(), "maskb": (8.0 * make_consts()[:, C_MASK2:C_MASK2 + 256]).astype(ml_dtypes.bfloat16),
        "w_in": f(inp["w_in"]), "w_out": f(inp["w_out"]),
        "w1g": f(inp["ffn1_w_gate"]), "w1u": f(inp["ffn1_w_up"]), "w1d": f(inp["ffn1_w_down"]),
        "w2g": f(inp["ffn2_w_gate"]), "w2u": f(inp["ffn2_w_up"]), "w2d": f(inp["ffn2_w_down"]),
    }
    maps = []
    for c in range(ncores):
        m = dict(shared)
        m["xp"] = g(inp["x_prompt"][c])
        m["cstate"] = g(inp["state_gdn_conv"][:nl, 16 * c:16 * c + 16].reshape(nl, 48, 1536))
        m["sgd"] = g(inp["state_gdn"][:nl, 16 * c:16 * c + 16])
        m["ck"] = g(inp["cache_swa_k"][:nl, 16 * c:16 * c + 16].reshape(nl, 16, 128, 128))
        m["cv"] = g(inp["cache_swa_v"][:nl, 16 * c:16 * c + 16].reshape(nl, 16, 128, 128))
        m["xs"] = g(inp["x_sample"][16 * c:16 * c + 16])
        maps.append(m)
    return maps


def kernel(**inputs):
    nc = build_nc()
    res = run_bass_kernel_spmd(nc, make_in_maps(inputs), core_ids=list(range(8)))
    R = res.results
    cat = lambda k, ax: np.concatenate([np.asarray(r[k]) for r in R], axis=ax)
    y_prompt = np.stack([np.asarray(r["yp"]) for r in R], 0)
    y_sample = cat("ys", 0)
    conv_p = np.stack([np.asarray(r["ocp"]) for r in R], 1)
    gdn_p = np.stack([np.asarray(r["ogp"]) for r in R], 1)
    k_p = np.stack([np.asarray(r["okp"]) for r in R], 1).reshape(DEPTH, 8, 128, 2, 64)
    v_p = np.stack([np.asarray(r["ovp"]) for r in R], 1).reshape(DEPTH, 8, 128, 2, 64)
    conv_s = cat("ocs", 1)
    gdn_s = cat("ogs", 1)
    k_s = cat("oks", 1).reshape(DEPTH, 128, 128, 2, 64)
    v_s = cat("ovs", 1).reshape(DEPTH, 128, 128, 2, 64)
    outs = (y_prompt, y_sample, conv_p, gdn_p, k_p, v_p, conv_s, gdn_s, k_s, v_s)
    return tuple(np.ascontiguousarray(o, dtype=np.float32) for o in outs)
```

```python
import contextlib
import os
SKIP = set(os.environ.get('KSKIP', '').split(','))
NOSELF = set(os.environ.get('KNOSELF', '').split(','))
import numpy as np
import ml_dtypes
import concourse.bass as bass
import concourse.mybir as mybir
from concourse.bass_utils import run_bass_kernel_spmd

F32 = mybir.dt.float32
BF16 = mybir.dt.bfloat16
AF = mybir.ActivationFunctionType
ALU = mybir.AluOpType
AX = mybir.AxisListType

D = 1024
DEPTH = 4
NPR = 2048
NSM = 64
T = NPR + NSM
DFF = 2816
NF = DFF // 128
INC = 2824
OFF_Z, OFF_B, OFF_A, OFF_SQ, OFF_SK, OFF_SV = 1536, 2048, 2052, 2056, 2568, 2696
EPS = 1e-6
TBS = [(0, 512), (512, 512), (1024, 512), (1536, 512), (2048, 64)]
FGROUPS = [(0, 4), (4, 4), (8, 4), (12, 4), (16, 4), (20, 2)]
ENGS = ["pe", "act", "dve", "pool", "sp"]
NDS = 12


ARENA = 134144


class Arena:
    def __init__(self, ap):
        self.ap = ap
        self.top = 0

    def view(self, off, shape, dt):
        esz = 4 if dt == F32 else 2
        n = esz
        for d_ in shape[1:]:
            n *= d_
        assert off % 4 == 0 and off + n <= ARENA, (off, n, shape)
        v = self.ap[0:shape[0], off // 2:(off + n) // 2]
        if dt == F32:
            v = v.bitcast(F32)
        if len(shape) == 3:
            v = v.rearrange("p (a b) -> p a b", a=shape[1])
        elif len(shape) == 4:
            v = v.rearrange("p (a b c) -> p a b c", a=shape[1], b=shape[2])
        return v

    @contextlib.contextmanager
    def alloc(self, shape, dt):
        off = (self.top + 63) // 64 * 64
        v = self.view(off, shape, dt)
        esz = 4 if dt == F32 else 2
        n = esz
        for d_ in shape[1:]:
            n *= d_
        old = self.top
        self.top = off + n
        try:
            yield v
        finally:
            self.top = old


class Op:
    __slots__ = ("fn", "deps", "dma", "sig", "count", "dsem", "dval", "gsem", "gval")

    def __init__(self, fn, deps, dma):
        self.fn = fn
        self.deps = deps
        self.dma = dma
        self.sig = False
        self.count = 0
        self.dsem = None
        self.dval = 0
        self.gval = 0


class Prog:
    def __init__(self, nc, sems, dsems):
        self.nc = nc
        self.sems = sems
        self.dsems = dsems
        self.cnt = {e: 0 for e in ENGS}
        self.dcnt = {e: 0 for e in ENGS}
        self.waited = {e: {} for e in ENGS}
        self.reset()

    def reset(self):
        self.ops = {e: [] for e in ENGS}
        self.last_w = {}
        self.readers = {}

    def op(self, eng, fn, r=(), w=(), dma=False):
        idx = len(self.ops[eng])
        w = list(w) + [t for t in r if t.startswith("ps")]
        r = [t for t in r if not t.startswith("ps")]
        deps = set()
        for t in r:
            lw = self.last_w.get(t)
            if lw is not None:
                deps.add(lw)
        for t in w:
            lw = self.last_w.get(t)
            if lw is not None:
                deps.add(lw)
            for rd in self.readers.get(t, ()):
                deps.add(rd)
        if eng == "pe" or (eng in NOSELF):
            deps = {d for d in deps if d[0] != eng}
        deps.discard((eng, idx))
        self.ops[eng].append(Op(fn, deps, dma))
        for t in w:
            self.last_w[t] = (eng, idx)
            self.readers[t] = []
        for t in r:
            self.readers.setdefault(t, []).append((eng, idx))

    def flush(self, name):
        for e in ENGS:
            for op in self.ops[e]:
                for (te, ti) in op.deps:
                    t = self.ops[te][ti]
                    if not t.dma:
                        t.sig = True
        for e in ENGS:
            for op in self.ops[e]:
                if op.dma:
                    i = self.dcnt[e]
                    self.dcnt[e] += 1
                    op.dsem = self.dsems[e][i % NDS]
                    op.dval = 16 * (i // NDS + 1)
                    op.gval = 16 * (i // NDS)
                elif op.sig:
                    self.cnt[e] += 1
                    op.count = self.cnt[e]
        with self.nc.Block() as blk:
            for e, bname in (("pe", "tensor"), ("act", "scalar"), ("dve", "vector"),
                             ("pool", "gpsimd"), ("sp", "sync")):
                getattr(blk, bname)(self._body(e))
        self.reset()

    def _body(self, e):
        ops = self.ops
        allops = self.ops

        def body(eng):
            waited = self.waited[e]

            def wait(sem, val):
                key = id(sem)
                if waited.get(key, 0) < val:
                    eng.wait_ge(sem, val)
                    waited[key] = val

            last_d = {}
            for op in ops[e]:
                for (te, ti) in sorted(op.deps):
                    t = allops[te][ti]
                    if t.dma:
                        wait(t.dsem, t.dval)
                    else:
                        wait(self.sems[te], t.count)
                if op.dma and op.gval > 0:
                    wait(op.dsem, op.gval)
                inst = op.fn(eng)
                if op.dma:
                    inst.then_inc(op.dsem, 16)
                    last_d[id(op.dsem)] = (op.dsem, op.dval)
                elif op.sig:
                    inst.then_inc(self.sems[e], 1)
            for (sem, val) in last_d.values():
                wait(sem, val)

        return body


def build_nc(nlayers=DEPTH, dbg=None):
    NL = nlayers
    nc = bass.Bass("TRN2", target_bir_lowering=False)
    dram = {}

    def din(name, shape, dt=F32):
        dram[name] = nc.dram_tensor(name, list(shape), dt, kind="ExternalInput").ap()
        return dram[name]

    def dout(name, shape, dt=F32):
        dram[name] = nc.dram_tensor(name, list(shape), dt, kind="ExternalOutput").ap()
        return dram[name]

    xp = din("xp", [NPR, D])
    xs = din("xs", [16, 4, D])
    smalls = din("smalls", [128, NSMALL])
    identf_d = din("identf", [128, 128])
    w1g = din("w1g", [NL, D, DFF])
    w1u = din("w1u", [NL, D, DFF])
    w1d = din("w1d", [NL, DFF, D])
    w2g = din("w2g", [NL, D, DFF])
    w2u = din("w2u", [NL, D, DFF])
    w2d = din("w2d", [NL, DFF, D])
    w_in = din("w_in", [NL, D, INC])
    w_out = din("w_out", [NL, D, D])
    consts_d = din("consts", [128, NCONST])
    identb_d = din("identb", [128, 128], BF16)
    ck_d = din("ck", [NL, 16, 128, 128])
    cv_d = din("cv", [NL, 16, 128, 128])
    cstate_d = din("cstate", [NL, 48, 1536])
    sgd = din("sgd", [NL, 16, 4, 128, 128])
    maskS_d = din("maskS", [128, 16, 64], BF16)
    maskb_d = din("maskb", [128, 256], BF16)
    Ad = nc.dram_tensor("Ad_scr", [132, 4096], F32, kind="Internal").ap()
    Ud = nc.dram_tensor("Ud_scr", [132, 4096], F32, kind="Internal").ap()
    ocp = dout("ocp", [NL, 3, 1536])
    ocs = dout("ocs", [NL, 16, 3, 1536])
    ogp = dout("ogp", [NL, 4, 128, 128])
    ogs = dout("ogs", [NL, 16, 4, 128, 128])
    okp = dout("okp", [NL, 128, 128])
    ovp = dout("ovp", [NL, 128, 128])
    oks = dout("oks", [NL, 16, 128, 128])
    ovs = dout("ovs", [NL, 16, 128, 128])
    yp = dout("yp", [NPR, D])
    ys = dout("ys", [16, 4, D])
    if dbg:
        dbgx = dout("dbgx", [128, 8, T])

    uid = [0]

    def SBT(name, shape, dt=F32):
        return AR.alloc(list(shape), dt)

    def SBR(name, shape, dt=F32):
        uid[0] += 1
        return nc.sbuf_tensor(f"{name}_u{uid[0]}", list(shape), dt)

    es = contextlib.ExitStack()
    with es:
        def sb(name, shape, dt=F32):
            return es.enter_context(SBR(name, list(shape), dt))

        sems = {e: es.enter_context(nc.semaphore("s_" + e)) for e in ENGS}
        dsems = {e: [es.enter_context(nc.semaphore(f"d_{e}{i}")) for i in range(NDS)]
                 for e in ("sp", "pool", "act")}
        P = Prog(nc, sems, dsems)
        pall = es.enter_context(nc.psum_tensor("pall", [128, 4096], F32))
        ps = [pall[:, i * 512:(i + 1) * 512] for i in range(8)]

        xT = sb("xT", [128, 8, T])
        arena_t = sb("arena", [128, ARENA // 2], BF16)
        AR = Arena(arena_t[:])
        XN_BYTES = 8 * T * 2
        xn = AR.view(0, [128, 8, T], BF16)
        AR.top = XN_BYTES
        sm = sb("sm", [128, NSMALL])
        identf = sb("identf_sb", [128, 128])
        ones_bf = sb("ones_bf", [128, 128], BF16)
        epsb = sb("epsb", [128, 1])
        cst = sb("cst", [128, NCONST])
        identb = sb("identb_sb", [128, 128], BF16)
        ones_f = sb("ones_f", [64, 128])
        e0sel = sb("e0sel", [64, 128])
        oneb = sb("oneb", [128, 1])
        triu_bf = sb("triu_bf", [64, 64], BF16)
        maskS = sb("maskS_sb", [128, 16, 64], BF16)
        maskb = sb("maskb_sb", [128, 256], BF16)

        P.op("sp", lambda e: e.dma_start(out=sm[:], in_=smalls), w=["sm"], dma=True)
        P.op("sp", lambda e: e.dma_start(out=identf[:], in_=identf_d), w=["identf"], dma=True)
        P.op("sp", lambda e: e.dma_start(out=cst[:], in_=consts_d), w=["cst"], dma=True)
        P.op("sp", lambda e: e.dma_start(out=identb[:], in_=identb_d), w=["identb"], dma=True)
        P.op("dve", lambda e: e.memset(ones_bf[:], 1.0), w=["ones"])
        P.op("dve", lambda e: e.memset(epsb[:], EPS), w=["epsb"])
        P.op("dve", lambda e: e.memset(ones_f[:], 1.0), w=["onesf"])
        P.op("dve", lambda e: e.memset(oneb[:], 1.0), w=["oneb"])
        P.op("dve", lambda e: e.memset(e0sel[:], 0.0), w=["e0sel"])
        P.op("dve", lambda e: e.memset(e0sel[0:1, :], 1.0), w=["e0sel"])
        P.op("dve", lambda e: e.tensor_copy(out=triu_bf[:], in_=cst[0:64, C_TRIU:C_TRIU + 64]), r=["cst"], w=["triu"])
        P.op("sp", lambda e: e.dma_start(out=maskS[:], in_=maskS_d), w=["maskS"], dma=True)
        P.op("sp", lambda e: e.dma_start(out=maskb[:], in_=maskb_d), w=["maskb"], dma=True)
        with contextlib.ExitStack() as ph:
            xin = [ph.enter_context(SBT(f"xin{i}", [128, D], F32)) for i in range(2)]
            for tt in range(17):
                b = tt % 2
                rows = 128 if tt < 16 else 64
                if tt < 16:
                    src = xp[tt * 128:(tt + 1) * 128, :]
                    P.op("sp", lambda e, b=b, rows=rows, src=src: e.dma_start(out=xin[b][0:rows, :], in_=src),
                         w=[f"xin{b}"], dma=True)
                else:
                    for t_ in range(4):
                        P.op("sp", lambda e, b=b, t_=t_: e.dma_start(out=xin[b][t_ * 16:(t_ + 1) * 16, :],
                                                                     in_=xs[:, t_, :]),
                             w=[f"xin{b}"], dma=True)
                for half in range(2):
                    pb = ps[(tt * 2 + half) % 4]
                    ptag = f"ps{(tt * 2 + half) % 4}"
                    for c4 in range(4):
                        c = half * 4 + c4
                        P.op("pe", lambda e, pb=pb, c4=c4, c=c, b=b, rows=rows: e.transpose(
                            out=pb[:, c4 * 128:c4 * 128 + rows], in_=xin[b][0:rows, c * 128:(c + 1) * 128],
                            identity=identf[0:rows, 0:rows]),
                            r=[f"xin{b}", "identf"], w=[ptag])
                    P.op("dve", lambda e, pb=pb, half=half, tt=tt, rows=rows: e.tensor_copy(
                        out=xT[:, half * 4:half * 4 + 4, tt * 128:tt * 128 + rows],
                        in_=pb.rearrange("p (c t) -> p c t", c=4)[:, :, 0:rows]),
                        r=[ptag], w=[f"xT{tt // 4}"])
            P.flush("init")

        def xtag(t0):
            return f"xT{t0 // 512}"

        def rmsnorm_ops(ph, wcol, pb=(6, 7)):
            sq = [ph.enter_context(SBT(f"sq{i}", [128, 8, 512], BF16)) for i in range(2)]
            rstd = [ph.enter_context(SBT(f"rstd{i}", [128, 512], F32)) for i in range(2)]

            def square(bi):
                t0, tn = TBS[bi]
                b = bi % 2
                E("act", "activation", [xtag(t0)], [f"nsq{b}"], out=sq[b][:, :, 0:tn], in_=xT[:, :, t0:t0 + tn], func=AF.Square)

            square(0)
            for bi, (t0, tn) in enumerate(TBS):
                b = bi % 2
                pk = pb[b]
                if bi + 1 < len(TBS):
                    square(bi + 1)
                for c in range(8):
                    E("pe", "matmul", [f"nsq{b}", "ones"], [f"ps{pk}"], out=ps[pk][:, 0:tn], lhsT=ones_bf[:], rhs=sq[b][:, c, 0:tn],
                      start=(c == 0), stop=(c == 7))
                E("act", "activation", [f"ps{pk}", "epsb"], [f"nrstd{b}"], out=rstd[b][:, 0:tn], in_=ps[pk][:, 0:tn], func=AF.Ln, scale=1.0 / D, bias=epsb[:])
                E("act", "activation", [f"nrstd{b}"], [f"nrstd{b}"], out=rstd[b][:, 0:tn], in_=rstd[b][:, 0:tn], func=AF.Exp, scale=-0.5)
                for c in range(8):
                    E("dve", "scalar_tensor_tensor", [xtag(t0), "sm", f"nrstd{b}"], [f"xn{bi}"], out=xn[:, c, t0:t0 + tn], in0=xT[:, c, t0:t0 + tn],
                      scalar=sm[:, wcol + c:wcol + c + 1], in1=rstd[b][:, 0:tn], op0=ALU.mult, op1=ALU.mult)

        def ffn(l, wg_d, wu_d, wd_d, ncol):
            with contextlib.ExitStack() as ph:
                wgu = [ph.enter_context(SBT(f"wgu{i}", [128, 2, 8, 512], BF16)) for i in range(2)]
                wdb = [ph.enter_context(SBT(f"wdb{i}", [128, 4, D], BF16)) for i in range(2)]
                hb = ph.enter_context(SBT("hb", [128, 4, T], BF16))
                sg = [ph.enter_context(SBT(f"sg{i}", [128, 512], F32)) for i in range(2)]
                rmsnorm_ops(ph, ncol)
                cnt = 0
                for gi, (f0, fn_) in enumerate(FGROUPS):
                    b = gi % 2
                    wcols = fn_ * 128
                    for fl in range(fn_):
                        for which, wsrc in ((0, wg_d), (1, wu_d)):
                            DMA("pool", [], [f"wgu{b}_{which}_{fl}"], out=wgu[b][:, which, :, fl * 128:(fl + 1) * 128],
                                in_=wsrc[l, :, (f0 + fl) * 128:(f0 + fl + 1) * 128].rearrange("(c p) f -> p c f", p=128))
                    for fl in range(fn_):
                        DMA("pool", [], [f"wdb{b}_{fl}"], out=wdb[b][:, fl, :], in_=wd_d[l, (f0 + fl) * 128:(f0 + fl + 1) * 128, :])
                    for bi, (t0, tn) in enumerate(TBS):
                        for fl in range(fn_):
                            pg, pu = ps[(cnt % 2) * 2], ps[(cnt % 2) * 2 + 1]
                            tg, tu = f"ps{(cnt % 2) * 2}", f"ps{(cnt % 2) * 2 + 1}"
                            sgi = cnt % 2
                            cnt += 1
                            for which, pt, tag in ((0, pg, tg), (1, pu, tu)):
                                for c in range(8):
                                    P.op("pe", lambda e, pt=pt, b=b, which=which, c=c, fl=fl, t0=t0, tn=tn: e.matmul(
                                        pt[:, 0:tn], lhsT=wgu[b][:, which, c, fl * 128:(fl + 1) * 128],
                                        rhs=xn[:, c, t0:t0 + tn], start=(c == 0), stop=(c == 7)),
                                        r=[f"wgu{b}_{which}_{fl}", f"xn{bi}"], w=[tag])
                            P.op("act", lambda e, pg=pg, sgi=sgi, tn=tn: e.activation(
                                out=sg[sgi][:, 0:tn], in_=pg[:, 0:tn], func=AF.Silu), r=[tg], w=[f"sg{sgi}"])
                            P.op("dve", lambda e, pu=pu, sgi=sgi, fl=fl, t0=t0, tn=tn: e.tensor_tensor(
                                out=hb[:, fl, t0:t0 + tn], in0=sg[sgi][:, 0:tn], in1=pu[:, 0:tn], op=ALU.mult),
                                r=[tu, f"sg{sgi}"], w=[f"hb{bi}"])
                    for bi, (t0, tn) in enumerate(TBS):
                        for o in range(8):
                            py, ty = ps[4 + (cnt % 4)], f"ps{4 + (cnt % 4)}"
                            cnt += 1
                            for fl in range(fn_):
                                P.op("pe", lambda e, py=py, b=b, fl=fl, o=o, t0=t0, tn=tn, fn_=fn_: e.matmul(
                                    py[:, 0:tn], lhsT=wdb[b][:, fl, o * 128:(o + 1) * 128],
                                    rhs=hb[:, fl, t0:t0 + tn], start=(fl == 0), stop=(fl == fn_ - 1)),
                                    r=[f"wdb{b}_{fl}", f"hb{bi}"], w=[ty])
                            P.op("dve", lambda e, py=py, o=o, t0=t0, tn=tn: e.scalar_tensor_tensor(
                                out=xT[:, o, t0:t0 + tn], in0=py[:, 0:tn], scalar=0.5, in1=xT[:, o, t0:t0 + tn],
                                op0=ALU.mult, op1=ALU.add), r=[ty, xtag(t0)], w=[xtag(t0)])
                P.flush("ffn")


        def swa(l):
            with contextlib.ExitStack() as ph:
                def pb_(name, shape, dt=F32):
                    return ph.enter_context(SBT(name, list(shape), dt))
                wsw = pb_("wsw", [128, 8, 768], BF16)
                wkd = pb_("wkd", [128, 8, 2, 128], BF16)
                wo = pb_("wo_s", [128, 4, D], BF16)
                kd = pb_("kd", [128, 2, T], BF16)
                vtm = pb_("vtm", [128, 17, 128], BF16)
                kvo = [pb_(f"kvo{i}", [128, 128]) for i in range(4)]
                so = pb_("so", [128, 4, 128])
                sq = pb_("sq_s", [128, 4, 128], BF16)
                rstd = pb_("rstd_s", [128, 128])
                mo = pb_("mo", [128, 4, 128], BF16)
                p2 = contextlib.ExitStack()
                p2.__enter__()
                def pb2(name, shape, dt=F32):
                    return p2.enter_context(SBT(name, list(shape), dt))
                qT = pb2("qT", [128, 4, T], BF16)
                sc = [pb2("sc", [128, 4, 256]) for _ in range(2)]
                mx = [pb2("mx", [128, 4]) for _ in range(2)]; mx2 = [pb2("mx2", [128, 4]) for _ in range(2)]
                rs = [pb2("rs", [128, 4]) for _ in range(2)]; es_ = [pb2("es_", [128, 4]) for _ in range(2)]
                pn = [pb2("pn", [128, 4, 256], BF16) for _ in range(2)]
                pTs = [pb2("pT", [128, 8, 128], BF16) for _ in range(2)]
                rmsnorm_ops(p2, SM_NM + 8 * l)
                P.op("pool", lambda e: e.dma_start(out=wsw[:], in_=w_in[l, :, OFF_SQ:OFF_SQ + 768].rearrange(
                    "(c p) f -> p c f", p=128)), w=["wsw"], dma=True)
                for g in range(2):
                    for hf in range(2):
                        P.op("pool", lambda e, g=g, hf=hf: e.dma_start(
                            out=wkd[:, :, g, hf * 64:(hf + 1) * 64],
                            in_=w_in[l, :, OFF_SK + g * 64:OFF_SK + (g + 1) * 64].rearrange("(c p) f -> p c f", p=128)),
                            w=["wkd"], dma=True)
                P.op("pool", lambda e: e.dma_start(out=wo[:], in_=w_out[l, 512:1024, :].rearrange(
                    "(c p) f -> p c f", p=128)), w=["wo"], dma=True)
                if "d2d" not in SKIP:
                    P.op("sp", lambda e: e.dma_start(out=oks[l, :, 0:124, :], in_=ck_d[l, :, 4:128, :]), dma=True)
                    P.op("sp", lambda e: e.dma_start(out=ovs[l, :, 0:124, :], in_=cv_d[l, :, 4:128, :]), dma=True)

                cnt = [0]
                def nb():
                    cnt[0] += 1
                    return cnt[0] % 4
                for j in range(4 if "projq" not in SKIP else 0):
                    for bi, (t0, tn) in enumerate(TBS):
                        k_ = nb()
                        for c in range(8):
                            P.op("pe", lambda e, k_=k_, c=c, j=j, t0=t0, tn=tn: e.matmul(
                                ps[k_][:, 0:tn], lhsT=wsw[:, c, j * 128:(j + 1) * 128], rhs=xn[:, c, t0:t0 + tn],
                                start=(c == 0), stop=(c == 7)), r=["wsw", f"xn{bi}"], w=[f"ps{k_}"])
                        P.op("act", lambda e, k_=k_, j=j, t0=t0, tn=tn: e.copy(out=qT[:, j, t0:t0 + tn], in_=ps[k_][:, 0:tn]),
                             r=[f"ps{k_}"], w=["qT"])
                for g in range(2 if "projk" not in SKIP else 0):
                    for bi, (t0, tn) in enumerate(TBS):
                        k_ = nb()
                        for c in range(8):
                            P.op("pe", lambda e, k_=k_, c=c, g=g, t0=t0, tn=tn: e.matmul(
                                ps[k_][:, 0:tn], lhsT=wkd[:, c, g, :], rhs=xn[:, c, t0:t0 + tn],
                                start=(c == 0), stop=(c == 7)), r=["wkd", f"xn{bi}"], w=[f"ps{k_}"])
                        P.op("act", lambda e, k_=k_, g=g, t0=t0, tn=tn: e.copy(out=kd[:, g, t0:t0 + tn], in_=ps[k_][:, 0:tn]),
                             r=[f"ps{k_}"], w=["kd"])
                for tt in range(17 if "projv" not in SKIP else 0):
                    rows = 128 if tt < 16 else 64
                    k_ = nb()
                    for c in range(8):
                        P.op("pe", lambda e, k_=k_, c=c, tt=tt, rows=rows: e.matmul(
                            ps[k_][0:rows, 0:256], lhsT=xn[:, c, tt * 128:tt * 128 + rows], rhs=wsw[:, c, 512:768],
                            start=(c == 0), stop=(c == 7)), r=["wsw", f"xn{tt // 4}"], w=[f"ps{k_}"])
                    P.op("act", lambda e, k_=k_, tt=tt, rows=rows: e.copy(out=vtm[0:rows, tt, :], in_=ps[k_][0:rows, 128:256]),
                         r=[f"ps{k_}"], w=["vtm"])
                    if tt >= 15:
                        i0 = (tt - 15) * 2
                        P.op("dve", lambda e, k_=k_, i0=i0, rows=rows: e.tensor_copy(out=kvo[i0][0:rows, :], in_=ps[k_][0:rows, 0:128]),
                             r=[f"ps{k_}"], w=[f"kvo{i0}"])
                        P.op("dve", lambda e, k_=k_, i0=i0, rows=rows: e.tensor_copy(out=kvo[i0 + 1][0:rows, :], in_=ps[k_][0:rows, 128:256]),
                             r=[f"ps{k_}"], w=[f"kvo{i0 + 1}"])
                if "kvo" not in SKIP:
                    P.op("sp", lambda e: e.dma_start(out=okp[l], in_=kvo[0][:]), r=["kvo0"], dma=True)
                    P.op("sp", lambda e: e.dma_start(out=ovp[l], in_=kvo[1][:]), r=["kvo1"], dma=True)
                for t_ in range(4 if "kvo" not in SKIP else 0):
                    P.op("sp", lambda e, t_=t_: e.dma_start(out=oks[l, :, 124 + t_, :], in_=kvo[2][t_ * 16:(t_ + 1) * 16, :]),
                         r=["kvo2"], dma=True)
                    P.op("sp", lambda e, t_=t_: e.dma_start(out=ovs[l, :, 124 + t_, :], in_=kvo[3][t_ * 16:(t_ + 1) * 16, :]),
                         r=["kvo3"], dma=True)

                def epilogue_gen(t0, n):
                    E("act", "activation", ["so"], ["sq_s"], out=sq[:, :, 0:n], in_=so[:, :, 0:n], func=AF.Square)
                    yield
                    for c in range(4):
                        E("pe", "matmul", ["sq_s", "ones"], ["ps3"], out=ps[3][:, 0:n], lhsT=ones_bf[:], rhs=sq[:, c, 0:n], start=(c == 0), stop=(c == 3))
                    yield
                    E("act", "activation", ["ps3", "epsb"], ["rstd_s"], out=rstd[:, 0:n], in_=ps[3][:, 0:n], func=AF.Ln, scale=1.0 / 512, bias=epsb[:])
                    yield
                    E("act", "activation", ["rstd_s"], ["rstd_s"], out=rstd[:, 0:n], in_=rstd[:, 0:n], func=AF.Exp, scale=-0.5)
                    yield
                    for c in range(4):
                        E("dve", "scalar_tensor_tensor", ["so", "sm", "rstd_s"], ["mo"], out=mo[:, c, 0:n], in0=so[:, c, 0:n],
                          scalar=sm[:, SM_SWN + 4 * l + c:SM_SWN + 4 * l + c + 1], in1=rstd[:, 0:n], op0=ALU.mult, op1=ALU.mult)
                        if c % 2 == 1:
                            yield
                    for half in range(2):
                        for o4 in range(4):
                            o = half * 4 + o4
                            for c in range(4):
                                E("pe", "matmul", ["wo", "mo"], ["ps7"], out=ps[7][:, o4 * 128:o4 * 128 + n], lhsT=wo[:, c, o * 128:(o + 1) * 128],
                                  rhs=mo[:, c, 0:n], start=(c == 0), stop=(c == 3))
                        yield
                        E("dve", "tensor_tensor", ["ps7", xtag(t0)], [xtag(t0)], out=xT[:, half * 4:half * 4 + 4, t0:t0 + n],
                          in0=ps[7].rearrange("p (o t) -> p o t", o=4)[:, :, 0:n], in1=xT[:, half * 4:half * 4 + 4, t0:t0 + n], op=ALU.add)
                        yield

                def epilogue(t0, n):
                    for _ in epilogue_gen(t0, n):
                        pass

                mask2 = cst[:, C_MASK2:C_MASK2 + 256]
                S4s = [pall[:, 0:1024].rearrange("p (h k) -> p h k", h=4), pall[:, 2048:3072].rearrange("p (h k) -> p h k", h=4)]
                S4t = [("ps0", "ps1"), ("ps4", "ps5")]
                PTbs = [ps[2].bitcast(BF16).rearrange("p (i q) -> p i q", i=8), ps[6].bitcast(BF16).rearrange("p (i q) -> p i q", i=8)]
                PO = ps[3].rearrange("p (j q) -> p j q", j=4)
                items = [(b, g) for b in range(16) for g in range(2)]

                def geom(b):
                    c0 = 128 if b == 0 else 0
                    return c0, 256 - c0, (b - 1) * 128 + c0

                def scores(i):
                    b, g = items[i]
                    c0, nk, k0 = geom(b)
                    par = i % 2
                    for hl in range(4):
                        h = g * 4 + hl
                        base = (h % 2) * 64
                        sl_ = (hl % 2) * 2 + hl // 2
                        E("pe", "matmul", ["qT", "kd"], [S4t[par][sl_ // 2]], out=S4s[par][:, sl_, c0:256],
                          lhsT=qT[base:base + 64, h // 2, b * 128:(b + 1) * 128], rhs=kd[base:base + 64, g, k0:k0 + nk], start=True, stop=False)
                        E("pe", "matmul", ["identb", "maskb"], [S4t[par][sl_ // 2]], out=S4s[par][:, sl_, c0:256],
                          lhsT=identb[:], rhs=maskb[:, c0:256], start=False, stop=True)

                def softmax_gen(i):
                    b, g = items[i]
                    c0, nk, k0 = geom(b)
                    par = i % 2
                    S4, sc_, pn_ = S4s[par], sc[par], pn[par]
                    mx_, mx2_, rs_, es2 = mx[par], mx2[par], rs[par], es_[par]
                    tg = lambda nm: f"{nm}{par}"
                    pt = [S4t[par][0], S4t[par][1]]
                    sk = sm[:, SM_SINK + 8 * l + 4 * g:SM_SINK + 8 * l + 4 * g + 4]
                    E("dve", "tensor_reduce", pt, [tg("mx")], out=mx_[:], in_=S4[:, :, c0:256], axis=AX.X, op=ALU.max)
                    yield
                    E("dve", "scalar_tensor_tensor", [tg("mx"), "sm"], [tg("mx2")], out=mx2_[:], in0=mx_[:], scalar=0.125, in1=sk, op0=ALU.mult, op1=ALU.max)
                    yield
                    E("dve", "tensor_scalar", [tg("mx2")], [tg("mx")], out=mx_[:], in0=mx2_[:], scalar1=-1.0, scalar2=None, op0=ALU.mult)
                    yield
                    for sl_ in range(4):
                        E("act", "activation", [pt[sl_ // 2], tg("mx")], [tg("sc"), tg("rs")], out=sc_[:, sl_, c0:256], in_=S4[:, sl_, c0:256], func=AF.Exp,
                          scale=0.125, bias=mx_[:, sl_:sl_ + 1], accum_out=rs_[:, sl_:sl_ + 1])
                    yield
                    E("dve", "tensor_tensor", [tg("mx"), "sm"], [tg("es")], out=es2[:], in0=sk, in1=mx_[:], op=ALU.add)
                    yield
                    E("act", "activation", [tg("es")], [tg("es")], out=es2[:], in_=es2[:], func=AF.Exp)
                    yield
                    E("dve", "tensor_tensor", [tg("rs"), tg("es")], [tg("rs")], out=rs_[:], in0=rs_[:], in1=es2[:], op=ALU.add)
                    yield
                    E("dve", "reciprocal", [tg("rs")], [tg("rs")], out=rs_[:], in_=rs_[:])
                    yield
                    E("dve", "tensor_tensor", [tg("sc"), tg("rs")], [tg("pn")], out=pn_[:, :, c0:256], in0=sc_[:, :, c0:256],
                      in1=bc(rs_[:], 2, [128, 4, nk]), op=ALU.mult)
                    yield

                def tail(i):
                    b, g = items[i]
                    par = i % 2
                    pn_ = pn[par]
                    PTb = PTbs[par]; pT = pTs[par]; ptag = ["ps2", "ps6"][par]; ttag = f"pT{par}"
                    kts = [1] if b == 0 else [0, 1]
                    for hl in range(4):
                        for kt in kts:
                            E("pe", "transpose", [f"pn{par}", "identb"], [ptag], out=PTb[:, hl * 2 + kt, :],
                              in_=pn_[:, (hl % 2) * 2 + hl // 2, kt * 128:(kt + 1) * 128], identity=identb[:])
                    if b == 0:
                        E("act", "copy", [ptag], [ttag], out=pT[:, 1::2, :], in_=PTb[:, 1::2, :])
                    else:
                        E("act", "copy", [ptag], [ttag], out=pT[:], in_=PTb)
                    for hl in range(4):
                        r0 = (hl % 2) * 64
                        for kt in kts:
                            E("pe", "matmul", ["vtm", ttag], ["ps3"], out=PO[r0:r0 + 64, g * 2 + hl // 2, :], lhsT=vtm[:, b - 1 + kt, g * 64:(g + 1) * 64],
                              rhs=pT[:, hl * 2 + kt, :], start=(kt == kts[0]), stop=(kt == kts[-1]), tile_position=(0, r0))

                if "attn" not in SKIP:
                    scores(0)
                    scores(1)
                    for b_ in range(16):
                        gens = [softmax_gen(2 * b_), softmax_gen(2 * b_ + 1)]
                        for _ in range(4):
                            for g_ in gens:
                                next(g_)
                        if b_ + 1 < 16:
                            scores(2 * b_ + 2)
                            scores(2 * b_ + 3)
                        alive = list(gens)
                        if b_ > 0:
                            alive.append(epilogue_gen((b_ - 1) * 128, 128))
                        while alive:
                            for g_ in list(alive):
                                try:
                                    next(g_)
                                except StopIteration:
                                    alive.remove(g_)
                        tail(2 * b_)
                        tail(2 * b_ + 1)
                        E("act", "copy", ["ps3"], ["so"], out=so[:], in_=PO)
                    epilogue(15 * 128, 128)

                P.flush("swa_p")
                p2.__exit__(None, None, None)
                if dbg == "swa_p":
                    return
                p3 = contextlib.ExitStack()
                p3.__enter__()
                def pb3(name, shape, dt=F32):
                    return p3.enter_context(SBT(name, list(shape), dt))
                ckb = pb3("ckb", [128, 16, 128], BF16)
                cvb = pb3("cvb", [128, 16, 128], BF16)
                qtm = pb3("qtm", [64, 512], BF16)
                qTs = pb3("qTs", [64, 16, 8, 4], BF16)
                kf = pb3("kf", [64, 16, 2, 132], BF16)
                scs = pb3("scs", [16, 8, 132])
                mxs = pb3("mxs", [16, 8]); mxs2 = pb3("mxs2", [16, 8]); rss = pb3("rss", [16, 8]); ess = pb3("ess", [16, 8])
                pc = pb3("pc", [16, 8, 128], BF16)
                pz = pb3("pz", [16, 16, 2, 64], BF16)
                ptc = pb3("ptc", [128, 8, 16], BF16)
                ptz = pb3("ptz", [64, 8, 16], BF16)
                osb = pb3("osb", [16, 16, 2, 64])
                otm = pb3("otm", [64, 512])
                P.op("pool", lambda e: e.dma_start(out=ckb[:], in_=ck_d[l].rearrange("s k f -> k s f")), w=["ckb"], dma=True)
                P.op("pool", lambda e: e.dma_start(out=cvb[:], in_=cv_d[l].rearrange("s k f -> k s f")), w=["cvb"], dma=True)
                P.op("dve", lambda e: e.memset(pz[:], 0.0), w=["pz"])
                k_ = 0
                for c in range(8):
                    P.op("pe", lambda e, c=c: e.matmul(ps[0][0:64, :], lhsT=xn[:, c, NPR:T], rhs=wsw[:, c, 0:512],
                                                      start=(c == 0), stop=(c == 7)), r=["wsw", "xn4"], w=["ps0"])
                P.op("act", lambda e: e.copy(out=qtm[:], in_=ps[0][0:64, :]), r=["ps0"], w=["qtm"])
                QTb = ps[1].bitcast(BF16)[0:64, 0:512].rearrange("p (h t s) -> p h t s", h=8, t=4)
                for h in range(8):
                    P.op("pe", lambda e, h=h: e.transpose(out=ps[1].bitcast(BF16)[0:64, h * 64:(h + 1) * 64],
                                                          in_=qtm[:, h * 64:(h + 1) * 64], identity=identb[0:64, 0:64]),
                         r=["qtm", "identb"], w=["ps1"])
                P.op("dve", lambda e: e.tensor_copy(out=qTs[:].rearrange("p s h t -> p h t s"), in_=QTb), r=["ps1"], w=["qTs"])
                KTb = ps[2].bitcast(BF16)[0:64, :].rearrange("p (i k) -> p i k", i=8)
                for w4 in range(4):
                    for sl in range(4):
                        for g in range(2):
                            P.op("pe", lambda e, w4=w4, sl=sl, g=g: e.transpose(
                                out=KTb[:, sl * 2 + g, :], in_=ckb[:, w4 * 4 + sl, g * 64:(g + 1) * 64], identity=identb[:]),
                                r=["ckb", "identb"], w=["ps2"])
                    P.op("act", lambda e, w4=w4: e.copy(
                        out=kf[:, w4 * 4:w4 * 4 + 4, :, 0:128],
                        in_=KTb.rearrange("p (s g) k -> p s g k", g=2)), r=["ps2"], w=["kf"])
                P.op("dve", lambda e: e.tensor_copy(
                    out=kf[:, :, :, 128:132].rearrange("p s g t -> p g t s"),
                    in_=kd[0:64, :, NPR:T].rearrange("p g (t s) -> p g t s", t=4)), r=["kd"], w=["kf"])
                smask = cst[0:16, C_SMASK:C_SMASK + 132]
                SC = pall[0:16, 0:1024].rearrange("p (i k) -> p i k", i=8)
                SN = ps[2][0:16, 0:32].rearrange("p (i k) -> p i k", i=8)
                PTC = ps[3].bitcast(BF16)[:, 0:128].rearrange("p (i q) -> p i q", i=8)
                PTZ = ps[3].bitcast(BF16)[0:64, 128:256].rearrange("p (i q) -> p i q", i=8)
                OS = ps[4][0:16, :].rearrange("p (i d) -> p i d", i=8)
                for w4 in range(4):
                    for sl in range(4):
                        s_ = w4 * 4 + sl
                        for g in range(2):
                            i = sl * 2 + g
                            P.op("pe", lambda e, s_=s_, g=g, i=i: e.matmul(
                                SC[:, i, :], lhsT=qTs[:, s_, g * 4:(g + 1) * 4, :], rhs=kf[:, s_, g, 0:128],
                                start=True, stop=True), r=["qTs", "kf"], w=[f"ps{i // 4}"])
                            P.op("pe", lambda e, s_=s_, g=g, i=i: e.matmul(
                                SN[:, i, :], lhsT=qTs[:, s_, g * 4:(g + 1) * 4, :], rhs=kf[:, s_, g, 128:132],
                                start=True, stop=True), r=["qTs", "kf"], w=["ps2"])
                    P.op("dve", lambda e: e.scalar_tensor_tensor(
                        out=scs[:, :, 0:128], in0=SC, scalar=0.125,
                        in1=smask[:, 0:128].unsqueeze(1).broadcast_to([16, 8, 128]), op0=ALU.mult, op1=ALU.add),
                        r=["ps0", "ps1", "cst"], w=["scs"])
                    P.op("dve", lambda e: e.scalar_tensor_tensor(
                        out=scs[:, :, 128:132], in0=SN, scalar=0.125,
                        in1=smask[:, 128:132].unsqueeze(1).broadcast_to([16, 8, 4]), op0=ALU.mult, op1=ALU.add),
                        r=["ps2", "cst"], w=["scs"])
                    P.op("dve", lambda e: e.tensor_reduce(out=mxs[:], in_=scs[:], axis=AX.X, op=ALU.max), r=["scs"], w=["mxs"])
                    sks = sm[0:16, SM_SSINK + 2 * l:SM_SSINK + 2 * l + 2].unsqueeze(1).broadcast_to([16, 4, 2])
                    mxs3 = mxs[:].rearrange("p (s g) -> p s g", g=2)
                    mxs23 = mxs2[:].rearrange("p (s g) -> p s g", g=2)
                    ess3 = ess[:].rearrange("p (s g) -> p s g", g=2)
                    P.op("dve", lambda e, sks=sks, mxs3=mxs3, mxs23=mxs23: e.tensor_tensor(out=mxs23, in0=mxs3, in1=sks, op=ALU.max),
                         r=["mxs", "sm"], w=["mxs2"])
                    P.op("dve", lambda e: e.tensor_tensor(out=scs[:], in0=scs[:], in1=mxs2[:].unsqueeze(2).broadcast_to([16, 8, 132]),
                                                          op=ALU.subtract), r=["scs", "mxs2"], w=["scs"])
                    P.op("act", lambda e: e.activation(out=scs[:], in_=scs[:], func=AF.Exp), r=["scs"], w=["scs"])
                    P.op("dve", lambda e: e.tensor_reduce(out=rss[:], in_=scs[:], axis=AX.X, op=ALU.add), r=["scs"], w=["rss"])
                    P.op("dve", lambda e, sks=sks, mxs23=mxs23, ess3=ess3: e.tensor_tensor(out=ess3, in0=sks, in1=mxs23, op=ALU.subtract),
                         r=["mxs2", "sm"], w=["ess"])
                    P.op("act", lambda e: e.activation(out=ess[:], in_=ess[:], func=AF.Exp), r=["ess"], w=["ess"])
                    P.op("dve", lambda e: e.tensor_tensor(out=rss[:], in0=rss[:], in1=ess[:], op=ALU.add), r=["rss", "ess"], w=["rss"])
                    P.op("dve", lambda e: e.reciprocal(out=rss[:], in_=rss[:]), r=["rss"], w=["rss"])
                    P.op("dve", lambda e: e.tensor_tensor(out=pc[:], in0=scs[:, :, 0:128],
                                                          in1=rss[:].unsqueeze(2).broadcast_to([16, 8, 128]), op=ALU.mult),
                         r=["scs", "rss"], w=["pc"])
                    for sl in range(4):
                        s_ = w4 * 4 + sl
                        P.op("dve", lambda e, sl=sl, s_=s_: e.tensor_tensor(
                            out=pz[:, s_, :, :].rearrange("p g (t s) -> p g t s", t=4)[:, :, :, s_],
                            in0=scs[:, sl * 2:sl * 2 + 2, 128:132],
                            in1=rss[:, sl * 2:sl * 2 + 2].unsqueeze(2).broadcast_to([16, 2, 4]), op=ALU.mult),
                            r=["scs", "rss"], w=["pz"])
                    for sl in range(4):
                        s_ = w4 * 4 + sl
                        for g in range(2):
                            i = sl * 2 + g
                            P.op("pe", lambda e, i=i: e.transpose(out=PTC[:, i, :], in_=pc[:, i, :], identity=identb[0:16, 0:16]),
                                 r=["pc", "identb"], w=["ps3"])
                            P.op("pe", lambda e, i=i, s_=s_, g=g: e.transpose(out=PTZ[:, i, :], in_=pz[:, s_, g, :], identity=identb[0:16, 0:16]),
                                 r=["pz", "identb"], w=["ps3"])
                    P.op("act", lambda e: e.copy(out=ptc[:], in_=PTC), r=["ps3"], w=["ptc"])
                    P.op("act", lambda e: e.copy(out=ptz[:], in_=PTZ), r=["ps3"], w=["ptz"])
                    for sl in range(4):
                        s_ = w4 * 4 + sl
                        for g in range(2):
                            i = sl * 2 + g
                            P.op("pe", lambda e, i=i, s_=s_, g=g: e.matmul(
                                OS[:, i, :], lhsT=ptc[:, i, :], rhs=cvb[:, s_, g * 64:(g + 1) * 64], start=True, stop=False),
                                r=["ptc", "cvb"], w=["ps4"])
                            P.op("pe", lambda e, i=i, s_=s_, g=g: e.matmul(
                                OS[:, i, :], lhsT=ptz[:, i, :], rhs=vtm[0:64, 16, g * 64:(g + 1) * 64], start=False, stop=True),
                                r=["ptz", "vtm"], w=["ps4"])
                    P.op("dve", lambda e, w4=w4: e.tensor_copy(
                        out=osb[:, w4 * 4:w4 * 4 + 4, :, :], in_=OS.rearrange("p (s g) d -> p s g d", g=2)), r=["ps4"], w=["osb"])
                for hl in range(4):
                    for t_ in range(4):
                        P.op("sp", lambda e, hl=hl, t_=t_: e.dma_start(
                            out=otm[t_ * 16:(t_ + 1) * 16, :].rearrange("s (g h d) -> s g h d", g=2, h=4)[:, :, hl, :],
                            in_=osb[hl * 4 + t_:hl * 4 + t_ + 1, :, :, :]), r=["osb"], w=["otm"], dma=True)
                for c in range(4):
                    P.op("pe", lambda e, c=c: e.transpose(out=ps[3][:, c * 64:(c + 1) * 64], in_=otm[:, c * 128:(c + 1) * 128],
                                                          identity=identf[0:64, 0:64]), r=["otm", "identf"], w=["ps3"])
                P.op("dve", lambda e: e.tensor_copy(out=so[:, :, 0:64], in_=ps[3][:, 0:256].rearrange("p (c t) -> p c t", c=4)),
                     r=["ps3"], w=["so"])
                epilogue(NPR, 64)
                P.flush("swa_s")
                p3.__exit__(None, None, None)

        def E(eng, meth, r, w, **kw):
            P.op(eng, lambda e, kw=kw, meth=meth: getattr(e, meth)(**kw), r=r, w=w)

        def DMA(eng, r, w, **kw):
            P.op(eng, lambda e, kw=kw: e.dma_start(**kw), r=r, w=w, dma=True)

        def bc(ap, axis, shape):
            return ap.unsqueeze(axis).broadcast_to(list(shape))

        def gdn(l):
            QKV_OFF = XN_BYTES
            qkv = AR.view(QKV_OFF, [128, 12, T], BF16)
            ZS_OFF = ARENA - 4 * T * 2
            zs = AR.view(ZS_OFF, [128, 4, T], BF16)
            GP = ZS_OFF - 4608
            names = ["btm", "gtm", "gctm", "gltm", "egc", "kdc", "bk"]
            G = {nm: AR.view(GP + 528 * i, [64, 33, 4], F32) for i, nm in enumerate(names)}
            btm, gtm, gctm, gltm, egc, kdc, bk = [G[nm] for nm in names]
            cdec = AR.view(GP + 3696, [128, 32, 4], F32)
            cdecs = AR.view(GP + 4208, [128, 16, 4], F32)
            TMP0 = QKV_OFF + 12 * T * 2
            cwc = SM_CONVW + 48 * l
            id64 = identf[0:64, 0:64]

            AR.top = TMP0
            with contextlib.ExitStack() as ph:
                def al(shape, dt=F32):
                    return ph.enter_context(AR.alloc(list(shape), dt))
                wba = al([128, 8, 8], BF16)
                xa = al([64, 33, 4]); xb = al([64, 33, 4]); nA = al([64, 4]); rhs_s = al([64, 16, 4])
                DMA("pool", [], ["wba"], out=wba[:], in_=w_in[l, :, OFF_B:OFF_B + 8].rearrange("(c p) f -> p c f", p=128))
                BA = ps[5][0:64, 0:264].rearrange("p (n f) -> p n f", f=8)
                for n in range(33):
                    for c in range(8):
                        E("pe", "matmul", ["wba", "xn4" if n == 32 else f"xn{n // 8}"], ["ps5"], out=BA[:, n, :],
                          lhsT=xn[:, c, n * 64:(n + 1) * 64], rhs=wba[:, c, :], start=(c == 0), stop=(c == 7))
                E("act", "activation", ["ps5"], ["btm"], out=btm[:], in_=BA[:, :, 0:4], func=AF.Sigmoid)
                E("dve", "tensor_tensor", ["ps5", "sm"], ["xa"], out=xa[:], in0=BA[:, :, 4:8],
                  in1=bc(sm[0:64, SM_DTB + 4 * l:SM_DTB + 4 * l + 4], 1, [64, 33, 4]), op=ALU.add)
                E("act", "activation", ["xa"], ["xb"], out=xb[:], in_=xa[:], func=AF.Abs)
                E("act", "activation", ["xb"], ["xb"], out=xb[:], in_=xb[:], func=AF.Exp, scale=-1.0)
                E("act", "activation", ["xb", "oneb"], ["xb"], out=xb[:], in_=xb[:], func=AF.Ln, bias=oneb[0:64, :])
                E("dve", "tensor_scalar", ["xa"], ["xa"], out=xa[:], in0=xa[:], scalar1=0.0, scalar2=None, op0=ALU.max)
                E("dve", "tensor_tensor", ["xa", "xb"], ["xa"], out=xa[:], in0=xa[:], in1=xb[:], op=ALU.add)
                E("act", "activation", ["sm"], ["nA"], out=nA[:], in_=sm[0:64, SM_ALOG + 4 * l:SM_ALOG + 4 * l + 4], func=AF.Exp)
                E("dve", "tensor_scalar", ["nA"], ["nA"], out=nA[:], in0=nA[:], scalar1=-1.0, scalar2=None, op0=ALU.mult)
                E("dve", "tensor_tensor", ["xa", "nA"], ["gtm"], out=gtm[:], in0=xa[:], in1=bc(nA[:], 1, [64, 33, 4]), op=ALU.mult)
                gflat = gtm[:, 0:32, :].rearrange("p n h -> p (n h)")
                E("pe", "matmul", ["gtm", "cst"], ["ps4"], out=ps[4][0:64, 0:128], lhsT=cst[0:64, C_TRI:C_TRI + 64], rhs=gflat, start=True, stop=True)
                E("pe", "matmul", ["gtm", "cst"], ["ps4"], out=ps[4][0:64, 128:132], lhsT=cst[0:64, C_TRIS:C_TRIS + 64], rhs=gtm[:, 32, :], start=True, stop=True)
                E("pe", "matmul", ["gtm", "onesf"], ["ps4"], out=ps[4][0:64, 256:384], lhsT=ones_f[0:64, 0:64], rhs=gflat, start=True, stop=True)
                E("pe", "matmul", ["gtm", "cst"], ["ps4"], out=ps[4][0:64, 384:388], lhsT=cst[0:64, C_SAMES:C_SAMES + 64], rhs=gtm[:, 32, :], start=True, stop=True)
                E("dve", "tensor_copy", ["ps4"], ["gctm"], out=gctm[:].rearrange("p n h -> p (n h)"), in_=ps[4][0:64, 0:132])
                E("dve", "tensor_copy", ["ps4"], ["gltm"], out=gltm[:].rearrange("p n h -> p (n h)"), in_=ps[4][0:64, 256:388])
                E("act", "activation", ["gctm"], ["egc"], out=egc[:], in_=gctm[:], func=AF.Exp)
                E("dve", "tensor_tensor", ["gltm", "gctm"], ["kdc"], out=kdc[:], in0=gltm[:], in1=gctm[:], op=ALU.subtract)
                E("act", "activation", ["kdc"], ["kdc"], out=kdc[:], in_=kdc[:], func=AF.Exp)
                E("dve", "tensor_tensor", ["btm", "egc"], ["bk"], out=bk[:], in0=btm[:], in1=egc[:], op=ALU.mult)
                E("pe", "matmul", ["gltm", "e0sel"], ["ps4"], out=ps[4][:, 0:128], lhsT=e0sel[:], rhs=gltm[:, 0:32, :].rearrange("p n h -> p (n h)"),
                  start=True, stop=True)
                E("dve", "tensor_tensor", ["gltm", "cst"], ["rhs_s"], out=rhs_s[:], in0=bc(cst[0:64, C_OH48:C_OH48 + 16], 2, [64, 16, 4]),
                  in1=bc(gltm[:, 32, :], 1, [64, 16, 4]), op=ALU.mult)
                E("pe", "matmul", ["rhs_s", "onesf"], ["ps4"], out=ps[4][:, 128:192], lhsT=ones_f[0:64, :], rhs=rhs_s[:].rearrange("p s h -> p (s h)"),
                  start=True, stop=True)
                E("act", "activation", ["ps4"], ["cdec"], out=cdec[:].rearrange("p n h -> p (n h)"), in_=ps[4][:, 0:128], func=AF.Exp)
                E("act", "activation", ["ps4"], ["cdecs"], out=cdecs[:].rearrange("p s h -> p (s h)"), in_=ps[4][:, 128:192], func=AF.Exp)

                P.flush("gdn_gates")
            AR.top = TMP0
            with contextlib.ExitStack() as ph:
                def al(shape, dt=F32):
                    return ph.enter_context(AR.alloc(list(shape), dt))
                wblk = al([128, 8, 256], BF16)
                hpre = al([128, T]); accb = [al([128, 1088]) for _ in range(2)]
                cs = al([128, 16, 3]); cin = al([48, 128])
                fulls = al([128, 16, 7]); accs = al([128, 16, 4])
                sqbs = [al([128, 1024], BF16), al([128, 1088], BF16)]
                htm = al([96, 256])
                assert AR.top <= GP, AR.top
                acnt = [0]
                kcnt = [0]
                for wb in range(8):
                    col0 = wb * 256
                    DMA("pool", [], ["wblk"], out=wblk[:], in_=w_in[l, :, col0:col0 + 256].rearrange("(c p) f -> p c f", p=128))
                    if wb < 6:
                        for c in range(8):
                            E("pe", "matmul", ["wblk", "xn3", "xn4"], ["ps7"], out=ps[7][0:96, 0:256], lhsT=xn[:, c, NPR - 32:T], rhs=wblk[:, c, :],
                              start=(c == 0), stop=(c == 7))
                        E("act", "copy", ["ps7"], ["htm"], out=htm[:], in_=ps[7][0:96, 0:256])
                        DMA("sp", ["htm"], [], out=ocp[l, :, col0:col0 + 256], in_=htm[29:32, :])
                        for i3 in range(3):
                            DMA("sp", ["htm"], [], out=ocs[l, :, i3, col0:col0 + 256], in_=htm[32 + (i3 + 1) * 16:32 + (i3 + 2) * 16, :])
                    for jj in range(2):
                        j = wb * 2 + jj
                        for bi, (t0, tn) in enumerate(TBS):
                            k_ = (0, 1, 7)[kcnt[0] % 3]
                            kcnt[0] += 1
                            for c in range(8):
                                E("pe", "matmul", ["wblk", f"xn{bi}"], [f"ps{k_}"], out=ps[k_][:, 0:tn], lhsT=wblk[:, c, jj * 128:(jj + 1) * 128],
                                  rhs=xn[:, c, t0:t0 + tn], start=(c == 0), stop=(c == 7))
                            if j >= 12:
                                E("act", "activation", [f"ps{k_}"], ["zs"], out=zs[:, j - 12, t0:t0 + tn], in_=ps[k_][:, 0:tn], func=AF.Silu)
                            else:
                                E("act", "copy", [f"ps{k_}"], ["hpre"], out=hpre[:, t0:t0 + tn], in_=ps[k_][:, 0:tn])
                        if j >= 12:
                            continue
                        w_ = [sm[:, cwc + j * 4 + i:cwc + j * 4 + i + 1] for i in range(4)]
                        DMA("sp", [], ["cin"], out=cin[:], in_=cstate_d[l, :, j * 128:(j + 1) * 128])
                        E("pe", "transpose", ["cin", "identf"], ["ps7"], out=ps[7][:, 256:304], in_=cin[:], identity=identf[0:48, 0:48])
                        E("dve", "tensor_copy", ["ps7"], ["cs"], out=cs[:].rearrange("p s i -> p (s i)"), in_=ps[7][:, 256:304])
                        geo = []
                        for hf in range(2):
                            a0 = hf * 1024
                            ln = 1024
                            acc = accb[hf]
                            atg = f"acc{hf}"
                            E("dve", "tensor_scalar", ["hpre", "sm"], [atg], out=acc[:, 0:ln], in0=hpre[:, a0:a0 + ln], scalar1=w_[3], scalar2=None, op0=ALU.mult)
                            for sh in (1, 2, 3):
                                lo = sh if hf == 0 else 0
                                E("dve", "scalar_tensor_tensor", ["hpre", "sm", atg], [atg], out=acc[:, lo:ln], in0=hpre[:, a0 + lo - sh:a0 + ln - sh],
                                  scalar=w_[3 - sh], in1=acc[:, lo:ln], op0=ALU.mult, op1=ALU.add)
                            tot = ln
                            blocks = TBS[0:2] if hf == 0 else TBS[2:5]
                            if hf == 1:
                                E("dve", "tensor_copy", ["cs"], ["fulls"], out=fulls[:, :, 0:3], in_=cs[:])
                                E("dve", "tensor_copy", ["hpre"], ["fulls"], out=fulls[:, :, 3:7], in_=hpre[:, NPR:T].rearrange("p (t s) -> p s t", t=4))
                                E("dve", "tensor_scalar", ["fulls", "sm"], ["accs"], out=accs[:], in0=fulls[:, :, 3:7], scalar1=w_[3], scalar2=None, op0=ALU.mult)
                                for i in (0, 1):
                                    E("dve", "scalar_tensor_tensor", ["fulls", "sm", "accs"], ["accs"], out=accs[:], in0=fulls[:, :, i:i + 4], scalar=w_[i],
                                      in1=accs[:], op0=ALU.mult, op1=ALU.add)
                                E("dve", "scalar_tensor_tensor", ["fulls", "sm", "accs"], [atg], out=acc[:, 1024:1088].rearrange("p (t s) -> p s t", t=4),
                                  in0=fulls[:, :, 2:6], scalar=w_[2], in1=accs[:], op0=ALU.mult, op1=ALU.add)
                                tot = 1088
                            b0 = 2 if hf == 0 else 4
                            geo.append((hf, a0, tot, blocks, acc, atg, sqbs[hf], f"sqb{hf}", b0))
                        if j >= 8:
                            for (hf, a0, tot, blocks, acc, atg, sqb, stg, b0) in geo:
                                E("act", "activation", [atg], [f"qkv{j}"], out=qkv[:, j, a0:a0 + tot], in_=acc[:, 0:tot], func=AF.Silu)
                            continue
                        for (hf, a0, tot, blocks, acc, atg, sqb, stg, b0) in geo:
                            E("act", "activation", [atg], [atg], out=acc[:, 0:tot], in_=acc[:, 0:tot], func=AF.Silu)
                        for (hf, a0, tot, blocks, acc, atg, sqb, stg, b0) in geo:
                            E("act", "activation", [atg], [stg], out=sqb[:, 0:tot], in_=acc[:, 0:tot], func=AF.Square)
                        SSs = []
                        for (hf, a0, tot, blocks, acc, atg, sqb, stg, b0) in geo:
                            SS = pall[:, b0 * 512:b0 * 512 + tot]
                            sstags = [f"ps{b0 + q}" for q in range((tot + 511) // 512)]
                            SSs.append((SS, sstags))
                            for (t0, tn) in blocks:
                                o_ = t0 - a0
                                E("pe", "matmul", [stg, "ones"], [f"ps{b0 + o_ // 512}"], out=SS[:, o_:o_ + tn], lhsT=ones_bf[:], rhs=sqb[:, o_:o_ + tn], start=True, stop=True)
                        for (SS, sstags) in SSs:
                            E("act", "activation", sstags + ["epsb"], sstags, out=SS, in_=SS, func=AF.Ln, bias=epsb[:])
                        for (SS, sstags) in SSs:
                            E("act", "activation", sstags, sstags, out=SS, in_=SS, func=AF.Exp, scale=-0.5)
                        for (hf, a0, tot, blocks, acc, atg, sqb, stg, b0), (SS, sstags) in zip(geo, SSs):
                            E("dve", "scalar_tensor_tensor", [atg] + sstags, [f"qkv{j}"], out=qkv[:, j, a0:a0 + tot], in0=acc[:, 0:tot],
                              scalar=(128 ** -0.5 if j < 4 else 1.0), in1=SS, op0=ALU.mult, op1=ALU.mult)
                P.flush("gdn_p1")
            if dbg == "gdn_p1":
                return

            M = AR.view(0, [128, 4096], F32)
            M3 = M.rearrange("p (r c) -> p r c", c=64)
            tmp = AR.view(16384, [128, 1024], F32)
            AR.top = 20480
            with contextlib.ExitStack() as ph:
                def al(shape, dt=F32):
                    return ph.enter_context(AR.alloc(list(shape), dt))
                dgs = [al([64, 8, 64]) for _ in range(2)]; t1s = [al([64, 8, 64]) for _ in range(2)]
                A_sb = [al([64, 8, 64]) for _ in range(2)]
                assert AR.top <= XN_BYTES
                groups = [(h, gq) for h in range(4) for gq in range(5)]

                def geo_(ai):
                    h, gq = groups[ai]
                    n0, nn = (gq * 8, 8) if gq < 4 else (32, 1)
                    return h, gq, n0, nn, ai % 2

                def stage_x(ai):
                    h, gq, n0, nn, par = geo_(ai)
                    c0 = n0 * 64
                    dg, t1 = dgs[par], t1s[par]
                    gcb, gct = ps[par], f"ps{par}"
                    kkb, kkt = ps[2 + par], f"ps{2 + par}"
                    mp1 = cst[0:64, C_MP1:C_MP1 + 64] if gq < 4 else cst[0:64, C_MP1S:C_MP1S + 64]
                    E("dve", "tensor_tensor", ["identf", "gctm"], [f"dg{par}"], out=dg[:, 0:nn, :], in0=bc(id64, 1, [64, nn, 64]),
                      in1=bc(gctm[:, n0:n0 + nn, h], 2, [64, nn, 64]), op=ALU.mult)
                    E("pe", "matmul", [f"dg{par}", "onesf"], [gct], out=gcb[0:64, 0:nn * 64], lhsT=ones_f[0:64, 0:64],
                      rhs=dg[:, 0:nn, :].rearrange("p n j -> p (n j)"), start=True, stop=True)
                    for q in range(nn):
                        cc = c0 + q * 64
                        E("pe", "matmul", ["qkv"], [kkt], out=kkb[0:64, q * 64:(q + 1) * 64], lhsT=qkv[:, 4 + h, cc:cc + 64],
                          rhs=qkv[:, 4 + h, cc:cc + 64], start=True, stop=True)
                    g3 = gcb[0:64, 0:nn * 64].rearrange("p (n j) -> p n j", j=64)
                    E("dve", "tensor_tensor", [gct, "cst"], [f"t1{par}"], out=t1[:, 0:nn, :], in0=g3, in1=bc(mp1, 1, [64, nn, 64]), op=ALU.add)
                    E("dve", "tensor_tensor", [f"t1{par}", "gctm"], [f"t1{par}"], out=t1[:, 0:nn, :], in0=t1[:, 0:nn, :],
                      in1=bc(gctm[:, n0:n0 + nn, h], 2, [64, nn, 64]), op=ALU.subtract)
                    E("act", "activation", [f"t1{par}"], [f"t1{par}"], out=t1[:, 0:nn, :], in_=t1[:, 0:nn, :], func=AF.Exp, scale=-1.0)

                def stage_y(ai):
                    h, gq, n0, nn, par = geo_(ai)
                    t1 = t1s[par]
                    asb = A_sb[par]; atag = f"A_sb{par}"
                    kkb, kkt = ps[2 + par], f"ps{2 + par}"
                    k3 = kkb[0:64, 0:nn * 64].rearrange("p (n j) -> p n j", j=64)
                    E("dve", "tensor_tensor", [kkt, f"t1{par}"], [atag], out=asb[:, 0:nn, :], in0=k3, in1=t1[:, 0:nn, :], op=ALU.mult)
                    E("dve", "tensor_tensor", [atag, "btm"], [atag], out=asb[:, 0:nn, :], in0=asb[:, 0:nn, :],
                      in1=bc(btm[:, n0:n0 + nn, h], 2, [64, nn, 64]), op=ALU.mult)
                    p0 = h * 32 + n0 if gq < 4 else 128 + h
                    DMA("sp", [atag], ["Ad"], out=Ad[p0:p0 + nn, :].rearrange("n (i j) -> i n j", i=64), in_=asb[:, 0:nn, :])

                stage_x(0)
                for ai in range(len(groups)):
                    if ai + 1 < len(groups):
                        stage_x(ai + 1)
                    stage_y(ai)
            DMA("sp", ["Ad"], ["M0"], out=M[:, :], in_=Ad[0:128, :])
            E("dve", "memset", [], ["M0"], ap=M[:, ::65], constant=1.0)
            for j in range(63):
                nr, ni = j + 1, 63 - j
                tv = tmp[:, 0:nr * ni].rearrange("p (r i) -> p r i", i=ni)
                E("dve", "tensor_tensor", ["M0"], ["tmp0"], out=tv, in0=bc(M3[:, 0:nr, j], 2, [128, nr, ni]), in1=bc(M3[:, j + 1:64, j], 1, [128, nr, ni]),
                  op=ALU.mult)
                E("dve", "tensor_tensor", ["M0", "tmp0"], ["M0"], out=M3[:, 0:nr, j + 1:64], in0=M3[:, 0:nr, j + 1:64], in1=tv, op=ALU.subtract)
            DMA("sp", ["M0"], ["Ud"], out=Ud[0:128, :], in_=M[:, :])
            DMA("sp", ["Ad", "M0"], ["M0"], out=M[0:4, :], in_=Ad[128:132, :])
            E("dve", "memset", [], ["M0"], ap=M[0:4, ::65], constant=1.0)
            M5 = M[0:4, :].rearrange("p (r c) -> p r c", c=64)
            for j in range(3):
                for r_ in range(j + 1):
                    pass
            Mf = M[0:4, :]
            def dsl(rt, ct):
                o0 = (rt * 16) * 64 + ct * 16
                return Mf[:, o0:o0 + 15 * 65 + 1:65]
            tmp4 = tmp[0:4, 0:16]
            for j in range(3):
                for i in range(j + 1, 4):
                    for r_ in range(j + 1):
                        E("dve", "tensor_tensor", ["M0"], ["tmp0"], out=tmp4, in0=dsl(r_, j), in1=dsl(i, j), op=ALU.mult)
                        E("dve", "tensor_tensor", ["M0", "tmp0"], ["M0"], out=dsl(r_, i), in0=dsl(r_, i), in1=tmp4, op=ALU.subtract)
            DMA("sp", ["M0"], ["Ud"], out=Ud[128:132, :], in_=M[0:4, :])
            P.flush("gdn_solve")

            AR.top = TMP0
            with contextlib.ExitStack() as ph:
                def al(shape, dt=F32):
                    return ph.enter_context(AR.alloc(list(shape), dt))
                Uc = [al([64, 4, 64], BF16) for _ in range(3)]
                wog = al([128, 4, D], BF16)
                wv1 = al([64, 4, 128])
                wv = [wv1, wv1]
                u_sb = al([64, 4, 128], BF16)
                Sf = al([128, 4, 128]); Sb = al([128, 4, 128], BF16)
                obufs = [al([128, 4, 256]) for _ in range(2)]; go = al([128, 4, 256], BF16)
                sqh = al([128, 256], BF16); rsh = al([128, 256])
                assert AR.top <= GP, AR.top
                top2 = AR.top
                AR.top = 0
                kbg = [al([64, 4, 128], BF16) for _ in range(2)]
                kdec = [al([64, 4, 128], BF16) for _ in range(2)]
                vb = [al([64, 4, 128], BF16) for _ in range(2)]
                qkT = [al([64, 4, 64], BF16) for _ in range(2)]
                t2 = al([64, 4, 64]); dg2 = al([64, 4, 64]); egb = al([128, 4, 64])
                kcT = [al([128, 4, 64], BF16) for _ in range(2)]
                qdT = [al([128, 4, 64], BF16) for _ in range(2)]
                Sfs = al([128, 16, 128]); Sbs = al([128, 16, 128], BF16)
                kdm = al([64, 16, 128], BF16); kcm = al([128, 16, 64], BF16); qdm = al([128, 16, 64], BF16)
                assert AR.top <= XN_BYTES, AR.top

                UdP = Ud[0:128, :].rearrange("(h n) (r i) -> n r h i", h=4, i=64)
                UdS = Ud[128:132, :].rearrange("h (r i) -> r h i", i=64)
                DMA("pool", [], ["wog"], out=wog[:], in_=w_out[l, 0:512, :].rearrange("(c p) f -> p c f", p=128))
                E("dve", "memset", [], ["Sf"], ap=Sf[:], constant=0.0)
                E("dve", "memset", [], ["Sb"], ap=Sb[:], constant=0.0)

                TP = ps[0].bitcast(BF16)[0:64, :].rearrange("p (i d) -> p i d", i=8)
                QK = ps[1][0:64, 0:256].rearrange("p (h i) -> p h i", h=4)
                GC2 = ps[1][0:64, 256:512].rearrange("p (h i) -> p h i", h=4)
                WV = ps[2][0:64, :].rearrange("p (h d) -> p h d", h=4)
                KC = ps[3][:, 0:256].rearrange("p (h i) -> p h i", h=4)
                EGB = ps[3][:, 256:512].rearrange("p (h i) -> p h i", h=4)
                UP = ps[4][0:64, :].rearrange("p (h d) -> p h d", h=4)
                OP = ps[5][:, 0:256].rearrange("p (h i) -> p h i", h=4)
                SP = ps[6].rearrange("p (h d) -> p h d", h=4)
                gnw = sm[:, SM_GNW + l:SM_GNW + l + 1]

                def prep1(n, rb):
                    c0 = n * 64
                    samp = n == 32
                    ub = n % 3
                    Ub = Uc[ub]
                    DMA("pool", ["Ud"], [f"Uc{ub}"], out=Ub[:], in_=(UdS if samp else UdP[n]))
                    E("dve", "tensor_tensor", [f"Uc{ub}", "triu"], [f"Uc{ub}"], out=Ub[:], in0=Ub[:], in1=bc(triu_bf[:], 1, [64, 4, 64]), op=ALU.mult)
                    E("dve", "tensor_tensor", ["identf", "gctm"], ["dg2"], out=dg2[:], in0=bc(id64, 1, [64, 4, 64]), in1=bc(gctm[:, n, :], 2, [64, 4, 64]), op=ALU.mult)
                    for h in range(4):
                        E("pe", "transpose", ["qkv", "identb"], ["ps0"], out=TP[:, h, :], in_=qkv[:, 4 + h, c0:c0 + 64], identity=identb[:])
                        E("pe", "transpose", ["qkv", "identb"], ["ps0"], out=TP[:, 4 + h, :], in_=qkv[:, 8 + h, c0:c0 + 64], identity=identb[:])
                    E("dve", "tensor_tensor", ["ps0", "bk"], [f"kbg{rb}"], out=kbg[rb][:], in0=TP[:, 0:4, :], in1=bc(bk[:, n, :], 2, [64, 4, 128]), op=ALU.mult)
                    E("dve", "tensor_tensor", ["ps0", "btm"], [f"vb{rb}"], out=vb[rb][:], in0=TP[:, 4:8, :], in1=bc(btm[:, n, :], 2, [64, 4, 128]), op=ALU.mult)
                    E("dve", "tensor_tensor", ["ps0", "kdc"], [f"kdec{rb}"], out=kdec[rb][:], in0=TP[:, 0:4, :], in1=bc(kdc[:, n, :], 2, [64, 4, 128]), op=ALU.mult)
                    for h in range(4):
                        E("pe", "matmul", ["qkv"], ["ps1"], out=QK[:, h, :], lhsT=qkv[:, 4 + h, c0:c0 + 64], rhs=qkv[:, h, c0:c0 + 64], start=True, stop=True)
                    E("pe", "matmul", ["dg2", "onesf"], ["ps1"], out=ps[1][0:64, 256:512], lhsT=ones_f[0:64, 0:64], rhs=dg2[:].rearrange("p h i -> p (h i)"),
                      start=True, stop=True)
                    E("pe", "matmul", ["dg2", "onesf"], ["ps3"], out=ps[3][:, 256:512], lhsT=ones_f[0:64, :], rhs=dg2[:].rearrange("p h i -> p (h i)"),
                      start=True, stop=True)

                def prep2(n, rb):
                    c0 = n * 64
                    samp = n == 32
                    ub = n % 3
                    Ub = Uc[ub]
                    mp2 = cst[0:64, C_MP2S:C_MP2S + 64] if samp else cst[0:64, C_MP2:C_MP2 + 64]
                    for h in range(4):
                        E("pe", "matmul", [f"Uc{ub}", f"vb{rb}"], ["ps2"], out=WV[:, h, :], lhsT=Ub[:, h, :], rhs=vb[rb][:, h, :], start=True, stop=True)
                    for h in range(4):
                        E("pe", "matmul", [f"Uc{ub}", f"kbg{rb}"], ["ps3"], out=KC[:, h, :], lhsT=kbg[rb][:, h, :], rhs=Ub[:, h, :], start=True, stop=True)
                    E("act", "activation", ["ps3"], ["egb"], out=egb[:], in_=EGB, func=AF.Exp)
                    E("act", "copy", ["ps3"], [f"kcT{rb}"], out=kcT[rb][:], in_=KC)
                    E("act", "copy", ["ps2"], ["wv"], out=wv[rb][:], in_=WV)
                    E("dve", "tensor_tensor", ["ps1", "cst"], ["t2"], out=t2[:], in0=GC2, in1=bc(mp2, 1, [64, 4, 64]), op=ALU.subtract)
                    E("dve", "tensor_tensor", ["t2", "gctm"], ["t2"], out=t2[:], in0=t2[:], in1=bc(gctm[:, n, :], 2, [64, 4, 64]), op=ALU.subtract)
                    E("act", "activation", ["t2"], ["t2"], out=t2[:], in_=t2[:], func=AF.Exp)
                    E("dve", "tensor_tensor", ["qkv", "egb"], [f"qdT{rb}"], out=qdT[rb][:], in0=qkv[:, 0:4, c0:c0 + 64], in1=egb[:], op=ALU.mult)
                    E("dve", "tensor_tensor", ["ps1", "t2"], [f"qkT{rb}"], out=qkT[rb][:], in0=QK, in1=t2[:], op=ALU.mult)

                def epilogue_gen(c0, ncol, ob):
                    obuf = obufs[ob]
                    otag = f"obuf{ob}"
                    for h in range(4):
                        E("act", "activation", [otag], ["sqh"], out=sqh[:, 0:ncol], in_=obuf[:, h, 0:ncol], func=AF.Square)
                        yield
                        E("pe", "matmul", ["sqh", "ones"], ["ps7"], out=ps[7][:, 0:ncol], lhsT=ones_bf[:], rhs=sqh[:, 0:ncol], start=True, stop=True)
                        yield
                        E("act", "activation", ["ps7", "epsb"], ["rsh"], out=rsh[:, 0:ncol], in_=ps[7][:, 0:ncol], func=AF.Ln, scale=1.0 / 128, bias=epsb[:])
                        yield
                        E("act", "activation", ["rsh"], ["rsh"], out=rsh[:, 0:ncol], in_=rsh[:, 0:ncol], func=AF.Exp, scale=-0.5)
                        yield
                        E("dve", "scalar_tensor_tensor", [otag, "sm", "rsh"], [otag], out=obuf[:, h, 0:ncol], in0=obuf[:, h, 0:ncol], scalar=gnw,
                          in1=rsh[:, 0:ncol], op0=ALU.mult, op1=ALU.mult)
                        yield
                        E("dve", "tensor_tensor", [otag, "zs"], ["go"], out=go[:, h, 0:ncol], in0=obuf[:, h, 0:ncol], in1=zs[:, h, c0:c0 + ncol], op=ALU.mult)
                        yield
                    for qr in range(4):
                        for o2 in range(2):
                            o = qr * 2 + o2
                            for h in range(4):
                                E("pe", "matmul", ["wog", "go"], ["ps7"], out=ps[7][:, o2 * 256:o2 * 256 + ncol], lhsT=wog[:, h, o * 128:(o + 1) * 128],
                                  rhs=go[:, h, 0:ncol], start=(h == 0), stop=(h == 3))
                        yield
                        E("dve", "tensor_tensor", ["ps7", xtag(c0)], [xtag(c0)], out=xT[:, qr * 2:qr * 2 + 2, c0:c0 + ncol],
                          in0=ps[7].rearrange("p (o t) -> p o t", o=2)[:, :, 0:ncol], in1=xT[:, qr * 2:qr * 2 + 2, c0:c0 + ncol], op=ALU.add)
                        yield

                pend = [None]

                def step(k):
                    for _ in range(k):
                        if pend[0] is None:
                            return
                        try:
                            next(pend[0])
                        except StopIteration:
                            pend[0] = None

                def epilogue(c0, ncol, ob):
                    step(10 ** 6)
                    pend[0] = epilogue_gen(c0, ncol, ob)
                    step(10 ** 6)

                prep1(0, 0)
                prep2(0, 0)
                for n in range(32):
                    rb = n % 2
                    prep1(n + 1, (n + 1) % 2)
                    step(3)
                    for h in range(4):
                        E("pe", "matmul", [f"kcT{rb}", "Sb"], ["ps4"], out=UP[:, h, :], lhsT=kcT[rb][:, h, :], rhs=Sb[:, h, :], start=True, stop=True)
                    E("dve", "tensor_tensor", ["wv", "ps4"], ["u_sb"], out=u_sb[:], in0=wv[rb][:], in1=UP, op=ALU.subtract)
                    prep2(n + 1, (n + 1) % 2)
                    step(3)
                    for h in range(4):
                        E("pe", "matmul", [f"kdec{rb}", "u_sb"], ["ps6"], out=SP[:, h, :], lhsT=kdec[rb][:, h, :], rhs=u_sb[:, h, :], start=True, stop=True)
                    for h in range(4):
                        E("pe", "matmul", ["Sb", f"qdT{rb}"], ["ps5"], out=OP[:, h, :], lhsT=Sb[:, h, :], rhs=qdT[rb][:, h, :], start=True, stop=False)
                        E("pe", "matmul", ["u_sb", f"qkT{rb}"], ["ps5"], out=OP[:, h, :], lhsT=u_sb[:, h, :], rhs=qkT[rb][:, h, :], start=False, stop=True)
                    for h in range(4):
                        E("dve", "scalar_tensor_tensor", ["Sf", "cdec", "ps6"], ["Sf"], out=Sf[:, h, :], in0=Sf[:, h, :], scalar=cdec[:, n, h:h + 1],
                          in1=SP[:, h, :], op0=ALU.mult, op1=ALU.add)
                    ob = (n // 4) % 2
                    E("act", "copy", ["ps5"], [f"obuf{ob}"], out=obufs[ob][:, :, (n % 4) * 64:(n % 4) * 64 + 64], in_=OP)
                    E("act", "copy", ["Sf"], ["Sb"], out=Sb[:], in_=Sf[:])
                    step(3)
                    if n % 4 == 3:
                        step(10 ** 6)
                        pend[0] = epilogue_gen((n - 3) * 64, 256, ob)
                step(10 ** 6)
                DMA("sp", ["Sf"], [], out=ogp[l].rearrange("h k v -> k h v"), in_=Sf[:])

                for h in range(4):
                    DMA("sp", [], ["Sfs"], out=Sfs[:], in_=sgd[l, :, h].rearrange("s k v -> k s v"))
                    DMA("pool", [], ["Sbs"], out=Sbs[:], in_=sgd[l, :, h].rearrange("s k v -> k s v"))
                    E("dve", "tensor_tensor", ["kcT0", "maskS"], ["kcm"], out=kcm[:], in0=bc(kcT[0][:, h, :], 1, [128, 16, 64]), in1=maskS[:], op=ALU.mult)
                    E("dve", "tensor_tensor", ["qdT0", "maskS"], ["qdm"], out=qdm[:], in0=bc(qdT[0][:, h, :], 1, [128, 16, 64]), in1=maskS[:], op=ALU.mult)
                    E("dve", "tensor_tensor", ["kdec0", "cst"], ["kdm"], out=kdm[:], in0=bc(kdec[0][:, h, :], 1, [64, 16, 128]),
                      in1=bc(cst[0:64, C_OHS:C_OHS + 16], 2, [64, 16, 128]), op=ALU.mult)
                    for s_ in range(16):
                        E("pe", "matmul", ["kcm", "Sbs"], ["ps4"], out=UP[:, h, :], lhsT=kcm[:, s_, :], rhs=Sbs[:, s_, :], start=(s_ == 0), stop=(s_ == 15))
                    E("dve", "tensor_tensor", ["wv", "ps4"], ["u_sb"], out=u_sb[:, h, :], in0=wv[0][:, h, :], in1=UP[:, h, :], op=ALU.subtract)
                    for s_ in range(16):
                        E("pe", "matmul", ["qdm", "Sbs"], ["ps5"], out=OP[:, h, :], lhsT=Sbs[:, s_, :], rhs=qdm[:, s_, :], start=(s_ == 0), stop=False)
                    E("pe", "matmul", ["u_sb", "qkT0"], ["ps5"], out=OP[:, h, :], lhsT=u_sb[:, h, :], rhs=qkT[0][:, h, :], start=False, stop=True)
                    E("act", "copy", ["ps5"], ["obuf0"], out=obufs[0][:, h, 0:64], in_=OP[:, h, :])
                    for s4 in range(4):
                        for sl in range(4):
                            s_ = s4 * 4 + sl
                            E("pe", "matmul", ["kdm", "u_sb"], ["ps6"], out=SP[:, sl, :], lhsT=kdm[:, s_, :], rhs=u_sb[:, h, :], start=True, stop=True)
                        for sl in range(4):
                            s_ = s4 * 4 + sl
                            E("dve", "scalar_tensor_tensor", ["Sfs", "cdecs", "ps6"], ["Sfs"], out=Sfs[:, s_, :], in0=Sfs[:, s_, :],
                              scalar=cdecs[:, s_, h:h + 1], in1=SP[:, sl, :], op0=ALU.mult, op1=ALU.add)
                    DMA("sp", ["Sfs"], [], out=ogs[l, :, h].rearrange("s k v -> k s v"), in_=Sfs[:])
                epilogue(NPR, 64, 0)
                P.flush("gdn_scan")
            AR.top = XN_BYTES

        def final_out():
            AR.top = 0
            with contextlib.ExitStack() as ph:
                def al(shape, dt=F32):
                    return ph.enter_context(AR.alloc(list(shape), dt))
                sq = [al([128, 8, 512], BF16) for _ in range(2)]
                rstd = [al([128, 512]) for _ in range(2)]
                yfm = [al([128, 8, 512]) for _ in range(2)]
                ytm = [al([128, D]) for _ in range(2)]
                ti = 0
                for bi, (t0, tn) in enumerate(TBS):
                    b = bi % 2
                    E("act", "activation", [xtag(t0)], [f"sq{b}"], out=sq[b][:, :, 0:tn], in_=xT[:, :, t0:t0 + tn], func=AF.Square)
                    for c in range(8):
                        E("pe", "matmul", [f"sq{b}", "ones"], [f"ps{b}"], out=ps[b][:, 0:tn], lhsT=ones_bf[:], rhs=sq[b][:, c, 0:tn],
                          start=(c == 0), stop=(c == 7))
                    E("act", "activation", [f"ps{b}", "epsb"], [f"rstd{b}"], out=rstd[b][:, 0:tn], in_=ps[b][:, 0:tn], func=AF.Ln, scale=1.0 / D, bias=epsb[:])
                    E("act", "activation", [f"rstd{b}"], [f"rstd{b}"], out=rstd[b][:, 0:tn], in_=rstd[b][:, 0:tn], func=AF.Exp, scale=-0.5)
                    for c in range(8):
                        E("dve", "scalar_tensor_tensor", [xtag(t0), "sm", f"rstd{b}"], [f"yfm{b}"], out=yfm[b][:, c, 0:tn], in0=xT[:, c, t0:t0 + tn],
                          scalar=sm[:, SM_NF + c:SM_NF + c + 1], in1=rstd[b][:, 0:tn], op0=ALU.mult, op1=ALU.mult)
                    for q in range((tn + 127) // 128):
                        rows = min(128, tn - q * 128)
                        yb = ti % 2
                        ti += 1
                        for c in range(8):
                            bank = 2 + c // 4
                            E("pe", "transpose", [f"yfm{b}", "identf"], [f"ps{bank}"], out=ps[bank][0:rows, (c % 4) * 128:(c % 4 + 1) * 128],
                              in_=yfm[b][:, c, q * 128:q * 128 + rows], identity=identf[:])
                        E("dve", "tensor_copy", ["ps2", "ps3"], [f"ytm{yb}"], out=ytm[yb][0:rows, :], in_=pall[0:rows, 1024:2048])
                        if t0 < NPR:
                            r0 = t0 + q * 128
                            DMA("sp", [f"ytm{yb}"], [], out=yp[r0:r0 + 128, :], in_=ytm[yb][:, :])
                        else:
                            for t_ in range(4):
                                DMA("sp", [f"ytm{yb}"], [], out=ys[:, t_, :], in_=ytm[yb][t_ * 16:(t_ + 1) * 16, :])
                P.flush("final")

        for l in range(nlayers):
            ffn(l, w1g, w1u, w1d, SM_N1 + 8 * l)
            if dbg == "ffn1" and l == nlayers - 1:
                break
            swa(l)
            if dbg in ("swa", "swa_p") and l == nlayers - 1:
                break
            gdn(l)
            if dbg in ("mix", "gdn_p1") and l == nlayers - 1:
                break
            ffn(l, w2g, w2u, w2d, SM_N2 + 8 * l)

        if dbg:
            P.op("sp", lambda e: e.dma_start(out=dbgx, in_=xT[:]), r=[f"xT{i}" for i in range(5)], dma=True)
            P.flush("dbg")

        if not dbg:
            final_out()
    return nc


SM_N1 = 0
SM_N2 = SM_N1 + 8 * DEPTH
SM_NM = SM_N2 + 8 * DEPTH
SM_NF = SM_NM + 8 * DEPTH
SM_SINK = SM_NF + 8
SM_SWN = SM_SINK + 8 * DEPTH
SM_SSINK = SM_SWN + 4 * DEPTH
SM_CONVW = SM_SSINK + 2 * DEPTH
SM_GNW = SM_CONVW + 48 * DEPTH
SM_ALOG = SM_GNW + DEPTH
SM_DTB = SM_ALOG + 4 * DEPTH
NSMALL = SM_DTB + 4 * DEPTH
C_MASK2 = 0
C_SMASK = 256
C_MP1, C_MP2, C_MP1S, C_MP2S, C_TRI, C_TRIS, C_SAMES, C_TRIU, C_OH48, C_OHS = 392, 456, 520, 584, 648, 712, 776, 840, 904, 920
NCONST = 936
NEG = -30000.0
BIGM = 30000.0


def make_consts():
    c = np.zeros((128, NCONST), np.float32)
    i = np.arange(128)[:, None]
    j = np.arange(128)[None, :]
    c[:, C_MASK2:C_MASK2 + 128] = np.where(j > i, 0.0, NEG)
    c[:, C_MASK2 + 128:C_MASK2 + 256] = np.where(j <= i, 0.0, NEG)
    t = (np.arange(16) % 4)[:, None]
    c[0:16, C_SMASK:C_SMASK + 128] = np.where(np.arange(128)[None, :] > t, 0.0, NEG)
    c[0:16, C_SMASK + 128:C_SMASK + 132] = np.where(np.arange(4)[None, :] <= t, 0.0, NEG)
    a = np.arange(64)[:, None]
    b = np.arange(64)[None, :]
    ta, sa, tb_, sb_ = a // 16, a % 16, b // 16, b % 16
    c[0:64, C_MP1:C_MP1 + 64] = np.where(a > b, 0.0, BIGM)
    c[0:64, C_MP2:C_MP2 + 64] = np.where(b >= a, 0.0, BIGM)
    c[0:64, C_MP1S:C_MP1S + 64] = np.where((sa == sb_) & (ta > tb_), 0.0, BIGM)
    c[0:64, C_MP2S:C_MP2S + 64] = np.where((sa == sb_) & (tb_ >= ta), 0.0, BIGM)
    c[0:64, C_TRI:C_TRI + 64] = (a <= b)
    c[0:64, C_TRIS:C_TRIS + 64] = (sa == sb_) & (ta <= tb_)
    c[0:64, C_SAMES:C_SAMES + 64] = (sa == sb_)
    c[0:64, C_TRIU:C_TRIU + 64] = (b >= a)
    c[0:64, C_OH48:C_OH48 + 16] = (a == 48 + np.arange(16)[None, :])
    c[0:64, C_OHS:C_OHS + 16] = (sa == np.arange(16)[None, :])
    return c


def make_maskS():
    m = np.zeros((128, 16, 64), np.float32)
    for s_ in range(16):
        m[:, s_, s_::16] = 1.0
    return m.astype(ml_dtypes.bfloat16)


def make_smalls(inp):
    s = np.zeros((128, NSMALL), np.float32)
    for l in range(DEPTH):
        s[:, SM_N1 + 8 * l:SM_N1 + 8 * l + 8] = inp["ffn1_norm"][l].reshape(8, 128).T
        s[:, SM_N2 + 8 * l:SM_N2 + 8 * l + 8] = inp["ffn2_norm"][l].reshape(8, 128).T
        s[:, SM_NM + 8 * l:SM_NM + 8 * l + 8] = inp["mix_norm"][l].reshape(8, 128).T
    s[:, SM_NF:SM_NF + 8] = inp["final_norm"].reshape(8, 128).T
    for l in range(DEPTH):
        for g in range(2):
            for sl_ in range(4):
                s[:, SM_SINK + 8 * l + 4 * g + sl_] = inp["swa_sinks"][l][4 * g + (sl_ % 2) * 2 + sl_ // 2]
        s[:, SM_SWN + 4 * l:SM_SWN + 4 * l + 4] = inp["swa_out_norm"][l].reshape(4, 128).T
        for g in range(2):
            s[0:16, SM_SSINK + 2 * l + g] = np.repeat(inp["swa_sinks"][l][g * 4:(g + 1) * 4], 4)
        cw = np.asarray(inp["gdn_conv_w"][l])
        s[:, SM_CONVW + 48 * l:SM_CONVW + 48 * l + 48] = cw.reshape(4, 12, 128).transpose(2, 1, 0).reshape(128, 48)
        s[:, SM_GNW + l] = inp["gdn_out_norm"][l]
        s[:, SM_ALOG + 4 * l:SM_ALOG + 4 * l + 4] = inp["gdn_a_log"][l][None, :]
        s[:, SM_DTB + 4 * l:SM_DTB + 4 * l + 4] = inp["gdn_dt_bias"][l][None, :]
    return s


def make_in_maps(inp, nl=DEPTH, ncores=8):
    f = lambda a: np.ascontiguousarray(np.asarray(a[:nl], dtype=np.float32))
    g = lambda a: np.ascontiguousarray(np.asarray(a, dtype=np.float32))
    smalls = make_smalls(inp)
    shared = {
        "smalls": smalls, "identf": np.eye(128, dtype=np.float32),
        "consts": make_consts(), "identb": np.eye(128).astype(ml_dtypes.bfloat16), "maskS": make_maskS(), "maskb": (8.0 * make_consts()[:, C_MASK2:C_MASK2 + 256]).astype(ml_dtypes.bfloat16),
        "w_in": f(inp["w_in"]), "w_out": f(inp["w_out"]),
        "w1g": f(inp["ffn1_w_gate"]), "w1u": f(inp["ffn1_w_up"]), "w1d": f(inp["ffn1_w_down"]),
        "w2g": f(inp["ffn2_w_gate"]), "w2u": f(inp["ffn2_w_up"]), "w2d": f(inp["ffn2_w_down"]),
    }
    maps = []
    for c in range(ncores):
        m = dict(shared)
        m["xp"] = g(inp["x_prompt"][c])
        m["cstate"] = g(inp["state_gdn_conv"][:nl, 16 * c:16 * c + 16].reshape(nl, 48, 1536))
        m["sgd"] = g(inp["state_gdn"][:nl, 16 * c:16 * c + 16])
        m["ck"] = g(inp["cache_swa_k"][:nl, 16 * c:16 * c + 16].reshape(nl, 16, 128, 128))
        m["cv"] = g(inp["cache_swa_v"][:nl, 16 * c:16 * c + 16].reshape(nl, 16, 128, 128))
        m["xs"] = g(inp["x_sample"][16 * c:16 * c + 16])
        maps.append(m)
    return maps


def kernel(**inputs):
    nc = build_nc()
    res = run_bass_kernel_spmd(nc, make_in_maps(inputs), core_ids=list(range(8)))
    R = res.results
    cat = lambda k, ax: np.concatenate([np.asarray(r[k]) for r in R], axis=ax)
    y_prompt = np.stack([np.asarray(r["yp"]) for r in R], 0)
    y_sample = cat("ys", 0)
    conv_p = np.stack([np.asarray(r["ocp"]) for r in R], 1)
    gdn_p = np.stack([np.asarray(r["ogp"]) for r in R], 1)
    k_p = np.stack([np.asarray(r["okp"]) for r in R], 1).reshape(DEPTH, 8, 128, 2, 64)
    v_p = np.stack([np.asarray(r["ovp"]) for r in R], 1).reshape(DEPTH, 8, 128, 2, 64)
    conv_s = cat("ocs", 1)
    gdn_s = cat("ogs", 1)
    k_s = cat("oks", 1).reshape(DEPTH, 128, 128, 2, 64)
    v_s = cat("ovs", 1).reshape(DEPTH, 128, 128, 2, 64)
    outs = (y_prompt, y_sample, conv_p, gdn_p, k_p, v_p, conv_s, gdn_s, k_s, v_s)
    return tuple(np.ascontiguousarray(o, dtype=np.float32) for o in outs)
```
